# Optimizing a Trainium2 kernel written in Bass

```python
import math
import jax, jax.numpy as jnp
from jax import lax
import numpy as np

D_MODEL = 1024
BATCH = 4
SEQ = 8192
DEPTH = 2

HEAD_DIM = 64
N_SB_HEADS = 8
N_DIFF_HEADS = 4
N_DIFF_MAPS = 2 * N_DIFF_HEADS
SB_WIDTH = N_SB_HEADS * HEAD_DIM
DIFF_V_DIM = 2 * HEAD_DIM
DIFF_WIDTH = N_DIFF_HEADS * DIFF_V_DIM
DIFF_QK_WIDTH = N_DIFF_MAPS * HEAD_DIM
MIX_WIDTH = SB_WIDTH + DIFF_WIDTH
IN_PROJ_WIDTH = 3 * SB_WIDTH + 2 * DIFF_QK_WIDTH + DIFF_WIDTH
D_FF = 2816
CONV_WIDTH = 3
PLE_DIM = 256
N_BUCKETS = 32
MAX_DISTANCE = 128
Q_BLOCK = 128
EPS = 1e-6

kernel_name = "hybrid_stickbreak_diffattn_convffn_ple"


def rmsnorm(x, g):
    xf = x.astype(jnp.float32)
    y = xf * lax.rsqrt(jnp.mean(xf * xf, axis=-1, keepdims=True) + EPS)
    return (y * g.astype(jnp.float32)).astype(x.dtype)


def rel_bucket(dist):
    max_exact = N_BUCKETS // 2
    d = jnp.maximum(dist, 1).astype(jnp.float32)
    large = max_exact + (jnp.log(d / max_exact) / math.log(MAX_DISTANCE / max_exact)
                         * (N_BUCKETS - max_exact)).astype(jnp.int32)
    large = jnp.minimum(large, N_BUCKETS - 1)
    return jnp.where(dist < max_exact, dist, large)


def to_blocks(t):
    B, H, S, d = t.shape
    return jnp.moveaxis(t.reshape(B, H, S // Q_BLOCK, Q_BLOCK, d), 2, 0)


def from_blocks(t):
    nb, B, H, qb, d = t.shape
    return jnp.moveaxis(t, 0, 2).reshape(B, H, nb * qb, d)


def stick_breaking_attention(q, k, v):
    B, H, S, d = q.shape
    nb = S // Q_BLOCK
    scale = d ** -0.5
    kf = k.astype(jnp.float32)
    key_pos = jnp.arange(S)

    def block(args):
        q_blk, start = args
        q_pos = start + jnp.arange(Q_BLOCK)
        past = key_pos[None, :] < q_pos[:, None]
        z = jnp.einsum('bhqd,bhkd->bhqk', q_blk.astype(jnp.float32), kf) * scale
        log_beta = jax.nn.log_sigmoid(z)
        log_not = jnp.where(past, jax.nn.log_sigmoid(-z), 0.0)
        between = lax.cumsum(log_not, axis=3, reverse=True) - log_not
        w = jnp.where(past, jnp.exp(log_beta + between), 0.0)
        return jnp.einsum('bhqk,bhkd->bhqd', w.astype(v.dtype), v)

    out = lax.map(block, (to_blocks(q), jnp.arange(nb) * Q_BLOCK))
    return from_blocks(out)


def differential_attention(q, k, v, lam, rel_bias):
    B, H2, S, d = q.shape
    H = H2 // 2
    nb = S // Q_BLOCK
    scale = d ** -0.5
    kf = k.astype(jnp.float32)
    key_pos = jnp.arange(S)
    table = rel_bias.astype(jnp.float32)

    def block(args):
        q_blk, start = args
        q_pos = start + jnp.arange(Q_BLOCK)
        dist = q_pos[:, None] - key_pos[None, :]
        causal = dist >= 0
        bias = jnp.moveaxis(table[rel_bucket(jnp.maximum(dist, 0))], -1, 0)
        logits = jnp.einsum('bhqd,bhkd->bhqk', q_blk.astype(jnp.float32), kf) * scale + bias[None]
        logits = jnp.where(causal, logits, -jnp.inf)
        probs = jax.nn.softmax(logits, axis=-1).reshape(B, H, 2, Q_BLOCK, S)
        w = probs[:, :, 0] - lam * probs[:, :, 1]
        return jnp.einsum('bhqk,bhkd->bhqd', w.astype(v.dtype), v)

    out = lax.map(block, (to_blocks(q), jnp.arange(nb) * Q_BLOCK))
    return from_blocks(out)


def causal_depthwise_conv(u, w, b):
    C = u.shape[-1]
    y = lax.conv_general_dilated(
        u, w[:, None, :].astype(u.dtype), window_strides=(1,),
        padding=[(CONV_WIDTH - 1, 0)], dimension_numbers=('NWC', 'WIO', 'NWC'),
        feature_group_count=C)
    return y + b.astype(u.dtype)


def split_heads(t, n):
    B, S, _ = t.shape
    return t.reshape(B, S, n, -1).transpose(0, 2, 1, 3)


def merge_heads(t):
    B, H, S, d = t.shape
    return t.transpose(0, 2, 1, 3).reshape(B, S, H * d)


def setup_inputs(seed: int = 0) -> dict:
    key = jax.random.key(seed)
    ks = jax.random.split(key, 24)
    f32 = jnp.float32
    nrm = lambda k, shape, s: jax.random.normal(k, shape, f32) * s
    gain = lambda k, shape: 1.0 + 0.05 * jax.random.normal(k, shape, f32)
    L = DEPTH
    return {
        "x": jax.random.normal(ks[0], (BATCH, SEQ, D_MODEL), f32),
        "p": jax.random.normal(ks[1], (DEPTH, BATCH, SEQ, PLE_DIM), f32),
        "w_in": nrm(ks[2], (L, D_MODEL, IN_PROJ_WIDTH), D_MODEL ** -0.5),
        "w_o": nrm(ks[3], (L, MIX_WIDTH, D_MODEL), MIX_WIDTH ** -0.5),
        "g_attn": gain(ks[4], (L, D_MODEL)),
        "lambda_q1": nrm(ks[5], (L, HEAD_DIM), 0.1),
        "lambda_k1": nrm(ks[6], (L, HEAD_DIM), 0.1),
        "lambda_q2": nrm(ks[7], (L, HEAD_DIM), 0.1),
        "lambda_k2": nrm(ks[8], (L, HEAD_DIM), 0.1),
        "g_subln": gain(ks[9], (L, DIFF_V_DIM)),
        "rel_bias": nrm(ks[10], (N_BUCKETS, N_DIFF_MAPS), 0.5),
        "g_ffn": gain(ks[11], (L, D_MODEL)),
        "w_up": nrm(ks[12], (L, D_MODEL, 2 * D_FF), D_MODEL ** -0.5),
        "conv_w": nrm(ks[13], (L, CONV_WIDTH, 2 * D_FF), CONV_WIDTH ** -0.5),
        "conv_b": nrm(ks[14], (L, 2 * D_FF), 0.02),
        "w_down": nrm(ks[15], (L, D_FF, D_MODEL), D_FF ** -0.5),
        "g_ple": gain(ks[16], (L, D_MODEL)),
        "w_ple_gate": nrm(ks[17], (L, D_MODEL, D_MODEL), D_MODEL ** -0.5),
        "w_ple_proj": nrm(ks[18], (L, PLE_DIM, D_MODEL), PLE_DIM ** -0.5),
        "g_final": gain(ks[19], (D_MODEL,)),
    }


def reference(x, p, w_in, w_o, g_attn, lambda_q1, lambda_k1, lambda_q2, lambda_k2,
              g_subln, rel_bias, g_ffn, w_up, conv_w, conv_b, w_down, g_ple,
              w_ple_gate, w_ple_proj, g_final):
    h = x
    offsets = [SB_WIDTH, 2 * SB_WIDTH, 3 * SB_WIDTH,
               3 * SB_WIDTH + DIFF_QK_WIDTH, 3 * SB_WIDTH + 2 * DIFF_QK_WIDTH]
    for i in range(DEPTH):
        xn = rmsnorm(h, g_attn[i])
        proj = xn @ w_in[i]
        sb_q, sb_k, sb_v, d_q, d_k, d_v = jnp.split(proj, offsets, axis=-1)

        sb_out = stick_breaking_attention(split_heads(sb_q, N_SB_HEADS),
                                          split_heads(sb_k, N_SB_HEADS),
                                          split_heads(sb_v, N_SB_HEADS))

        lambda_init = 0.8 - 0.6 * math.exp(-0.3 * i)
        lam = (jnp.exp(jnp.sum(lambda_q1[i].astype(jnp.float32) * lambda_k1[i].astype(jnp.float32)))
               - jnp.exp(jnp.sum(lambda_q2[i].astype(jnp.float32) * lambda_k2[i].astype(jnp.float32)))
               + lambda_init)
        diff_out = differential_attention(split_heads(d_q, N_DIFF_MAPS),
                                          split_heads(d_k, N_DIFF_MAPS),
                                          split_heads(d_v, N_DIFF_HEADS),
                                          lam, rel_bias)
        diff_out = rmsnorm(diff_out, g_subln[i]) * (1.0 - lambda_init)

        mix = jnp.concatenate([merge_heads(sb_out), merge_heads(diff_out)], axis=-1)
        h = h + mix @ w_o[i]

        xn = rmsnorm(h, g_ffn[i])
        u = causal_depthwise_conv(xn @ w_up[i], conv_w[i], conv_b[i])
        gate, val = jnp.split(u, 2, axis=-1)
        h = h + (jax.nn.gelu(gate) * val) @ w_down[i]

        xn = rmsnorm(h, g_ple[i])
        h = h + jax.nn.sigmoid(xn @ w_ple_gate[i]) * (p[i] @ w_ple_proj[i])
    return rmsnorm(h, g_final)
```

```python
import math
from contextlib import ExitStack
import numpy as np
import concourse.bass as bass
import concourse.mybir as mybir
from concourse.bass_utils import run_bass_kernel_spmd

F32 = mybir.dt.float32
BF16 = mybir.dt.bfloat16
AF = mybir.ActivationFunctionType
ALU = mybir.AluOpType
AX = mybir.AxisListType

D = 1024
DFF = 2816
NFC = 22
PLE = 256
EPS = 1e-6
NEG = -30000.0
SEM_LIM = 30000
N_DMA_SEM = 8


class Buf:
    __slots__ = ("w", "wd", "r", "rd")

    def __init__(self):
        self.w = None
        self.wd = []
        self.r = {}
        self.rd = []


class Prog:
    def __init__(self, nc, es):
        self.nc = nc
        self.es = es
        self.eng = {"pe": nc.tensor, "act": nc.scalar, "dve": nc.vector,
                    "pool": nc.gpsimd, "sp": nc.sync}
        self.meta = []
        self.cnt = {e: 0 for e in self.eng}
        self.sems = {e: [] for e in self.eng}
        self.dsems = {e: [es.enter_context(nc.semaphore(f"d_{e}_{j}")) for j in range(N_DMA_SEM)]
                      for e in ("sp", "pool")}
        self.dcount = {"sp": 0, "pool": 0}
        self.waited = {e: {p: 0 for p in self.eng} for e in self.eng}
        self.dwaited = {e: {} for e in self.eng}
        self.n_wait = 0

    def _sem(self, eng, g):
        j, v = (g - 1) // SEM_LIM, (g - 1) % SEM_LIM + 1
        lst = self.sems[eng]
        while len(lst) <= j:
            lst.append(self.es.enter_context(self.nc.semaphore(f"s_{eng}_{len(lst)}")))
        return lst[j], v

    def barrier(self):
        for eng, E in self.eng.items():
            for p in self.eng:
                g = self.cnt[p]
                if g == 0 or self.waited[eng][p] >= g:
                    continue
                self.waited[eng][p] = g
                s, v = self._sem(p, g)
                E.wait_ge(s, v)
                self.n_wait += 1
            for q in ("sp", "pool"):
                k = self.dcount[q]
                for j in range(min(k, N_DMA_SEM)):
                    last_k = ((k - 1 - j) // N_DMA_SEM) * N_DMA_SEM + j
                    v = 16 * (last_k // N_DMA_SEM + 1)
                    s = self.dsems[q][j]
                    if self.dwaited[eng].get(id(s), 0) < v:
                        self.dwaited[eng][id(s)] = v
                        E.wait_ge(s, v)
                        self.n_wait += 1

    def add(self, eng, fn, reads=(), writes=(), dma=False, sig=True, cc=None):
        i = len(self.meta)
        deps = set()
        for b in reads:
            if b.w is not None:
                deps.add(b.w)
            deps.update(b.wd)
        for b in writes:
            if b.w is not None:
                deps.add(b.w)
            if not dma or b.r or b.rd:
                deps.update(b.wd)
            deps.update(b.r.values())
            deps.update(b.rd)
        for b in writes:
            if dma:
                if b.r or b.rd:
                    b.wd = []
                b.wd.append(i)
            else:
                b.w = i
                b.wd = []
            b.r = {}
            b.rd = []
        for b in reads:
            if dma:
                b.rd.append(i)
            else:
                b.r[eng] = i
        deps.discard(i)
        E = self.eng[eng]
        need = {}
        waits = []
        for d in deps:
            deng, ddma, info = self.meta[d]
            if ddma:
                s, v = info
                key = id(s)
                if self.dwaited[eng].get(key, 0) < v:
                    self.dwaited[eng][key] = v
                    waits.append((s, v))
                continue
            if deng == eng and eng == "pe" and not dma:
                continue
            if info < 0:
                g = -info
                assert self.cnt[deng] >= g, "dependency on an unsignalled op whose group is not closed"
                info = g
            if info > need.get(deng, 0):
                need[deng] = info
        for deng, g in need.items():
            if self.waited[eng][deng] >= g:
                continue
            self.waited[eng][deng] = g
            waits.append(self._sem(deng, g))
        if dma and cc is None:
            k = self.dcount[eng]
            self.dcount[eng] += 1
            s = self.dsems[eng][k % N_DMA_SEM]
            v = 16 * (k // N_DMA_SEM + 1)
            if k >= N_DMA_SEM:
                key = id(s)
                if self.dwaited[eng].get(key, 0) < v - 16:
                    self.dwaited[eng][key] = v - 16
                    waits.append((s, v - 16))
        self.n_wait += len(waits)
        if cc is not None:
            for (ws, wv) in waits:
                E.wait_ge(ws, wv)
            fn().then_inc(cc[0], 1)
            self.meta.append((eng, True, cc))
            return i
        if fn is None:
            for (ws, wv) in waits:
                E.wait_ge(ws, wv)
            self.meta.append((eng, False, self.cnt[eng]))
            return i
        for (ws, wv) in waits[:-1]:
            E.wait_ge(ws, wv)
            self.n_standalone = getattr(self, 'n_standalone', 0) + 1
        inst = fn()
        if waits:
            inst._wait_ge(*waits[-1])
        if dma:
            inst.then_inc(s, 16)
            self.meta.append((eng, True, (s, v)))
        elif sig:
            self.cnt[eng] += 1
            g = self.cnt[eng]
            s, v = self._sem(eng, g)
            inst.then_inc(s, 1)
            self.meta.append((eng, False, g))
        else:
            self.meta.append((eng, False, -(self.cnt[eng] + 1)))
        return i


def build_program(T, L, taps=False):
    nc = bass.Bass("TRN2", target_bir_lowering=False)
    es = ExitStack()
    P = Prog(nc, es)
    NB = T // 512
    NKB = T // 128
    H = T // 2
    NBh = H // 512
    rank = nc.sync.snap(nc.sync.partition_id() % 2, min_val=0, max_val=1)
    r_qk = rank * 2048
    r_v = rank * (2 * H)
    r_h = rank * H
    VP = min(2048, H)
    npv = H // VP

    def din(name, shape, dt=F32):
        return nc.dram_tensor(name, list(shape), dt, kind="ExternalInput").ap()

    def dscr(name, shape, dt):
        kind = "ExternalOutput" if taps else "Internal"
        return nc.dram_tensor(name, list(shape), dt, kind=kind).ap()

    xT = din("xT", [D, H])
    pT = din("pT", [L, PLE, H])
    w_in = din("w_in", [L, D, 3072])
    w_o = din("w_o", [L, D, D])
    w_up = din("w_up", [L, D, 2 * DFF])
    w_down = din("w_down", [L, DFF, D])
    w_pg = din("w_pg", [L, D, D])
    w_pp = din("w_pp", [L, PLE, D])
    gvec_d = din("gvec", [128, L * 3 * 8])
    gfin_d = din("gfin", [128, 8])
    cw_d = din("convw", [128, L * 3 * 44])
    cb_d = din("convb", [128, L * 44])
    gsub_d = din("gsub", [128, L])
    lamv_d = din("lamv", [128, L * 4 * 64])
    btoe_d = din("btoe", [4, 128, 1024])
    negw_d = din("negw", [2, 128, 1024])
    cst_d = din("cst", [3, 128, 128])
    hmask_d = din("hmask", [128, 1])
    outT = nc.dram_tensor("outT", [D, H], F32, kind="ExternalOutput").ap()

    qk_loc = nc.dram_tensor("qk_loc", [NBh * 2048, 512], BF16)
    v_loc = nc.dram_tensor("v_loc", [NBh * 1024, 512], BF16)
    qk_scr, v_scr = qk_loc.ap(), v_loc.ap()
    qk_full = nc.dram_tensor("qk_full", [NBh * 4096, 512], BF16)
    v_full = nc.dram_tensor("v_full", [NBh * 2048, 512], BF16)
    qkfull_b, vfull_b = Buf(), Buf()
    qk_mine = nc.dram_tensor("qk_mine", [NBh * 2048, 512], BF16)
    v_mine = nc.dram_tensor("v_mine", [NBh * 1024, 512], BF16)
    mix_mine = nc.dram_tensor("mix_mine", [D, H], BF16)
    qkmine_b, vmine_b, mixmine_b = Buf(), Buf(), Buf()
    tail_loc = nc.dram_tensor("tail_loc", [D, 2], BF16)
    tail_full = nc.dram_tensor("tail_full", [2 * D, 2], BF16)
    tail_b, tailfull_b = Buf(), Buf()
    mix_loc = nc.dram_tensor("mix_loc", [512, T], BF16)
    mix_full = nc.dram_tensor("mix_full", [D, T], BF16)
    mix_scr = mix_loc.ap()
    mixfull_b = Buf()
    cc_sem = es.enter_context(nc.semaphore("cc_sem"))
    cc_n = [0]
    xn2_scr = dscr("xn2_scr", [D, H], BF16)
    hA = dscr("hA", [D, H], F32)
    hB = dscr("hB", [D, H], F32)
    hC = dscr("hC", [D, H], F32)

    def blkbufs(n=NBh):
        return [Buf() for _ in range(n)]
    qk_b, v_b, xn2_b, hA_b, hB_b, hC_b = (blkbufs() for _ in range(6))
    mix_bu = [blkbufs(NB) for _ in range(4)]

    uid = [0]

    def sb(name, shape, dt, stack=None):
        uid[0] += 1
        return (stack or es).enter_context(nc.sbuf_tensor(f"t{uid[0]}_{name}", list(shape), dt))

    pbig = [es.enter_context(nc.psum_tensor(f"psw{i}", [128, 1024], F32)) for i in range(4)]
    psum = [pbig[i // 2][:, (i % 2) * 512:(i % 2) * 512 + 512] for i in range(8)]
    psb = [Buf() for _ in range(8)]
    ident = sb("ident", [128, 128], BF16)
    ntri = sb("ntri", [128, 128], BF16)
    ones = sb("ones", [128, 128], BF16)
    gvec = sb("gvec_s", [128, L * 3 * 8], F32)
    gvec32 = sb("gvec32", [128, L * 3 * 8], F32)
    gfin = sb("gfin_s", [128, 8], F32)
    gfin32 = sb("gfin32", [128, 8], F32)
    cw = sb("cw_s", [128, L * 3 * 44], F32)
    cb = sb("cb_s", [128, L * 44], F32)
    gsub = sb("gsub_s", [128, L], F32)
    gsub2 = sb("gsub2", [128, L], F32)
    lamv = sb("lamv_s", [128, L * 4 * 64], F32)
    lamt = sb("lamt", [128, 64], F32)
    lams = sb("lams", [128, 2 * L], F32)
    neglam = sb("neglam", [128, L], F32)
    cstage = sb("cstage", [128, 3, 128], F32)
    hmask = sb("hmask_s", [128, 1], F32)
    b_const = Buf()

    def A(eng, fn, reads=(), writes=(), dma=False, sig=True, cc=None):
        return P.add(eng, fn, reads, writes, dma, sig, cc)

    A("sp", lambda: nc.sync.dma_start(out=cstage[:], in_=cst_d.rearrange("k p n -> p k n")),
      writes=[b_const], dma=True)
    for dst, src in ((gvec, gvec_d), (gfin, gfin_d), (cw, cw_d), (cb, cb_d), (gsub, gsub_d), (lamv, lamv_d), (hmask, hmask_d)):
        A("sp", lambda dst=dst, src=src: nc.sync.dma_start(out=dst[:], in_=src), writes=[b_const], dma=True)
    A("dve", lambda: nc.vector.tensor_copy(out=ident[:], in_=cstage[:, 0, :]), reads=[b_const], writes=[b_const])
    A("dve", lambda: nc.vector.tensor_copy(out=ntri[:], in_=cstage[:, 1, :]), reads=[b_const], writes=[b_const])
    A("dve", lambda: nc.vector.tensor_copy(out=ones[:], in_=cstage[:, 2, :]), reads=[b_const], writes=[b_const])
    A("dve", lambda: nc.vector.tensor_scalar(out=gvec32[:], in0=gvec[:], scalar1=32.0, scalar2=None, op0=ALU.mult),
      reads=[b_const], writes=[b_const])
    A("dve", lambda: nc.vector.tensor_scalar(out=gfin32[:], in0=gfin[:], scalar1=32.0, scalar2=None, op0=ALU.mult),
      reads=[b_const], writes=[b_const])
    for l in range(L):
        li = 0.8 - 0.6 * math.exp(-0.3 * l)
        for j in range(2):
            o = (l * 4 + 2 * j) * 64
            A("dve", lambda o=o: nc.vector.tensor_tensor(out=lamt[:], in0=lamv[:, o:o + 64], in1=lamv[:, o + 64:o + 128],
                                                         op=ALU.mult), reads=[b_const], writes=[b_const])
            A("dve", lambda l=l, j=j: nc.vector.reduce_sum(out=lams[:, 2 * l + j:2 * l + j + 1], in_=lamt[:], axis=AX.X),
              reads=[b_const], writes=[b_const])
        A("act", lambda l=l: nc.scalar.activation(out=lams[:, 2 * l:2 * l + 2], in_=lams[:, 2 * l:2 * l + 2], func=AF.Exp),
          reads=[b_const], writes=[b_const])
        A("dve", lambda l=l, li=li: nc.vector.scalar_tensor_tensor(
            out=neglam[:, l:l + 1], in0=lams[:, 2 * l + 1:2 * l + 2], scalar=-li, in1=lams[:, 2 * l:2 * l + 1],
            op0=ALU.add, op1=ALU.subtract), reads=[b_const], writes=[b_const])
        A("dve", lambda l=l, li=li: nc.vector.tensor_scalar(
            out=gsub2[:, l:l + 1], in0=gsub[:, l:l + 1], scalar1=(1.0 - li) * math.sqrt(128.0), scalar2=None, op0=ALU.mult),
            reads=[b_const], writes=[b_const])

    rr = {"ps": 0}

    def next_ps():
        i = rr["ps"] % 8
        rr["ps"] += 1
        return i

    def load_weight(dst, dstbuf, src2d, nchunk, ncols, stg, stgb, colstep):
        k = 0
        for c in range(nchunk):
            for c0 in range(0, ncols, colstep):
                w = min(colstep, ncols - c0)
                s, sbuf_ = stg[k % 2], stgb[k % 2]
                A("sp", lambda s=s, c=c, c0=c0, w=w: nc.sync.dma_start(
                    out=s[:, :w], in_=src2d[c * 128:(c + 1) * 128, c0:c0 + w]), writes=[sbuf_], dma=True)
                if k % 2 == 0:
                    A("pool", lambda s=s, c=c, c0=c0, w=w: nc.gpsimd.tensor_copy(out=dst[:, c, c0:c0 + w], in_=s[:, :w]),
                      reads=[sbuf_], writes=[dstbuf[0]])
                else:
                    A("act", lambda s=s, c=c, c0=c0, w=w: nc.scalar.activation(out=dst[:, c, c0:c0 + w], in_=s[:, :w], func=AF.Copy),
                      reads=[sbuf_], writes=[dstbuf[1]])
                k += 1

    def weight_pieces(dst, dstbuf, src2d, nchunk, ncols, stg, stgb, colstep):
        out = []
        k = 0
        for c in range(nchunk):
            for c0 in range(0, ncols, colstep):
                w = min(colstep, ncols - c0)
                s_, sb_ = stg[k % 2], stgb[k % 2]

                def piece(s_=s_, sb_=sb_, c=c, c0=c0, w=w, k=k):
                    A("sp", lambda: nc.sync.dma_start(out=s_[:, :w], in_=src2d[c * 128:(c + 1) * 128, c0:c0 + w]),
                      writes=[sb_], dma=True)
                    if k % 2 == 0:
                        A("pool", lambda: nc.gpsimd.tensor_copy(out=dst[:, c, c0:c0 + w], in_=s_[:, :w]),
                          reads=[sb_], writes=[dstbuf[0]])
                    else:
                        A("act", lambda: nc.scalar.activation(out=dst[:, c, c0:c0 + w], in_=s_[:, :w], func=AF.Copy),
                          reads=[sb_], writes=[dstbuf[1]])
                out.append(piece)
                k += 1
        return out

    def norm_block(hb, hbb, n, gcol, sq, sqb, lnv, lnvb, rstd, rstdb, xn, xnb, xc0=0, out_f32=None, outb=None):
        A("act", lambda: nc.scalar.activation(out=sq[:, :, :n], in_=hb[:, :, :n], func=AF.Square),
          reads=[hbb], writes=[sqb])
        pi = next_ps()
        for c in range(8):
            A("pe", lambda c=c, pi=pi: nc.tensor.matmul(psum[pi][:, :n], ones[:], sq[:, c, :n], start=(c == 0), stop=(c == 7)),
              reads=[sqb, b_const], writes=[psb[pi]], sig=(c == 7))
        A("act", lambda pi=pi: nc.scalar.activation(out=lnv[:, :n], in_=psum[pi][:, :n], func=AF.Ln, bias=1024.0 * EPS, scale=1.0),
          reads=[psb[pi]], writes=[lnvb])
        A("act", lambda: nc.scalar.activation(out=rstd[:, :n], in_=lnv[:, :n], func=AF.Exp, scale=-0.5),
          reads=[lnvb], writes=[rstdb])
        for c in range(8):
            if out_f32 is None:
                A("dve", lambda c=c: nc.vector.scalar_tensor_tensor(
                    out=xn[:, c, xc0:xc0 + n], in0=hb[:, c, :n], scalar=gcol(c), in1=rstd[:, :n], op0=ALU.mult, op1=ALU.mult),
                    reads=[hbb, rstdb, b_const], writes=[xnb])
            else:
                A("dve", lambda c=c: nc.vector.scalar_tensor_tensor(
                    out=out_f32[:, c, :n], in0=hb[:, c, :n], scalar=gcol(c), in1=rstd[:, :n], op0=ALU.mult, op1=ALU.mult),
                    reads=[hbb, rstdb, b_const], writes=[outb])

    def hview(ap2d, t0, n):
        return ap2d[:, t0:t0 + n].rearrange("(c p) t -> p c t", p=128)

    h_src, h_src_b = xT, [Buf() for _ in range(NBh)]

    for l in range(L):
        last = (l == L - 1)
        P.barrier()
        with ExitStack() as ps_:
            w_in_sb = sb("w_in_sb", [128, 8, 3072], BF16, ps_)
            wb = [Buf(), Buf()]
            stg = [sb(f"stg1_{i}", [128, 1536], F32, ps_) for i in range(2)]
            stgb = [Buf(), Buf()]
            load_weight(w_in_sb, wb, w_in[l], 8, 3072, stg, stgb, 1536)
            hbs = [sb(f"p1_h{i}", [128, 8, 512], F32, ps_) for i in range(2)]
            hbb = [Buf(), Buf()]
            sq = sb("p1_sq", [128, 8, 512], BF16, ps_); sqb = Buf()
            lnv = sb("p1_lnv", [128, 512], F32, ps_); lnvb = Buf()
            rstd = sb("p1_rstd", [128, 512], F32, ps_); rstdb = Buf()
            xn = sb("p1_xn", [128, 8, 512], BF16, ps_); xnb = Buf()
            qst = [sb(f"p1_qst{i}", [128, 4, 512], BF16, ps_) for i in range(2)]
            qstb = [Buf(), Buf()]
            vst = [sb(f"p1_vst{i}", [128, 1024], BF16, ps_) for i in range(2)]
            vstb = [Buf(), Buf()]
            qcols = [128 * j for j in range(16)]
            vcols = [2048, 2560]

            def ld(blk):
                A("sp", lambda blk=blk: nc.sync.dma_start(out=hbs[blk % 2][:], in_=hview(h_src, blk * 512, 512)),
                  reads=[h_src_b[blk]], writes=[hbb[blk % 2]], dma=True)
            ld(0)
            ev = 0
            for blk in range(NBh):
                if blk + 1 < NBh:
                    ld(blk + 1)
                hb, hb_b = hbs[blk % 2], hbb[blk % 2]
                norm_block(hb, hb_b, 512, lambda c: gvec32[:, (l * 3 + 0) * 8 + c:(l * 3 + 0) * 8 + c + 1],
                           sq, sqb, lnv, lnvb, rstd, rstdb, xn, xnb)
                t0 = blk * 512
                for j4 in range(4):
                    st, stb = qst[j4 % 2], qstb[j4 % 2]
                    for jj in range(4):
                        j = j4 * 4 + jj
                        col = qcols[j]
                        pi = next_ps()
                        for c in range(8):
                            A("pe", lambda c=c, pi=pi, col=col: nc.tensor.matmul(
                                psum[pi][:], w_in_sb[:, c, col:col + 128], xn[:, c, :], start=(c == 0), stop=(c == 7)),
                                reads=[*wb, xnb], writes=[psb[pi]], sig=(c == 7))
                        scale = 0.125 if (j % 8) in (0, 1, 4, 5) else 1.0
                        if ev % 2 == 0:
                            A("act", lambda pi=pi, st=st, jj=jj, scale=scale: nc.scalar.activation(
                                out=st[:, jj, :], in_=psum[pi][:], func=AF.Copy, scale=scale), reads=[psb[pi]], writes=[stb])
                        else:
                            A("dve", lambda pi=pi, st=st, jj=jj, scale=scale: nc.vector.tensor_scalar(
                                out=st[:, jj, :], in0=psum[pi][:], scalar1=scale, scalar2=None, op0=ALU.mult),
                                reads=[psb[pi]], writes=[stb])
                        ev += 1
                    A("sp", lambda st=st, j4=j4, t0=t0: nc.sync.dma_start(
                        out=qk_scr[blk * 2048 + j4 * 512:blk * 2048 + (j4 + 1) * 512, :].rearrange("(c p) t -> p c t", p=128), in_=st[:]),
                        reads=[stb], writes=[qk_b[blk]], dma=True)
                for s in range(4):
                    st, stb = vst[s % 2], vstb[s % 2]
                    for half in range(2):
                        pi = next_ps()
                        vc = vcols[half]
                        for c in range(8):
                            A("pe", lambda c=c, pi=pi, s=s, vc=vc: nc.tensor.matmul(
                                psum[pi][:], xn[:, c, s * 128:(s + 1) * 128], w_in_sb[:, c, vc:vc + 512],
                                start=(c == 0), stop=(c == 7)), reads=[*wb, xnb], writes=[psb[pi]], sig=(c == 7))
                        if ev % 2 == 0:
                            A("act", lambda pi=pi, st=st, half=half: nc.scalar.activation(
                                out=st[:, half * 512:(half + 1) * 512], in_=psum[pi][:], func=AF.Copy), reads=[psb[pi]], writes=[stb])
                        else:
                            A("dve", lambda pi=pi, st=st, half=half: nc.vector.tensor_copy(
                                out=st[:, half * 512:(half + 1) * 512], in_=psum[pi][:]), reads=[psb[pi]], writes=[stb])
                        ev += 1
                    for g in range(2):
                        r0 = blk * 1024 + g * 512 + s * 128
                        A("sp", lambda st=st, r0=r0, g=g: nc.sync.dma_start(
                            out=v_scr[r0:r0 + 128, :], in_=st[:, g * 512:(g + 1) * 512]),
                            reads=[stb], writes=[v_b[blk]], dma=True)
                cc_n[0] += 1
                A("pool", lambda blk=blk: nc.gpsimd.collective_compute(
                    "AllGather", ALU.bypass, replica_groups=[[0, 1], [2, 3], [4, 5], [6, 7]],
                    ins=[qk_loc.ap()[blk * 2048:(blk + 1) * 2048, :].opt()], outs=[qk_full.ap()[blk * 4096:(blk + 1) * 4096, :].opt()]),
                  reads=[qk_b[blk]], writes=[qkfull_b], dma=True, cc=(cc_sem, cc_n[0]))
                cc_n[0] += 1
                A("pool", lambda blk=blk: nc.gpsimd.collective_compute(
                    "AllGather", ALU.bypass, replica_groups=[[0, 1], [2, 3], [4, 5], [6, 7]],
                    ins=[v_loc.ap()[blk * 1024:(blk + 1) * 1024, :].opt()], outs=[v_full.ap()[blk * 2048:(blk + 1) * 2048, :].opt()]),
                  reads=[v_b[blk]], writes=[vfull_b], dma=True, cc=(cc_sem, cc_n[0]))

        A("sp", lambda: nc.sync.dma_start(
            out=qk_mine.ap().rearrange("(b j o w) t -> b j o (w t)", j=2, o=1, w=1024),
            in_=qk_full.ap().rearrange("(b j g w) t -> b j g (w t)", j=2, g=2, w=1024)[:, :, bass.ds(rank, 1), :]),
          reads=[qkfull_b], writes=[qkmine_b], dma=True)
        A("sp", lambda: nc.sync.dma_start(
            out=v_mine.ap().rearrange("(b j o w) c -> b j o (w c)", j=2, o=1, w=512),
            in_=v_full.ap().rearrange("(b j g w) c -> b j g (w c)", j=2, g=2, w=512)[:, :, bass.ds(rank, 1), :]),
          reads=[vfull_b], writes=[vmine_b], dma=True)
        P.barrier()
        with ExitStack() as ps_:
            btoe = sb("btoe", [128, 4, 1024], F32, ps_); btb = Buf()
            negw = sb("negw", [128, 2, 1024], BF16, ps_)
            negst = sb("negst", [128, 2, 1024], F32, ps_)
            A("sp", lambda: nc.sync.dma_start(out=btoe[:], in_=btoe_d.rearrange("m p x -> p m x")), writes=[btb], dma=True)
            A("sp", lambda: nc.sync.dma_start(out=negst[:], in_=negw_d.rearrange("m p x -> p m x")), writes=[btb], dma=True)
            A("dve", lambda: nc.vector.tensor_copy(out=negw[:], in_=negst[:]), reads=[btb], writes=[btb])
            c31 = sb("c31", [128, 4], F32, ps_)
            A("dve", lambda: nc.vector.tensor_copy(out=c31[:], in_=btoe[:, :, 1023]), reads=[btb], writes=[btb])
            for m_ in range(4):
                A("dve", lambda m_=m_: nc.vector.tensor_scalar(out=btoe[:, m_, :], in0=btoe[:, m_, :], scalar1=c31[:, m_:m_ + 1],
                                                               scalar2=None, op0=ALU.subtract), reads=[btb], writes=[btb])
            qt2 = [sb(f"qt2_{i}", [128, T], BF16, ps_) for i in range(2)]
            kt2 = [sb(f"kt2_{i}", [128, T], BF16, ps_) for i in range(2)]
            v2 = [sb(f"v2_{i}", [128, NKB, 128], BF16, ps_) for i in range(2)]
            pairb = [Buf(), Buf()]
            e_t = [sb(f"e_t{i}", [128, 512], F32, ps_) for i in range(2)]; e_b = [Buf(), Buf()]
            sp_t = [sb(f"sp_t{i}", [128, 512], BF16, ps_) for i in range(2)]; sp_b = [Buf(), Buf()]
            la_t = [sb(f"la_t{i}", [128, 512], F32, ps_) for i in range(2)]; la_b = [Buf(), Buf()]
            a_t = [sb(f"a_t{i}", [128, 512], BF16, ps_) for i in range(2)]; a_b = [Buf(), Buf()]
            tc_t = [sb(f"tc_t{i}", [128, 512], F32, ps_) for i in range(2)]; tc_b = [Buf(), Buf()]
            ost = [sb(f"ost{i}", [128, 512], BF16, ps_) for i in range(2)]; ost_b = [Buf(), Buf()]
            pdw = [sb(f"pdw{i}", [128, 1024], BF16, ps_) for i in range(2)]
            pd_t = [[pdw[i][:, 512 * c:512 * c + 512] for i in range(2)] for c in range(2)]
            pdw_b = [Buf(), Buf()]
            pd_b = [[pdw_b[0], pdw_b[1]], [pdw_b[0], pdw_b[1]]]
            rsumw = sb("rsumw", [128, 1024], F32, ps_); rsumw_b = Buf()
            rsbfw = sb("rsbfw", [128, 1024], BF16, ps_); rsbfw_b = Buf()
            rw_t = sb("rw_t", [128, 1024], F32, ps_)
            r_t = [rw_t[:, 0:512], rw_t[:, 512:1024]]; r_b = [Buf(), Buf()]
            o_t = sb("o_t", [128, 512], F32, ps_); o_b = Buf()
            sq2 = sb("sq2", [128, 512], BF16, ps_); sq2b = Buf()
            ln2 = sb("ln2", [128, 512], F32, ps_); ln2b = Buf()
            rs2 = sb("rs2", [128, 512], F32, ps_); rs2b = Buf()
            mixo = sb("mixo", [128, 512], BF16, ps_); mixob = Buf()
            ew_t = sb("ew_t", [128, 1024], F32, ps_); ew_b = Buf()
            spw_ts = [sb(f"spw_t{i}", [128, 1024], BF16, ps_) for i in range(2)]; spw_bs = [Buf(), Buf()]
            law_ts = [sb(f"law_t{i}", [128, 1024], F32, ps_) for i in range(2)]; law_bs = [Buf(), Buf()]
            aw_ts = [sb(f"aw_t{i}", [128, 1024], BF16, ps_) for i in range(2)]; aw_bs = [Buf(), Buf()]
            tcw_t = sb("tcw_t", [128, 1024], F32, ps_); tcw_b = Buf()
            ostw = sb("ostw", [128, 1024], BF16, ps_); ostw_b = Buf()
            zbig = [pbig[0], pbig[1]]
            sbig, obig = pbig[2], pbig[3]
            ZB, LB, SB_, OB = (0, 1), (2, 3), (4, 5), (6, 7)

            def load_pair(u, slot):
                if u < 2:
                    qc_, kc_, vcol = u, 2 + u, 128 * u
                else:
                    qc_, kc_, vcol = 4 + (u - 2), 6 + (u - 2), 256 + 128 * (u - 2)
                qk4 = qk_mine.ap().rearrange("(b j w) t -> b j w t", j=2, w=1024)
                for (dst, c_) in ((qt2[slot], qc_), (kt2[slot], kc_)):
                    for j in range(2):
                        A("sp", lambda dst=dst, c_=c_, j=j: nc.sync.dma_start(
                            out=dst[:, j * H:(j + 1) * H].rearrange("p (b t) -> p b t", t=512),
                            in_=qk4[:, j, c_ * 128:(c_ + 1) * 128, :].rearrange("b p t -> p b t")),
                            reads=[qkmine_b], writes=[pairb[slot]], dma=True)
                for j in range(2):
                    for b_ in range(NBh):
                        kb0 = (j * H + b_ * 512) // 128
                        r0 = b_ * 1024 + j * 512
                        A("sp", lambda kb0=kb0, r0=r0: nc.sync.dma_start(
                            out=v2[slot][:, kb0:kb0 + 4, :],
                            in_=v_mine.ap()[r0:r0 + 512, vcol:vcol + 128].rearrange("(kb p) d -> p kb d", p=128)),
                            reads=[vmine_b], writes=[pairb[slot]], dma=True)

            def mix_exchange(u):
                for k in (2 * u, 2 * u + 1):
                    cc_n[0] += 1
                    A("pool", lambda k=k: nc.gpsimd.collective_compute(
                        "AllGather", ALU.bypass, replica_groups=[[0, 1], [2, 3], [4, 5], [6, 7]],
                        ins=[mix_loc.ap()[64 * k:64 * k + 64, :].opt()], outs=[mix_full.ap()[128 * k:128 * k + 128, :].opt()]),
                      reads=mix_bu[u], writes=[mixfull_b], dma=True, cc=(cc_sem, cc_n[0]))

            pending = []
            load_pair(0, 0)
            for u in range(4):
                if u >= 1:
                    mix_exchange(u - 1)
                slot = u % 2
                if u + 1 < 4:
                    load_pair(u + 1, (u + 1) % 2)
                QT, KT, V2, pb = qt2[slot], kt2[slot], v2[slot], pairb[slot]
                is_sb = u < 2
                for qc in range(NB):
                    q0 = qc * 512
                    nkb = 4 * (qc + 1)
                    order = list(range(nkb - 1, -1, -1))

                    def qk(i, bank, ch, last_stop):
                        kb = order[i]
                        k0 = kb * 128
                        diag = k0 >= q0
                        lo = 64 * ch
                        A("pe", lambda: nc.tensor.matmul(psum[bank][:], KT[lo:lo + 64, k0:k0 + 128], QT[lo:lo + 64, q0:q0 + 512],
                                                         start=True, stop=(last_stop and not (diag and is_sb))),
                          reads=[pb], writes=[psb[bank]])
                        if diag and is_sb:
                            c0 = k0 - q0
                            A("pe", lambda: nc.tensor.matmul(psum[bank][:], ident[:], negw[:, 0, 512 - c0:1024 - c0],
                                                             start=False, stop=last_stop),
                              reads=[btb, b_const], writes=[psb[bank]])

                    if is_sb:
                        def zqk(i):
                            kb = order[i]
                            k0 = kb * 128
                            diag = k0 >= q0
                            zt, zk = zbig[i % 2], 2 * (i % 2)
                            for ch in range(2):
                                lo = 64 * ch
                                A("pe", lambda lo=lo, ch=ch: nc.tensor.matmul(
                                    zt[:, 512 * ch:512 * ch + 512], KT[lo:lo + 64, k0:k0 + 128], QT[lo:lo + 64, q0:q0 + 512],
                                    start=True, stop=not diag), reads=[pb], writes=[psb[zk + ch]], sig=(ch == 1 and not diag))
                            if diag:
                                c0 = k0 - q0
                                for ch in range(2):
                                    A("pe", lambda ch=ch: nc.tensor.matmul(
                                        zt[:, 512 * ch:512 * ch + 512], ident[:], negw[:, 0, 512 - c0:1024 - c0],
                                        start=False, stop=True), reads=[btb, b_const], writes=[psb[zk + ch]], sig=(ch == 1))

                        def act_A(i):
                            A("act", lambda: nc.scalar.activation(out=aw_ts[i % 2][:], in_=law_ts[i % 2][:], func=AF.Exp),
                              reads=[law_bs[i % 2]], writes=[aw_bs[i % 2]])

                        def pe_PV(i):
                            kbp = order[i]
                            for ch in range(2):
                                A("pe", lambda ch=ch: nc.tensor.matmul(
                                    obig[0:64, 512 * ch:512 * ch + 512], V2[:, kbp, 64 * ch:64 * ch + 64], aw_ts[i % 2][:, 512 * ch:512 * ch + 512],
                                    start=(i == 0), stop=(i == nkb - 1)), reads=[pb, aw_bs[i % 2]], writes=[psb[6 + ch]], sig=(ch == 1))

                        zqk(0)
                        for i in range(nkb):
                            zt, zk = zbig[i % 2], 2 * (i % 2)
                            zpair = [psb[zk], psb[zk + 1]]
                            A("act", lambda zt=zt: nc.scalar.activation(out=ew_t[:], in_=zt[:], func=AF.Exp),
                              reads=zpair, writes=[ew_b])
                            if i + 1 < nkb:
                                zqk(i + 1)
                            spw_t, spw_b = spw_ts[i % 2], spw_bs[i % 2]
                            law_t, law_b = law_ts[i % 2], law_bs[i % 2]
                            A("act", lambda spw_t=spw_t: nc.scalar.activation(out=spw_t[:], in_=ew_t[:], func=AF.Ln, bias=1.0, scale=1.0),
                              reads=[ew_b], writes=[spw_b])
                            for ch in range(2):
                                A("pe", lambda ch=ch, zt=zt, spw_t=spw_t: nc.tensor.matmul(
                                    zt[:, 512 * ch:512 * ch + 512], ntri[:], spw_t[:, 512 * ch:512 * ch + 512],
                                    start=False, stop=True, skip_group_check=True),
                                  reads=[spw_b, b_const], writes=[psb[zk + ch]], sig=(ch == 1))
                            if i + 1 < nkb:
                                for ch in range(2):
                                    A("pe", lambda ch=ch, spw_t=spw_t: nc.tensor.matmul(
                                        sbig[:, 512 * ch:512 * ch + 512], ones[:], spw_t[:, 512 * ch:512 * ch + 512],
                                        start=True, stop=True), reads=[spw_b, b_const], writes=[psb[4 + ch]], sig=(ch == 1))
                            if i >= 1:
                                act_A(i - 1)
                                pe_PV(i - 1)
                            if i == 0:
                                A("dve", lambda zt=zt, law_t=law_t: nc.vector.tensor_copy(out=law_t[:], in_=zt[:]),
                                  reads=zpair, writes=[law_b])
                            else:
                                A("dve", lambda zt=zt, law_t=law_t: nc.vector.tensor_tensor(out=law_t[:], in0=zt[:], in1=tcw_t[:], op=ALU.subtract),
                                  reads=zpair + [tcw_b], writes=[law_b])
                            if i + 1 < nkb:
                                if i == 0:
                                    A("dve", lambda: nc.vector.tensor_copy(out=tcw_t[:], in_=sbig[:]),
                                      reads=[psb[4], psb[5]], writes=[tcw_b])
                                else:
                                    A("dve", lambda: nc.vector.tensor_tensor(out=tcw_t[:], in0=sbig[:], in1=tcw_t[:], op=ALU.add),
                                      reads=[psb[4], psb[5], tcw_b], writes=[tcw_b])
                        act_A(nkb - 1)
                        pe_PV(nkb - 1)
                        A("dve", lambda: nc.vector.tensor_copy(out=ostw[0:64, :], in_=obig[0:64, :]),
                          reads=[psb[6], psb[7]], writes=[ostw_b])
                        A("sp", lambda: nc.sync.dma_start(
                            out=mix_scr[128 * u:128 * u + 128, q0:q0 + 512].rearrange("(c p) t -> p c t", p=64),
                            in_=ostw[0:64, :].rearrange("p (c t) -> p c t", c=2)),
                          reads=[ostw_b], writes=[mix_bu[u][qc]], dma=True)
                    else:
                        hd = u - 2
                        zb = [(0, 1), (2, 3)]
                        for ch in range(2):
                            qk(0, zb[0][ch], ch, True)
                        for i in range(nkb):
                            if i == 2 and pending:
                                pending.pop()()
                            kb = order[i]
                            k0 = kb * 128
                            near = (k0 >= q0 - 128)
                            zz = zb[i % 2]
                            if i + 1 < nkb:
                                for ch in range(2):
                                    qk(i + 1, zb[(i + 1) % 2][ch], ch, True)
                            if near:
                                for ch in range(2):
                                    m = 2 * hd + ch
                                    pt, ptb = pd_t[ch][i % 2], pd_b[ch][i % 2]
                                    x0 = q0 - k0 + 384
                                    A("dve", lambda ch=ch, m=m, x0=x0, zz=zz: nc.vector.tensor_tensor(
                                        out=la_t[ch][:], in0=psum[zz[ch]][:], in1=btoe[:, m, x0:x0 + 512], op=ALU.add),
                                        reads=[psb[zz[ch]], btb], writes=[la_b[ch]])
                                    A("act", lambda ch=ch, pt=pt: nc.scalar.activation(out=pt[:], in_=la_t[ch][:], func=AF.Exp),
                                      reads=[la_b[ch]], writes=[ptb])
                            else:
                                A("act", lambda i=i: nc.scalar.activation(out=pdw[i % 2][:], in_=pbig[i % 2][:], func=AF.Exp),
                                  reads=[psb[zz[0]], psb[zz[1]]], writes=[pdw_b[i % 2]])
                            for ch in range(2):
                                pt, ptb = pd_t[ch][i % 2], pd_b[ch][i % 2]
                                A("pe", lambda ch=ch, kb=kb, i=i, pt=pt: nc.tensor.matmul(psum[OB[ch]][:], V2[:, kb, :], pt[:],
                                                                                        start=(i == 0), stop=(i == nkb - 1)),
                                  reads=[pb, ptb], writes=[psb[OB[ch]]])
                            if i == 0:
                                A("dve", lambda i=i: nc.vector.tensor_copy(out=rsumw[:], in_=pdw[i % 2][:]),
                                  reads=[pdw_b[i % 2]], writes=[rsumw_b])
                            else:
                                A("dve", lambda i=i: nc.vector.tensor_tensor(out=rsumw[:], in0=pdw[i % 2][:], in1=rsumw[:], op=ALU.add),
                                  reads=[pdw_b[i % 2], rsumw_b], writes=[rsumw_b])
                        A("dve", lambda: nc.vector.tensor_copy(out=rsbfw[:], in_=rsumw[:]), reads=[rsumw_b], writes=[rsbfw_b])
                        for ch in range(2):
                            A("pe", lambda ch=ch: nc.tensor.matmul(psum[SB_[ch]][:], ones[:], rsbfw[:, 512 * ch:512 * ch + 512], start=True, stop=True),
                              reads=[rsbfw_b, b_const], writes=[psb[SB_[ch]]])
                        A("act", lambda: nc.scalar.activation(out=rw_t[:], in_=pbig[2][:], func=AF.Ln),
                          reads=[psb[4], psb[5]], writes=[r_b[0], r_b[1]])
                        A("act", lambda: nc.scalar.activation(out=rw_t[:], in_=rw_t[:], func=AF.Exp, scale=-1.0),
                          reads=[r_b[0], r_b[1]], writes=[r_b[0], r_b[1]])
                        for ch in range(2):
                            A("dve", lambda ch=ch: nc.vector.tensor_tensor(out=r_t[ch][:], in0=psum[OB[ch]][:], in1=r_t[ch][:], op=ALU.mult),
                              reads=[psb[OB[ch]], r_b[ch]], writes=[r_b[ch]])
                        A("dve", lambda: nc.vector.scalar_tensor_tensor(out=o_t[:], in0=r_t[1][:], scalar=neglam[:, l:l + 1], in1=r_t[0][:],
                                                                        op0=ALU.mult, op1=ALU.add),
                          reads=[r_b[0], r_b[1], b_const], writes=[o_b])
                        def tail(q0=q0, qc=qc, u=u, hd=hd):
                            A("act", lambda: nc.scalar.activation(out=sq2[:], in_=o_t[:], func=AF.Square), reads=[o_b], writes=[sq2b])
                            A("pe", lambda: nc.tensor.matmul(psum[5][:], ones[:], sq2[:], start=True, stop=True),
                              reads=[sq2b, b_const], writes=[psb[5]])
                            A("act", lambda: nc.scalar.activation(out=ln2[:], in_=psum[5][:], func=AF.Ln, bias=128.0 * EPS, scale=1.0),
                              reads=[psb[5]], writes=[ln2b])
                            A("act", lambda: nc.scalar.activation(out=rs2[:], in_=ln2[:], func=AF.Exp, scale=-0.5), reads=[ln2b], writes=[rs2b])
                            A("dve", lambda: nc.vector.scalar_tensor_tensor(out=mixo[:], in0=o_t[:], scalar=gsub2[:, l:l + 1], in1=rs2[:],
                                                                            op0=ALU.mult, op1=ALU.mult),
                              reads=[o_b, rs2b, b_const], writes=[mixob])
                            row = 256 + 128 * hd
                            A("sp", lambda: nc.sync.dma_start(out=mix_scr[row:row + 128, q0:q0 + 512], in_=mixo[:]),
                              reads=[mixob], writes=[mix_bu[u][qc]], dma=True)
                        pending.append(tail)
                if pending:
                    pending.pop()()

        for k in (6, 7):
            cc_n[0] += 1
            A("pool", lambda k=k: nc.gpsimd.collective_compute(
                "AllGather", ALU.bypass, replica_groups=[[0, 1], [2, 3], [4, 5], [6, 7]],
                ins=[mix_loc.ap()[64 * k:64 * k + 64, :].opt()], outs=[mix_full.ap()[128 * k:128 * k + 128, :].opt()]),
              reads=mix_bu[3], writes=[mixfull_b], dma=True, cc=(cc_sem, cc_n[0]))
        A("sp", lambda: nc.sync.dma_start(out=mix_mine.ap(), in_=mix_full.ap()[:, bass.ds(r_h, H)]),
          reads=[mixfull_b], writes=[mixmine_b], dma=True)
        P.barrier()
        ffn_scope = ExitStack()
        w_up_sb = sb("w_up_sb", [128, 8, 2 * DFF], BF16, ffn_scope); wub = [Buf(), Buf()]
        stg_f = [sb(f"stg3b_{i}", [128, 1408], F32, ffn_scope) for i in range(2)]; stgb_f = [Buf(), Buf()]
        up_pieces = weight_pieces(w_up_sb, wub, w_up[l], 8, 2 * DFF, stg_f, stgb_f, 1408)
        with ExitStack() as ps_:
            w_o_sb = sb("w_o_sb", [128, 8, 1024], BF16, ps_); wb = [Buf(), Buf()]
            stg = [sb(f"stg3a_{i}", [128, 1024], F32, ps_) for i in range(2)]; stgb = [Buf(), Buf()]
            load_weight(w_o_sb, wb, w_o[l], 8, 1024, stg, stgb, 1024)
            hbs = [sb(f"p3a_h{i}", [128, 8, 512], F32, ps_) for i in range(2)]; hbb = [Buf(), Buf()]
            mxs = [sb(f"p3a_m{i}", [128, 8, 512], BF16, ps_) for i in range(2)]; mxb = [Buf(), Buf()]
            sq = sb("p3a_sq", [128, 8, 512], BF16, ps_); sqb = Buf()
            lnv = sb("p3a_lnv", [128, 512], F32, ps_); lnvb = Buf()
            rstd = sb("p3a_rstd", [128, 512], F32, ps_); rstdb = Buf()
            xns = [sb(f"p3a_xn{i}", [128, 8, 512], BF16, ps_) for i in range(1)] * 2; xnb = [Buf()] * 2

            def ld(blk):
                A("sp", lambda blk=blk: nc.sync.dma_start(out=hbs[blk % 2][:], in_=hview(h_src, blk * 512, 512)),
                  reads=[h_src_b[blk]], writes=[hbb[blk % 2]], dma=True)
                A("sp", lambda blk=blk: nc.sync.dma_start(out=mxs[blk % 2][:], in_=hview(mix_mine.ap(), blk * 512, 512)),
                  reads=[mixmine_b], writes=[mxb[blk % 2]], dma=True)
            ld(0)
            for blk in range(NBh):
                if blk + 1 < NBh:
                    ld(blk + 1)
                hb, hb_b, mx, mx_b = hbs[blk % 2], hbb[blk % 2], mxs[blk % 2], mxb[blk % 2]
                for oc in range(8):
                    pi = next_ps()
                    for c in range(8):
                        A("pe", lambda c=c, oc=oc, pi=pi: nc.tensor.matmul(psum[pi][:], w_o_sb[:, c, oc * 128:(oc + 1) * 128], mx[:, c, :],
                                                                         start=(c == 0), stop=(c == 7)),
                          reads=[*wb, mx_b], writes=[psb[pi]], sig=(c == 7))
                    A("dve", lambda oc=oc, pi=pi: nc.vector.tensor_tensor(out=hb[:, oc, :], in0=psum[pi][:], in1=hb[:, oc, :], op=ALU.add),
                      reads=[psb[pi], hb_b], writes=[hb_b])
                A("sp", lambda blk=blk, hb=hb: nc.sync.dma_start(out=hview(hA, blk * 512, 512), in_=hb[:]),
                  reads=[hb_b], writes=[hA_b[blk]], dma=True)
                xn, xn_b = xns[blk % 2], xnb[blk % 2]
                norm_block(hb, hb_b, 512, lambda c: gvec32[:, (l * 3 + 1) * 8 + c:(l * 3 + 1) * 8 + c + 1],
                           sq, sqb, lnv, lnvb, rstd, rstdb, xn, xn_b)
                A("sp", lambda blk=blk, xn=xn: nc.sync.dma_start(out=hview(xn2_scr, blk * 512, 512), in_=xn[:]),
                  reads=[xn_b], writes=[xn2_b[blk]], dma=True)
                per = (len(up_pieces) + NBh - 1) // NBh
                for pc in up_pieces[blk * per:(blk + 1) * per]:
                    pc()
                if blk == NBh - 1:
                    A("sp", lambda xn=xn: nc.sync.dma_start(out=tail_loc.ap().rearrange("(c p) t -> p c t", p=128), in_=xn[:, :, 510:512]),
                      reads=[xn_b], writes=[tail_b], dma=True)
        cc_n[0] += 1
        A("pool", lambda: nc.gpsimd.collective_compute(
            "AllGather", ALU.bypass, replica_groups=[[0, 1], [2, 3], [4, 5], [6, 7]],
            ins=[tail_loc.ap().opt()], outs=[tail_full.ap().opt()]),
          reads=[tail_b], writes=[tailfull_b], dma=True, cc=(cc_sem, cc_n[0]))

        P.barrier()
        with ExitStack() as ps_:
            w_dn_sb = sb("w_dn_sb", [128, NFC, 1024], BF16, ps_); wdb = [Buf(), Buf()]
            load_weight(w_dn_sb, wdb, w_down[l], NFC, 1024, stg_f, stgb_f, 1024)
            NT = 256
            xbs = [sb(f"p3b_x{i}", [128, 8, NT + 2], BF16, ps_) for i in range(2)]; xbb = [Buf(), Buf()]
            hbs = [sb(f"p3b_h{i}", [128, 8, NT], F32, ps_) for i in range(2)]; hbb = [Buf(), Buf()]
            act_t = sb("p3b_act", [128, NFC, NT], BF16, ps_); actb = [Buf() for _ in range(NFC)]
            yg = [sb(f"p3b_yg{i}", [128, NT], F32, ps_) for i in range(2)]; ygb = [Buf(), Buf()]
            yv = [sb(f"p3b_yv{i}", [128, NT], F32, ps_) for i in range(2)]; yvb = [Buf(), Buf()]
            gg = [sb(f"p3b_gg{i}", [128, NT], F32, ps_) for i in range(2)]; ggb = [Buf(), Buf()]
            nblk = H // NT
            tl = sb("p3b_tl", [128, 8, 2], BF16, ps_); tlb = Buf()

            def ld(b):
                t0 = b * NT
                x_, xb_ = xbs[b % 2], xbb[b % 2]
                rb = [xn2_b[t0 // 512]] + ([xn2_b[(t0 - 2) // 512]] if t0 > 0 else [])
                if b == 0:
                    A("sp", lambda: nc.sync.dma_start(out=tl[:], in_=tail_full.ap()[0:D, :].rearrange("(c p) t -> p c t", p=128)),
                      reads=[tailfull_b], writes=[tlb], dma=True)
                    A("dve", lambda x_=x_: nc.vector.tensor_scalar(out=x_[:, :, 0:2], in0=tl[:], scalar1=hmask[:, 0:1], scalar2=None,
                                                                   op0=ALU.mult), reads=[tlb, b_const], writes=[xb_])
                    A("sp", lambda x_=x_: nc.sync.dma_start(out=x_[:, :, 2:NT + 2], in_=hview(xn2_scr, 0, NT)),
                      reads=rb, writes=[xb_], dma=True)
                else:
                    A("sp", lambda x_=x_, t0=t0: nc.sync.dma_start(out=x_[:], in_=hview(xn2_scr, t0 - 2, NT + 2)),
                      reads=rb, writes=[xb_], dma=True)
                A("sp", lambda b=b, t0=t0: nc.sync.dma_start(out=hbs[b % 2][:], in_=hview(hA, t0, NT)),
                  reads=[hA_b[t0 // 512]], writes=[hbb[b % 2]], dma=True)
            ld(0)
            cwo = lambda k, j: (l * 3 + k) * 44 + j
            for b in range(nblk):
                if b + 1 < nblk:
                    ld(b + 1)
                x_, xb_, hb, hb_b = xbs[b % 2], xbb[b % 2], hbs[b % 2], hbb[b % 2]
                for i in range(NFC):
                    pg, pv = next_ps(), next_ps()
                    for (pi, cbase) in ((pg, 128 * i), (pv, DFF + 128 * i)):
                        for c in range(8):
                            A("pe", lambda c=c, pi=pi, cbase=cbase: nc.tensor.matmul(
                                psum[pi][:, :NT + 2], w_up_sb[:, c, cbase:cbase + 128], x_[:, c, :], start=(c == 0), stop=(c == 7)),
                                reads=[*wub, xb_], writes=[psb[pi]], sig=(c == 7))
                    s = i % 2
                    for (pi, y, yb, j) in ((pg, yg[s], ygb[s], i), (pv, yv[s], yvb[s], NFC + i)):
                        A("act", lambda pi=pi, y=y, j=j: nc.scalar.activation(
                            out=y[:], in_=psum[pi][:, 2:NT + 2], func=AF.Identity,
                            bias=cb[:, l * 44 + j:l * 44 + j + 1], scale=cw[:, cwo(2, j):cwo(2, j) + 1]),
                            reads=[psb[pi], b_const], writes=[yb])
                        A("dve", lambda pi=pi, y=y, j=j: nc.vector.scalar_tensor_tensor(
                            out=y[:], in0=psum[pi][:, 1:NT + 1], scalar=cw[:, cwo(1, j):cwo(1, j) + 1], in1=y[:],
                            op0=ALU.mult, op1=ALU.add), reads=[psb[pi], yb, b_const], writes=[yb])
                        A("dve", lambda pi=pi, y=y, j=j: nc.vector.scalar_tensor_tensor(
                            out=y[:], in0=psum[pi][:, 0:NT], scalar=cw[:, cwo(0, j):cwo(0, j) + 1], in1=y[:],
                            op0=ALU.mult, op1=ALU.add), reads=[psb[pi], yb, b_const], writes=[yb])
                    A("act", lambda s=s: nc.scalar.activation(out=gg[s][:], in_=yg[s][:], func=AF.Gelu_apprx_tanh),
                      reads=[ygb[s]], writes=[ggb[s]])
                    A("pool", lambda s=s, i=i: nc.gpsimd.tensor_tensor(out=act_t[:, i, :], in0=gg[s][:], in1=yv[s][:], op=ALU.mult),
                      reads=[ggb[s], yvb[s]], writes=[actb[i]])
                for oc in range(8):
                    pi = next_ps()
                    for i in range(NFC):
                        A("pe", lambda i=i, oc=oc, pi=pi: nc.tensor.matmul(psum[pi][:, :NT], w_dn_sb[:, i, oc * 128:(oc + 1) * 128], act_t[:, i, :],
                                                                         start=(i == 0), stop=(i == NFC - 1)),
                          reads=[*wdb, actb[i]], writes=[psb[pi]], sig=(i == NFC - 1))
                    A("dve", lambda oc=oc, pi=pi: nc.vector.tensor_tensor(out=hb[:, oc, :], in0=psum[pi][:, :NT], in1=hb[:, oc, :], op=ALU.add),
                      reads=[psb[pi], hb_b], writes=[hb_b])
                A("sp", lambda b=b, hb=hb: nc.sync.dma_start(out=hview(hB, b * NT, NT), in_=hb[:]),
                  reads=[hb_b], writes=[hB_b[(b * NT) // 512]], dma=True)

        P.barrier()
        ffn_scope.close()
        with ExitStack() as ps_:
            w_pg_sb = sb("w_pg_sb", [128, 8, 1024], BF16, ps_); wgb = [Buf(), Buf()]
            w_pp_sb = sb("w_pp_sb", [128, 2, 1024], BF16, ps_); wpb = [Buf(), Buf()]
            stg = [sb(f"stg3c_{i}", [128, 1024], F32, ps_) for i in range(2)]; stgb = [Buf(), Buf()]
            load_weight(w_pg_sb, wgb, w_pg[l], 8, 1024, stg, stgb, 1024)
            load_weight(w_pp_sb, wpb, w_pp[l], 2, 1024, stg, stgb, 1024)
            hbs = [sb(f"p3c_h{i}", [128, 8, 512], F32, ps_) for i in range(2)]; hbb = [Buf(), Buf()]
            pfs = [sb(f"p3c_pf{i}", [128, 2, 512], F32, ps_) for i in range(2)]; pfb = [Buf(), Buf()]
            pbf = sb("p3c_pb", [128, 2, 512], BF16, ps_); pbb = Buf()
            sq = sb("p3c_sq", [128, 8, 512], BF16, ps_); sqb = Buf()
            lnv = sb("p3c_lnv", [128, 512], F32, ps_); lnvb = Buf()
            rstd = sb("p3c_rstd", [128, 512], F32, ps_); rstdb = Buf()
            xn = sb("p3c_xn", [128, 8, 512], BF16, ps_); xnb = Buf()
            sg = [sb(f"p3c_sg{i}", [128, 512], F32, ps_) for i in range(2)]; sgb = [Buf(), Buf()]
            outs = [sb(f"p3c_o{i}", [128, 8, 512], F32, ps_) for i in range(2)] if last else None
            outb = [Buf(), Buf()]
            dst, dst_b = (hC, hC_b)

            def ld(blk):
                A("sp", lambda blk=blk: nc.sync.dma_start(out=hbs[blk % 2][:], in_=hview(hB, blk * 512, 512)),
                  reads=[hB_b[blk]], writes=[hbb[blk % 2]], dma=True)
                A("sp", lambda blk=blk: nc.sync.dma_start(out=pfs[blk % 2][:], in_=pT[l][:, blk * 512:(blk + 1) * 512].rearrange("(c p) t -> p c t", p=128)),
                  writes=[pfb[blk % 2]], dma=True)
            ld(0)
            for blk in range(NBh):
                if blk + 1 < NBh:
                    ld(blk + 1)
                hb, hb_b = hbs[blk % 2], hbb[blk % 2]
                norm_block(hb, hb_b, 512, lambda c: gvec32[:, (l * 3 + 2) * 8 + c:(l * 3 + 2) * 8 + c + 1],
                           sq, sqb, lnv, lnvb, rstd, rstdb, xn, xnb)
                A("pool", lambda blk=blk: nc.gpsimd.tensor_copy(out=pbf[:], in_=pfs[blk % 2][:]), reads=[pfb[blk % 2]], writes=[pbb])
                for oc in range(8):
                    pg, pp = next_ps(), next_ps()
                    for c in range(8):
                        A("pe", lambda c=c, oc=oc, pg=pg: nc.tensor.matmul(psum[pg][:], w_pg_sb[:, c, oc * 128:(oc + 1) * 128], xn[:, c, :],
                                                                         start=(c == 0), stop=(c == 7)),
                          reads=[*wgb, xnb], writes=[psb[pg]], sig=(c == 7))
                    for c in range(2):
                        A("pe", lambda c=c, oc=oc, pp=pp: nc.tensor.matmul(psum[pp][:], w_pp_sb[:, c, oc * 128:(oc + 1) * 128], pbf[:, c, :],
                                                                         start=(c == 0), stop=(c == 1)),
                          reads=[*wpb, pbb], writes=[psb[pp]], sig=(c == 1))
                    s = oc % 2
                    A("act", lambda s=s, pg=pg: nc.scalar.activation(out=sg[s][:], in_=psum[pg][:], func=AF.Sigmoid),
                      reads=[psb[pg]], writes=[sgb[s]])
                    A("dve", lambda s=s, pp=pp: nc.vector.tensor_tensor(out=sg[s][:], in0=psum[pp][:], in1=sg[s][:], op=ALU.mult),
                      reads=[psb[pp], sgb[s]], writes=[sgb[s]])
                    A("dve", lambda s=s, oc=oc: nc.vector.tensor_tensor(out=hb[:, oc, :], in0=sg[s][:], in1=hb[:, oc, :], op=ALU.add),
                      reads=[sgb[s], hb_b], writes=[hb_b])
                if not last:
                    A("sp", lambda blk=blk, hb=hb: nc.sync.dma_start(out=hview(dst, blk * 512, 512), in_=hb[:]),
                      reads=[hb_b], writes=[dst_b[blk]], dma=True)
                else:
                    o_, ob_ = outs[blk % 2], outb[blk % 2]
                    norm_block(hb, hb_b, 512, lambda c: gfin32[:, c:c + 1], sq, sqb, lnv, lnvb, rstd, rstdb, None, None,
                               out_f32=o_, outb=ob_)
                    A("sp", lambda blk=blk, o_=o_: nc.sync.dma_start(out=hview(outT, blk * 512, 512), in_=o_[:]),
                      reads=[ob_], writes=[dst_b[blk]], dma=True)
        h_src, h_src_b = hC, hC_b

    P.barrier()
    A("pool", None, reads=hC_b)
    A("sp", None, reads=hC_b)
    n, nw = len(P.meta), (P.n_wait, getattr(P, 'n_standalone', 0), dict(P.cnt))
    es.close()
    return nc, (n, nw)


def _rel_bucket_np(dist):
    max_exact = 16
    d = np.maximum(dist, 1).astype(np.float32)
    large = max_exact + (np.log(d / np.float32(max_exact)) / np.float32(math.log(128 / max_exact))
                         * np.float32(32 - max_exact)).astype(np.int32)
    large = np.minimum(large, 31)
    return np.where(dist < max_exact, dist, large)


def host_prep(T, L, x_b, p_b, w, rank=0):
    f = np.float32
    m = {}
    m["xT"] = np.ascontiguousarray(x_b.T)
    m["pT"] = np.ascontiguousarray(np.transpose(p_b, (0, 2, 1)))
    for k_src, k_dst in (("w_up", "w_up"), ("w_down", "w_down"),
                         ("w_ple_gate", "w_pg"), ("w_ple_proj", "w_pp")):
        m[k_dst] = np.ascontiguousarray(w[k_src][:L])
    r = rank
    dh = [2 * r, 2 * r + 1]
    cols = []
    for g in range(2):
        sbp = [2 * g, 2 * g + 1]
        dhg = [2 * g, 2 * g + 1]
        for u in sbp:
            cols += list(range(128 * u, 128 * u + 128))
        for u in sbp:
            cols += list(range(512 + 128 * u, 512 + 128 * u + 128))
        for d_ in dhg:
            cols += list(range(1536 + 128 * d_, 1536 + 128 * d_ + 128))
        for d_ in dhg:
            cols += list(range(2048 + 128 * d_, 2048 + 128 * d_ + 128))
    for g in range(2):
        for u in (2 * g, 2 * g + 1):
            cols += list(range(1024 + 128 * u, 1024 + 128 * u + 128))
        for d_ in (2 * g, 2 * g + 1):
            cols += list(range(2560 + 128 * d_, 2560 + 128 * d_ + 128))
    m["w_in"] = np.ascontiguousarray(w["w_in"][:L][:, :, cols])
    m["hmask"] = np.full((128, 1), float(rank), np.float32)
    def orig_row(rr, rho):
        return 256 * rr + rho if rho < 256 else 512 + 256 * rr + (rho - 256)
    rows = [orig_row(rr, 64 * k + i) for k in range(8) for rr in range(2) for i in range(64)]
    m["w_o"] = np.ascontiguousarray(w["w_o"][:L][:, rows, :])
    g = np.stack([w["g_attn"][:L], w["g_ffn"][:L], w["g_ple"][:L]], axis=1)
    m["gvec"] = np.ascontiguousarray(g.reshape(L, 3, 8, 128).transpose(3, 0, 1, 2).reshape(128, L * 3 * 8))
    m["gfin"] = np.ascontiguousarray(w["g_final"].reshape(8, 128).T)
    m["convw"] = np.ascontiguousarray(w["conv_w"][:L].reshape(L, 3, 44, 128).transpose(3, 0, 1, 2).reshape(128, L * 3 * 44))
    m["convb"] = np.ascontiguousarray(w["conv_b"][:L].reshape(L, 44, 128).transpose(2, 0, 1).reshape(128, L * 44))
    m["gsub"] = np.ascontiguousarray(w["g_subln"][:L].T)
    lam = np.stack([w["lambda_q1"][:L], w["lambda_k1"][:L], w["lambda_q2"][:L], w["lambda_k2"][:L]], axis=1)
    m["lamv"] = np.ascontiguousarray(np.broadcast_to(lam.reshape(1, L * 4 * 64), (128, L * 4 * 64)))
    kl = np.arange(128)[:, None]
    xx = np.arange(1024)[None, :]
    dd = xx - 384 - kl
    idx = _rel_bucket_np(np.maximum(dd, 0))
    maps = [2 * d_ + j for d_ in dh for j in range(2)]
    bt = np.transpose(w["rel_bias"][:, maps][idx], (2, 0, 1)).astype(f)
    bt = np.where((dd < 0)[None], f(NEG), bt)
    m["btoe"] = np.ascontiguousarray(bt)
    negw = np.zeros((2, 128, 1024), f)
    xq = xx - 512
    negw[0] = np.where(xq <= kl, NEG, 0.0)
    negw[1] = np.where(xq < kl, NEG, 0.0)
    m["negw"] = negw
    cst = np.zeros((3, 128, 128), f)
    cst[0] = np.eye(128, dtype=f)
    jj = np.arange(128)[:, None]; ss = np.arange(128)[None, :]
    cst[1] = np.where(jj >= ss, -1.0, 0.0)
    cst[2] = 1.0
    m["cst"] = cst
    return {k: np.ascontiguousarray(v, dtype=f) for k, v in m.items()}


_CACHE = {}


def kernel(**inputs):
    x = np.asarray(inputs["x"], np.float32)
    p = np.asarray(inputs["p"], np.float32)
    w = {k: np.asarray(v, np.float32) for k, v in inputs.items() if k not in ("x", "p")}
    B, T, _ = x.shape
    L = p.shape[0]
    H = T // 2
    key = (T, L)
    if key not in _CACHE:
        _CACHE[key] = build_program(T, L)[0]
    nc = _CACHE[key]
    n_cores = 8
    maps = []
    per = {}
    for core in range(n_cores):
        b = (core * B) // n_cores
        r = core % 2
        maps.append(host_prep(T, L, x[b, r * H:(r + 1) * H], p[:, b, r * H:(r + 1) * H], w, rank=r))
    res = run_bass_kernel_spmd(nc, maps, core_ids=list(range(n_cores)))
    out = np.empty((B, T, D), np.float32)
    for core in range(n_cores):
        b, r = (core * B) // n_cores, core % 2
        out[b, r * H:(r + 1) * H] = res.results[core]["outT"].T
    return out
```

```python
import math
from contextlib import ExitStack
import numpy as np
import concourse.bass as bass
import concourse.mybir as mybir
from concourse.bass_utils import run_bass_kernel_spmd

F32 = mybir.dt.float32
BF16 = mybir.dt.bfloat16
AF = mybir.ActivationFunctionType
ALU = mybir.AluOpType
AX = mybir.AxisListType

D = 1024
DFF = 2816
NFC = 22
PLE = 256
EPS = 1e-6
NEG = -30000.0
SEM_LIM = 30000
N_DMA_SEM = 8


class Buf:
    __slots__ = ("w", "wd", "r", "rd")

    def __init__(self):
        self.w = None
        self.wd = []
        self.r = {}
        self.rd = []


class Prog:
    def __init__(self, nc, es):
        self.nc = nc
        self.es = es
        self.eng = {"pe": nc.tensor, "act": nc.scalar, "dve": nc.vector,
                    "pool": nc.gpsimd, "sp": nc.sync}
        self.meta = []
        self.cnt = {e: 0 for e in self.eng}
        self.sems = {e: [] for e in self.eng}
        self.dsems = {e: [es.enter_context(nc.semaphore(f"d_{e}_{j}")) for j in range(N_DMA_SEM)]
                      for e in ("sp", "pool")}
        self.dcount = {"sp": 0, "pool": 0}
        self.waited = {e: {p: 0 for p in self.eng} for e in self.eng}
        self.dwaited = {e: {} for e in self.eng}
        self.n_wait = 0

    def _sem(self, eng, g):
        j, v = (g - 1) // SEM_LIM, (g - 1) % SEM_LIM + 1
        lst = self.sems[eng]
        while len(lst) <= j:
            lst.append(self.es.enter_context(self.nc.semaphore(f"s_{eng}_{len(lst)}")))
        return lst[j], v

    def barrier(self):
        for eng, E in self.eng.items():
            for p in self.eng:
                g = self.cnt[p]
                if g == 0 or self.waited[eng][p] >= g:
                    continue
                self.waited[eng][p] = g
                s, v = self._sem(p, g)
                E.wait_ge(s, v)
                self.n_wait += 1
            for q in ("sp", "pool"):
                k = self.dcount[q]
                for j in range(min(k, N_DMA_SEM)):
                    last_k = ((k - 1 - j) // N_DMA_SEM) * N_DMA_SEM + j
                    v = 16 * (last_k // N_DMA_SEM + 1)
                    s = self.dsems[q][j]
                    if self.dwaited[eng].get(id(s), 0) < v:
                        self.dwaited[eng][id(s)] = v
                        E.wait_ge(s, v)
                        self.n_wait += 1

    def add(self, eng, fn, reads=(), writes=(), dma=False, sig=True, cc=None):
        i = len(self.meta)
        deps = set()
        for b in reads:
            if b.w is not None:
                deps.add(b.w)
            deps.update(b.wd)
        for b in writes:
            if b.w is not None:
                deps.add(b.w)
            if not dma or b.r or b.rd:
                deps.update(b.wd)
            deps.update(b.r.values())
            deps.update(b.rd)
        for b in writes:
            if dma:
                if b.r or b.rd:
                    b.wd = []
                b.wd.append(i)
            else:
                b.w = i
                b.wd = []
            b.r = {}
            b.rd = []
        for b in reads:
            if dma:
                b.rd.append(i)
            else:
                b.r[eng] = i
        deps.discard(i)
        E = self.eng[eng]
        need = {}
        waits = []
        for d in deps:
            deng, ddma, info = self.meta[d]
            if ddma:
                s, v = info
                key = id(s)
                if self.dwaited[eng].get(key, 0) < v:
                    self.dwaited[eng][key] = v
                    waits.append((s, v))
                continue
            if deng == eng and eng == "pe" and not dma:
                continue
            if info < 0:
                g = -info
                assert self.cnt[deng] >= g, "dependency on an unsignalled op whose group is not closed"
                info = g
            if info > need.get(deng, 0):
                need[deng] = info
        for deng, g in need.items():
            if self.waited[eng][deng] >= g:
                continue
            self.waited[eng][deng] = g
            waits.append(self._sem(deng, g))
        if dma and cc is None:
            k = self.dcount[eng]
            self.dcount[eng] += 1
            s = self.dsems[eng][k % N_DMA_SEM]
            v = 16 * (k // N_DMA_SEM + 1)
            if k >= N_DMA_SEM:
                key = id(s)
                if self.dwaited[eng].get(key, 0) < v - 16:
                    self.dwaited[eng][key] = v - 16
                    waits.append((s, v - 16))
        self.n_wait += len(waits)
        if cc is not None:
            for (ws, wv) in waits:
                E.wait_ge(ws, wv)
            fn().then_inc(cc[0], 1)
            self.meta.append((eng, True, cc))
            return i
        if fn is None:
            for (ws, wv) in waits:
                E.wait_ge(ws, wv)
            self.meta.append((eng, False, self.cnt[eng]))
            return i
        for (ws, wv) in waits[:-1]:
            E.wait_ge(ws, wv)
            self.n_standalone = getattr(self, 'n_standalone', 0) + 1
        inst = fn()
        if waits:
            inst._wait_ge(*waits[-1])
        if dma:
            inst.then_inc(s, 16)
            self.meta.append((eng, True, (s, v)))
        elif sig:
            self.cnt[eng] += 1
            g = self.cnt[eng]
            s, v = self._sem(eng, g)
            inst.then_inc(s, 1)
            self.meta.append((eng, False, g))
        else:
            self.meta.append((eng, False, -(self.cnt[eng] + 1)))
        return i


def build_program(T, L, taps=False):
    nc = bass.Bass("TRN2", target_bir_lowering=False)
    es = ExitStack()
    P = Prog(nc, es)
    NB = T // 512
    NKB = T // 128
    H = T // 2
    NBh = H // 512
    rank = nc.sync.snap(nc.sync.partition_id() % 2, min_val=0, max_val=1)
    r_qk = rank * 2048
    r_v = rank * (2 * H)
    r_h = rank * H
    VP = min(2048, H)
    npv = H // VP

    def din(name, shape, dt=F32):
        return nc.dram_tensor(name, list(shape), dt, kind="ExternalInput").ap()

    def dscr(name, shape, dt):
        kind = "ExternalOutput" if taps else "Internal"
        return nc.dram_tensor(name, list(shape), dt, kind=kind).ap()

    xT = din("xT", [D, H])
    pT = din("pT", [L, PLE, H])
    w_in = din("w_in", [L, D, 3072])
    w_o = din("w_o", [L, D, D])
    w_up = din("w_up", [L, D, 2 * DFF])
    w_down = din("w_down", [L, DFF, D])
    w_pg = din("w_pg", [L, D, D])
    w_pp = din("w_pp", [L, PLE, D])
    gvec_d = din("gvec", [128, L * 3 * 8])
    gfin_d = din("gfin", [128, 8])
    cw_d = din("convw", [128, L * 3 * 44])
    cb_d = din("convb", [128, L * 44])
    gsub_d = din("gsub", [128, L])
    lamv_d = din("lamv", [128, L * 4 * 64])
    btoe_d = din("btoe", [4, 128, 1024])
    negw_d = din("negw", [2, 128, 1024])
    cst_d = din("cst", [3, 128, 128])
    hmask_d = din("hmask", [128, 1])
    outT = nc.dram_tensor("outT", [D, H], F32, kind="ExternalOutput").ap()

    qk_loc = nc.dram_tensor("qk_loc", [NBh * 2048, 512], BF16)
    v_loc = nc.dram_tensor("v_loc", [NBh * 1024, 512], BF16)
    qk_scr, v_scr = qk_loc.ap(), v_loc.ap()
    qk_full = nc.dram_tensor("qk_full", [NBh * 4096, 512], BF16)
    v_full = nc.dram_tensor("v_full", [NBh * 2048, 512], BF16)
    halves = [(0, NBh // 2), (NBh // 2, NBh)] if NBh >= 2 else [(0, NBh)]
    qkfull_b, vfull_b = [Buf() for _ in halves], [Buf() for _ in halves]

    def half_of(blk):
        return 0 if (len(halves) == 1 or blk < NBh // 2) else 1
    qk_mine = nc.dram_tensor("qk_mine", [NBh * 2048, 512], BF16)
    v_mine = nc.dram_tensor("v_mine", [NBh * 1024, 512], BF16)
    mix_mine = nc.dram_tensor("mix_mine", [D, H], BF16)
    qkmine_b, vmine_b, mixmine_b = Buf(), Buf(), Buf()
    tail_loc = nc.dram_tensor("tail_loc", [D, 2], BF16)
    tail_full = nc.dram_tensor("tail_full", [2 * D, 2], BF16)
    tail_b, tailfull_b = Buf(), Buf()
    mix_loc = nc.dram_tensor("mix_loc", [512, T], BF16)
    mix_full = nc.dram_tensor("mix_full", [D, T], BF16)
    mix_scr = mix_loc.ap()
    mixfull_b = Buf()
    cc_sem = es.enter_context(nc.semaphore("cc_sem"))
    cc_n = [0]
    xn2_scr = dscr("xn2_scr", [D, H], BF16)
    hA = dscr("hA", [D, H], F32)
    hB = dscr("hB", [D, H], F32)
    hC = dscr("hC", [D, H], F32)

    def blkbufs(n=NBh):
        return [Buf() for _ in range(n)]
    qk_b, v_b, xn2_b, hA_b, hB_b, hC_b = (blkbufs() for _ in range(6))
    mix_bu = [blkbufs(NB) for _ in range(4)]

    uid = [0]

    def sb(name, shape, dt, stack=None):
        uid[0] += 1
        return (stack or es).enter_context(nc.sbuf_tensor(f"t{uid[0]}_{name}", list(shape), dt))

    pbig = [es.enter_context(nc.psum_tensor(f"psw{i}", [128, 1024], F32)) for i in range(4)]
    psum = [pbig[i // 2][:, (i % 2) * 512:(i % 2) * 512 + 512] for i in range(8)]
    psb = [Buf() for _ in range(8)]
    ident = sb("ident", [128, 128], BF16)
    ntri = sb("ntri", [128, 128], BF16)
    ones = sb("ones", [128, 128], BF16)
    gvec = sb("gvec_s", [128, L * 3 * 8], F32)
    gvec32 = sb("gvec32", [128, L * 3 * 8], F32)
    gfin = sb("gfin_s", [128, 8], F32)
    gfin32 = sb("gfin32", [128, 8], F32)
    cw = sb("cw_s", [128, L * 3 * 44], F32)
    cb = sb("cb_s", [128, L * 44], F32)
    gsub = sb("gsub_s", [128, L], F32)
    gsub2 = sb("gsub2", [128, L], F32)
    lamv = sb("lamv_s", [128, L * 4 * 64], F32)
    lamt = sb("lamt", [128, 64], F32)
    lams = sb("lams", [128, 2 * L], F32)
    neglam = sb("neglam", [128, L], F32)
    cstage = sb("cstage", [128, 3, 128], F32)
    hmask = sb("hmask_s", [128, 1], F32)
    b_const = Buf()

    def A(eng, fn, reads=(), writes=(), dma=False, sig=True, cc=None):
        return P.add(eng, fn, reads, writes, dma, sig, cc)

    A("sp", lambda: nc.sync.dma_start(out=cstage[:], in_=cst_d.rearrange("k p n -> p k n")),
      writes=[b_const], dma=True)
    for dst, src in ((gvec, gvec_d), (gfin, gfin_d), (cw, cw_d), (cb, cb_d), (gsub, gsub_d), (lamv, lamv_d), (hmask, hmask_d)):
        A("sp", lambda dst=dst, src=src: nc.sync.dma_start(out=dst[:], in_=src), writes=[b_const], dma=True)
    A("dve", lambda: nc.vector.tensor_copy(out=ident[:], in_=cstage[:, 0, :]), reads=[b_const], writes=[b_const])
    A("dve", lambda: nc.vector.tensor_copy(out=ntri[:], in_=cstage[:, 1, :]), reads=[b_const], writes=[b_const])
    A("dve", lambda: nc.vector.tensor_copy(out=ones[:], in_=cstage[:, 2, :]), reads=[b_const], writes=[b_const])
    A("dve", lambda: nc.vector.tensor_scalar(out=gvec32[:], in0=gvec[:], scalar1=32.0, scalar2=None, op0=ALU.mult),
      reads=[b_const], writes=[b_const])
    A("dve", lambda: nc.vector.tensor_scalar(out=gfin32[:], in0=gfin[:], scalar1=32.0, scalar2=None, op0=ALU.mult),
      reads=[b_const], writes=[b_const])
    for l in range(L):
        li = 0.8 - 0.6 * math.exp(-0.3 * l)
        for j in range(2):
            o = (l * 4 + 2 * j) * 64
            A("dve", lambda o=o: nc.vector.tensor_tensor(out=lamt[:], in0=lamv[:, o:o + 64], in1=lamv[:, o + 64:o + 128],
                                                         op=ALU.mult), reads=[b_const], writes=[b_const])
            A("dve", lambda l=l, j=j: nc.vector.reduce_sum(out=lams[:, 2 * l + j:2 * l + j + 1], in_=lamt[:], axis=AX.X),
              reads=[b_const], writes=[b_const])
        A("act", lambda l=l: nc.scalar.activation(out=lams[:, 2 * l:2 * l + 2], in_=lams[:, 2 * l:2 * l + 2], func=AF.Exp),
          reads=[b_const], writes=[b_const])
        A("dve", lambda l=l, li=li: nc.vector.scalar_tensor_tensor(
            out=neglam[:, l:l + 1], in0=lams[:, 2 * l + 1:2 * l + 2], scalar=-li, in1=lams[:, 2 * l:2 * l + 1],
            op0=ALU.add, op1=ALU.subtract), reads=[b_const], writes=[b_const])
        A("dve", lambda l=l, li=li: nc.vector.tensor_scalar(
            out=gsub2[:, l:l + 1], in0=gsub[:, l:l + 1], scalar1=(1.0 - li) * math.sqrt(128.0), scalar2=None, op0=ALU.mult),
            reads=[b_const], writes=[b_const])

    rr = {"ps": 0}

    def next_ps():
        i = rr["ps"] % 8
        rr["ps"] += 1
        return i

    def load_weight(dst, dstbuf, src2d, nchunk, ncols, stg, stgb, colstep):
        k = 0
        for c in range(nchunk):
            for c0 in range(0, ncols, colstep):
                w = min(colstep, ncols - c0)
                s, sbuf_ = stg[k % 2], stgb[k % 2]
                A("sp", lambda s=s, c=c, c0=c0, w=w: nc.sync.dma_start(
                    out=s[:, :w], in_=src2d[c * 128:(c + 1) * 128, c0:c0 + w]), writes=[sbuf_], dma=True)
                if k % 2 == 0:
                    A("pool", lambda s=s, c=c, c0=c0, w=w: nc.gpsimd.tensor_copy(out=dst[:, c, c0:c0 + w], in_=s[:, :w]),
                      reads=[sbuf_], writes=[dstbuf[0]])
                else:
                    A("act", lambda s=s, c=c, c0=c0, w=w: nc.scalar.activation(out=dst[:, c, c0:c0 + w], in_=s[:, :w], func=AF.Copy),
                      reads=[sbuf_], writes=[dstbuf[1]])
                k += 1

    def weight_pieces(dst, dstbuf, src2d, nchunk, ncols, stg, stgb, colstep):
        out = []
        k = 0
        for c in range(nchunk):
            for c0 in range(0, ncols, colstep):
                w = min(colstep, ncols - c0)
                s_, sb_ = stg[k % 2], stgb[k % 2]

                def piece(s_=s_, sb_=sb_, c=c, c0=c0, w=w, k=k):
                    A("sp", lambda: nc.sync.dma_start(out=s_[:, :w], in_=src2d[c * 128:(c + 1) * 128, c0:c0 + w]),
                      writes=[sb_], dma=True)
                    if k % 2 == 0:
                        A("pool", lambda: nc.gpsimd.tensor_copy(out=dst[:, c, c0:c0 + w], in_=s_[:, :w]),
                          reads=[sb_], writes=[dstbuf[0]])
                    else:
                        A("act", lambda: nc.scalar.activation(out=dst[:, c, c0:c0 + w], in_=s_[:, :w], func=AF.Copy),
                          reads=[sb_], writes=[dstbuf[1]])
                out.append(piece)
                k += 1
        return out

    def norm_block(hb, hbb, n, gcol, sq, sqb, lnv, lnvb, rstd, rstdb, xn, xnb, xc0=0, out_f32=None, outb=None):
        A("act", lambda: nc.scalar.activation(out=sq[:, :, :n], in_=hb[:, :, :n], func=AF.Square),
          reads=[hbb], writes=[sqb])
        pi = next_ps()
        for c in range(8):
            A("pe", lambda c=c, pi=pi: nc.tensor.matmul(psum[pi][:, :n], ones[:], sq[:, c, :n], start=(c == 0), stop=(c == 7)),
              reads=[sqb, b_const], writes=[psb[pi]], sig=(c == 7))
        A("act", lambda pi=pi: nc.scalar.activation(out=lnv[:, :n], in_=psum[pi][:, :n], func=AF.Ln, bias=1024.0 * EPS, scale=1.0),
          reads=[psb[pi]], writes=[lnvb])
        A("act", lambda: nc.scalar.activation(out=rstd[:, :n], in_=lnv[:, :n], func=AF.Exp, scale=-0.5),
          reads=[lnvb], writes=[rstdb])
        for c in range(8):
            if out_f32 is None:
                A("dve", lambda c=c: nc.vector.scalar_tensor_tensor(
                    out=xn[:, c, xc0:xc0 + n], in0=hb[:, c, :n], scalar=gcol(c), in1=rstd[:, :n], op0=ALU.mult, op1=ALU.mult),
                    reads=[hbb, rstdb, b_const], writes=[xnb])
            else:
                A("dve", lambda c=c: nc.vector.scalar_tensor_tensor(
                    out=out_f32[:, c, :n], in0=hb[:, c, :n], scalar=gcol(c), in1=rstd[:, :n], op0=ALU.mult, op1=ALU.mult),
                    reads=[hbb, rstdb, b_const], writes=[outb])

    def hview(ap2d, t0, n):
        return ap2d[:, t0:t0 + n].rearrange("(c p) t -> p c t", p=128)

    h_src, h_src_b = xT, [Buf() for _ in range(NBh)]

    for l in range(L):
        last = (l == L - 1)
        def pick_half(hi, b0, b1):
            A("sp", lambda: nc.sync.dma_start(
                out=qk_mine.ap().rearrange("(b j o w) t -> b j o (w t)", j=2, o=1, w=1024)[b0:b1],
                in_=qk_full.ap().rearrange("(b j g w) t -> b j g (w t)", j=2, g=2, w=1024)[b0:b1, :, bass.ds(rank, 1), :]),
              reads=[qkfull_b[hi]], writes=[qkmine_b], dma=True)
            A("sp", lambda: nc.sync.dma_start(
                out=v_mine.ap().rearrange("(b j o w) c -> b j o (w c)", j=2, o=1, w=512)[b0:b1],
                in_=v_full.ap().rearrange("(b j g w) c -> b j g (w c)", j=2, g=2, w=512)[b0:b1, :, bass.ds(rank, 1), :]),
              reads=[vfull_b[hi]], writes=[vmine_b], dma=True)

        P.barrier()
        with ExitStack() as ps_:
            w_in_sb = sb("w_in_sb", [128, 8, 3072], BF16, ps_)
            wb = [Buf(), Buf()]
            stg = [sb(f"stg1_{i}", [128, 1536], F32, ps_) for i in range(2)]
            stgb = [Buf(), Buf()]
            load_weight(w_in_sb, wb, w_in[l], 8, 3072, stg, stgb, 1536)
            hbs = [sb(f"p1_h{i}", [128, 8, 512], F32, ps_) for i in range(2)]
            hbb = [Buf(), Buf()]
            sq = sb("p1_sq", [128, 8, 512], BF16, ps_); sqb = Buf()
            lnv = sb("p1_lnv", [128, 512], F32, ps_); lnvb = Buf()
            rstd = sb("p1_rstd", [128, 512], F32, ps_); rstdb = Buf()
            xn = sb("p1_xn", [128, 8, 512], BF16, ps_); xnb = Buf()
            qst = [sb(f"p1_qst{i}", [128, 4, 512], BF16, ps_) for i in range(2)]
            qstb = [Buf(), Buf()]
            vst = [sb(f"p1_vst{i}", [128, 1024], BF16, ps_) for i in range(2)]
            vstb = [Buf(), Buf()]
            qcols = [128 * j for j in range(16)]
            vcols = [2048, 2560]

            def ld(blk):
                A("sp", lambda blk=blk: nc.sync.dma_start(out=hbs[blk % 2][:], in_=hview(h_src, blk * 512, 512)),
                  reads=[h_src_b[blk]], writes=[hbb[blk % 2]], dma=True)
            ld(0)
            ev = 0
            for blk in range(NBh):
                if blk + 1 < NBh:
                    ld(blk + 1)
                hb, hb_b = hbs[blk % 2], hbb[blk % 2]
                norm_block(hb, hb_b, 512, lambda c: gvec32[:, (l * 3 + 0) * 8 + c:(l * 3 + 0) * 8 + c + 1],
                           sq, sqb, lnv, lnvb, rstd, rstdb, xn, xnb)
                t0 = blk * 512
                for j4 in range(4):
                    st, stb = qst[j4 % 2], qstb[j4 % 2]
                    for jj in range(4):
                        j = j4 * 4 + jj
                        col = qcols[j]
                        pi = next_ps()
                        for c in range(8):
                            A("pe", lambda c=c, pi=pi, col=col: nc.tensor.matmul(
                                psum[pi][:], w_in_sb[:, c, col:col + 128], xn[:, c, :], start=(c == 0), stop=(c == 7)),
                                reads=[*wb, xnb], writes=[psb[pi]], sig=(c == 7))
                        scale = 0.125 if (j % 8) in (0, 1, 4, 5) else 1.0
                        if ev % 2 == 0:
                            A("act", lambda pi=pi, st=st, jj=jj, scale=scale: nc.scalar.activation(
                                out=st[:, jj, :], in_=psum[pi][:], func=AF.Copy, scale=scale), reads=[psb[pi]], writes=[stb])
                        else:
                            A("dve", lambda pi=pi, st=st, jj=jj, scale=scale: nc.vector.tensor_scalar(
                                out=st[:, jj, :], in0=psum[pi][:], scalar1=scale, scalar2=None, op0=ALU.mult),
                                reads=[psb[pi]], writes=[stb])
                        ev += 1
                    A("sp", lambda st=st, j4=j4, t0=t0: nc.sync.dma_start(
                        out=qk_scr[blk * 2048 + j4 * 512:blk * 2048 + (j4 + 1) * 512, :].rearrange("(c p) t -> p c t", p=128), in_=st[:]),
                        reads=[stb], writes=[qk_b[blk]], dma=True)
                for s in range(4):
                    st, stb = vst[s % 2], vstb[s % 2]
                    for half in range(2):
                        pi = next_ps()
                        vc = vcols[half]
                        for c in range(8):
                            A("pe", lambda c=c, pi=pi, s=s, vc=vc: nc.tensor.matmul(
                                psum[pi][:], xn[:, c, s * 128:(s + 1) * 128], w_in_sb[:, c, vc:vc + 512],
                                start=(c == 0), stop=(c == 7)), reads=[*wb, xnb], writes=[psb[pi]], sig=(c == 7))
                        if ev % 2 == 0:
                            A("act", lambda pi=pi, st=st, half=half: nc.scalar.activation(
                                out=st[:, half * 512:(half + 1) * 512], in_=psum[pi][:], func=AF.Copy), reads=[psb[pi]], writes=[stb])
                        else:
                            A("dve", lambda pi=pi, st=st, half=half: nc.vector.tensor_copy(
                                out=st[:, half * 512:(half + 1) * 512], in_=psum[pi][:]), reads=[psb[pi]], writes=[stb])
                        ev += 1
                    for g in range(2):
                        r0 = blk * 1024 + g * 512 + s * 128
                        A("sp", lambda st=st, r0=r0, g=g: nc.sync.dma_start(
                            out=v_scr[r0:r0 + 128, :], in_=st[:, g * 512:(g + 1) * 512]),
                            reads=[stb], writes=[v_b[blk]], dma=True)
                cc_n[0] += 1
                A("pool", lambda blk=blk: nc.gpsimd.collective_compute(
                    "AllGather", ALU.bypass, replica_groups=[[0, 1], [2, 3], [4, 5], [6, 7]],
                    ins=[qk_loc.ap()[blk * 2048:(blk + 1) * 2048, :].opt()], outs=[qk_full.ap()[blk * 4096:(blk + 1) * 4096, :].opt()]),
                  reads=[qk_b[blk]], writes=[qkfull_b[half_of(blk)]], dma=True, cc=(cc_sem, cc_n[0]))
                cc_n[0] += 1
                A("pool", lambda blk=blk: nc.gpsimd.collective_compute(
                    "AllGather", ALU.bypass, replica_groups=[[0, 1], [2, 3], [4, 5], [6, 7]],
                    ins=[v_loc.ap()[blk * 1024:(blk + 1) * 1024, :].opt()], outs=[v_full.ap()[blk * 2048:(blk + 1) * 2048, :].opt()]),
                  reads=[v_b[blk]], writes=[vfull_b[half_of(blk)]], dma=True, cc=(cc_sem, cc_n[0]))
                for hi, (b0, b1) in enumerate(halves):
                    if blk == min(b1 + 1, NBh - 1):
                        pick_half(hi, b0, b1)

        P.barrier()
        with ExitStack() as ps_:
            btoe = sb("btoe", [128, 4, 1024], F32, ps_); btb = Buf()
            negw = sb("negw", [128, 2, 1024], BF16, ps_)
            negst = sb("negst", [128, 2, 1024], F32, ps_)
            A("sp", lambda: nc.sync.dma_start(out=btoe[:], in_=btoe_d.rearrange("m p x -> p m x")), writes=[btb], dma=True)
            A("sp", lambda: nc.sync.dma_start(out=negst[:], in_=negw_d.rearrange("m p x -> p m x")), writes=[btb], dma=True)
            A("dve", lambda: nc.vector.tensor_copy(out=negw[:], in_=negst[:]), reads=[btb], writes=[btb])
            c31 = sb("c31", [128, 4], F32, ps_)
            A("dve", lambda: nc.vector.tensor_copy(out=c31[:], in_=btoe[:, :, 1023]), reads=[btb], writes=[btb])
            for m_ in range(4):
                A("dve", lambda m_=m_: nc.vector.tensor_scalar(out=btoe[:, m_, :], in0=btoe[:, m_, :], scalar1=c31[:, m_:m_ + 1],
                                                               scalar2=None, op0=ALU.subtract), reads=[btb], writes=[btb])
            qt2 = [sb(f"qt2_{i}", [128, T], BF16, ps_) for i in range(2)]
            kt2 = [sb(f"kt2_{i}", [128, T], BF16, ps_) for i in range(2)]
            v2 = [sb(f"v2_{i}", [128, NKB, 128], BF16, ps_) for i in range(2)]
            pairb = [Buf(), Buf()]
            e_t = [sb(f"e_t{i}", [128, 512], F32, ps_) for i in range(2)]; e_b = [Buf(), Buf()]
            sp_t = [sb(f"sp_t{i}", [128, 512], BF16, ps_) for i in range(2)]; sp_b = [Buf(), Buf()]
            la_t = [sb(f"la_t{i}", [128, 512], F32, ps_) for i in range(2)]; la_b = [Buf(), Buf()]
            a_t = [sb(f"a_t{i}", [128, 512], BF16, ps_) for i in range(2)]; a_b = [Buf(), Buf()]
            tc_t = [sb(f"tc_t{i}", [128, 512], F32, ps_) for i in range(2)]; tc_b = [Buf(), Buf()]
            ost = [sb(f"ost{i}", [128, 512], BF16, ps_) for i in range(2)]; ost_b = [Buf(), Buf()]
            pdw = [sb(f"pdw{i}", [128, 1024], BF16, ps_) for i in range(2)]
            pd_t = [[pdw[i][:, 512 * c:512 * c + 512] for i in range(2)] for c in range(2)]
            pdw_b = [Buf(), Buf()]
            pd_b = [[pdw_b[0], pdw_b[1]], [pdw_b[0], pdw_b[1]]]
            rsum = [sb(f"rsum{c}", [128, 512], F32, ps_) for c in range(2)]; rsum_b = [Buf(), Buf()]
            rsbf = [sb(f"rsbf{c}", [128, 512], BF16, ps_) for c in range(2)]; rsbf_b = [Buf(), Buf()]
            rw_t = sb("rw_t", [128, 1024], F32, ps_)
            r_t = [rw_t[:, 0:512], rw_t[:, 512:1024]]; r_b = [Buf(), Buf()]
            o_t = sb("o_t", [128, 512], F32, ps_); o_b = Buf()
            sq2 = sb("sq2", [128, 512], BF16, ps_); sq2b = Buf()
            ln2 = sb("ln2", [128, 512], F32, ps_); ln2b = Buf()
            rs2 = sb("rs2", [128, 512], F32, ps_); rs2b = Buf()
            mixo = sb("mixo", [128, 512], BF16, ps_); mixob = Buf()
            ew_t = sb("ew_t", [128, 1024], F32, ps_); ew_b = Buf()
            spw_ts = [sb(f"spw_t{i}", [128, 1024], BF16, ps_) for i in range(2)]; spw_bs = [Buf(), Buf()]
            law_ts = [sb(f"law_t{i}", [128, 1024], F32, ps_) for i in range(2)]; law_bs = [Buf(), Buf()]
            aw_ts = [sb(f"aw_t{i}", [128, 1024], BF16, ps_) for i in range(2)]; aw_bs = [Buf(), Buf()]
            tcw_t = sb("tcw_t", [128, 1024], F32, ps_); tcw_b = Buf()
            ostw = sb("ostw", [128, 1024], BF16, ps_); ostw_b = Buf()
            zbig = [pbig[0], pbig[1]]
            sbig, obig = pbig[2], pbig[3]
            ZB, LB, SB_, OB = (0, 1), (2, 3), (4, 5), (6, 7)

            def load_pair(u, slot):
                if u < 2:
                    qc_, kc_, vcol = u, 2 + u, 128 * u
                else:
                    qc_, kc_, vcol = 4 + (u - 2), 6 + (u - 2), 256 + 128 * (u - 2)
                qk4 = qk_mine.ap().rearrange("(b j w) t -> b j w t", j=2, w=1024)
                for (dst, c_) in ((qt2[slot], qc_), (kt2[slot], kc_)):
                    for j in range(2):
                        A("sp", lambda dst=dst, c_=c_, j=j: nc.sync.dma_start(
                            out=dst[:, j * H:(j + 1) * H].rearrange("p (b t) -> p b t", t=512),
                            in_=qk4[:, j, c_ * 128:(c_ + 1) * 128, :].rearrange("b p t -> p b t")),
                            reads=[qkmine_b], writes=[pairb[slot]], dma=True)
                for j in range(2):
                    for b_ in range(NBh):
                        kb0 = (j * H + b_ * 512) // 128
                        r0 = b_ * 1024 + j * 512
                        A("sp", lambda kb0=kb0, r0=r0: nc.sync.dma_start(
                            out=v2[slot][:, kb0:kb0 + 4, :],
                            in_=v_mine.ap()[r0:r0 + 512, vcol:vcol + 128].rearrange("(kb p) d -> p kb d", p=128)),
                            reads=[vmine_b], writes=[pairb[slot]], dma=True)

            def mix_exchange(u):
                for k in (2 * u, 2 * u + 1):
                    cc_n[0] += 1
                    A("pool", lambda k=k: nc.gpsimd.collective_compute(
                        "AllGather", ALU.bypass, replica_groups=[[0, 1], [2, 3], [4, 5], [6, 7]],
                        ins=[mix_loc.ap()[64 * k:64 * k + 64, :].opt()], outs=[mix_full.ap()[128 * k:128 * k + 128, :].opt()]),
                      reads=mix_bu[u], writes=[mixfull_b], dma=True, cc=(cc_sem, cc_n[0]))

            pending = []
            load_pair(0, 0)
            for u in range(4):
                if u >= 1:
                    mix_exchange(u - 1)
                slot = u % 2
                if u + 1 < 4:
                    load_pair(u + 1, (u + 1) % 2)
                QT, KT, V2, pb = qt2[slot], kt2[slot], v2[slot], pairb[slot]
                is_sb = u < 2
                for qc in range(NB):
                    q0 = qc * 512
                    nkb = 4 * (qc + 1)
                    order = list(range(nkb - 1, -1, -1))

                    def qk(i, bank, ch, last_stop):
                        kb = order[i]
                        k0 = kb * 128
                        diag = k0 >= q0
                        lo = 64 * ch
                        A("pe", lambda: nc.tensor.matmul(psum[bank][:], KT[lo:lo + 64, k0:k0 + 128], QT[lo:lo + 64, q0:q0 + 512],
                                                         start=True, stop=(last_stop and not (diag and is_sb))),
                          reads=[pb], writes=[psb[bank]])
                        if diag and is_sb:
                            c0 = k0 - q0
                            A("pe", lambda: nc.tensor.matmul(psum[bank][:], ident[:], negw[:, 0, 512 - c0:1024 - c0],
                                                             start=False, stop=last_stop),
                              reads=[btb, b_const], writes=[psb[bank]])

                    if is_sb:
                        def zqk(i):
                            kb = order[i]
                            k0 = kb * 128
                            diag = k0 >= q0
                            zt, zk = zbig[i % 2], 2 * (i % 2)
                            for ch in range(2):
                                lo = 64 * ch
                                A("pe", lambda lo=lo, ch=ch: nc.tensor.matmul(
                                    zt[:, 512 * ch:512 * ch + 512], KT[lo:lo + 64, k0:k0 + 128], QT[lo:lo + 64, q0:q0 + 512],
                                    start=True, stop=not diag), reads=[pb], writes=[psb[zk + ch]], sig=(ch == 1 and not diag))
                            if diag:
                                c0 = k0 - q0
                                for ch in range(2):
                                    A("pe", lambda ch=ch: nc.tensor.matmul(
                                        zt[:, 512 * ch:512 * ch + 512], ident[:], negw[:, 0, 512 - c0:1024 - c0],
                                        start=False, stop=True), reads=[btb, b_const], writes=[psb[zk + ch]], sig=(ch == 1))

                        def act_A(i):
                            A("act", lambda: nc.scalar.activation(out=aw_ts[i % 2][:], in_=law_ts[i % 2][:], func=AF.Exp),
                              reads=[law_bs[i % 2]], writes=[aw_bs[i % 2]])

                        def pe_PV(i):
                            kbp = order[i]
                            for ch in range(2):
                                A("pe", lambda ch=ch: nc.tensor.matmul(
                                    obig[0:64, 512 * ch:512 * ch + 512], V2[:, kbp, 64 * ch:64 * ch + 64], aw_ts[i % 2][:, 512 * ch:512 * ch + 512],
                                    start=(i == 0), stop=(i == nkb - 1)), reads=[pb, aw_bs[i % 2]], writes=[psb[6 + ch]], sig=(ch == 1))

                        zqk(0)
                        for i in range(nkb):
                            zt, zk = zbig[i % 2], 2 * (i % 2)
                            zpair = [psb[zk], psb[zk + 1]]
                            A("act", lambda zt=zt: nc.scalar.activation(out=ew_t[:], in_=zt[:], func=AF.Exp),
                              reads=zpair, writes=[ew_b])
                            if i + 1 < nkb:
                                zqk(i + 1)
                            spw_t, spw_b = spw_ts[i % 2], spw_bs[i % 2]
                            law_t, law_b = law_ts[i % 2], law_bs[i % 2]
                            A("act", lambda spw_t=spw_t: nc.scalar.activation(out=spw_t[:], in_=ew_t[:], func=AF.Ln, bias=1.0, scale=1.0),
                              reads=[ew_b], writes=[spw_b])
                            for ch in range(2):
                                A("pe", lambda ch=ch, zt=zt, spw_t=spw_t: nc.tensor.matmul(
                                    zt[:, 512 * ch:512 * ch + 512], ntri[:], spw_t[:, 512 * ch:512 * ch + 512],
                                    start=False, stop=True, skip_group_check=True),
                                  reads=[spw_b, b_const], writes=[psb[zk + ch]], sig=(ch == 1))
                            if i + 1 < nkb:
                                for ch in range(2):
                                    A("pe", lambda ch=ch, spw_t=spw_t: nc.tensor.matmul(
                                        sbig[:, 512 * ch:512 * ch + 512], ones[:], spw_t[:, 512 * ch:512 * ch + 512],
                                        start=True, stop=True), reads=[spw_b, b_const], writes=[psb[4 + ch]], sig=(ch == 1))
                            if i >= 1:
                                act_A(i - 1)
                                pe_PV(i - 1)
                            if i == 0:
                                A("dve", lambda zt=zt, law_t=law_t: nc.vector.tensor_copy(out=law_t[:], in_=zt[:]),
                                  reads=zpair, writes=[law_b])
                            else:
                                A("dve", lambda zt=zt, law_t=law_t: nc.vector.tensor_tensor(out=law_t[:], in0=zt[:], in1=tcw_t[:], op=ALU.subtract),
                                  reads=zpair + [tcw_b], writes=[law_b])
                            if i + 1 < nkb:
                                if i == 0:
                                    A("dve", lambda: nc.vector.tensor_copy(out=tcw_t[:], in_=sbig[:]),
                                      reads=[psb[4], psb[5]], writes=[tcw_b])
                                else:
                                    A("dve", lambda: nc.vector.tensor_tensor(out=tcw_t[:], in0=sbig[:], in1=tcw_t[:], op=ALU.add),
                                      reads=[psb[4], psb[5], tcw_b], writes=[tcw_b])
                        act_A(nkb - 1)
                        pe_PV(nkb - 1)
                        A("dve", lambda: nc.vector.tensor_copy(out=ostw[0:64, :], in_=obig[0:64, :]),
                          reads=[psb[6], psb[7]], writes=[ostw_b])
                        A("sp", lambda: nc.sync.dma_start(
                            out=mix_scr[128 * u:128 * u + 128, q0:q0 + 512].rearrange("(c p) t -> p c t", p=64),
                            in_=ostw[0:64, :].rearrange("p (c t) -> p c t", c=2)),
                          reads=[ostw_b], writes=[mix_bu[u][qc]], dma=True)
                    else:
                        hd = u - 2
                        zb = [(0, 1), (2, 3)]
                        for ch in range(2):
                            qk(0, zb[0][ch], ch, True)
                        for i in range(nkb):
                            if i == 2 and pending:
                                pending.pop()()
                            kb = order[i]
                            k0 = kb * 128
                            near = (k0 >= q0 - 128)
                            zz = zb[i % 2]
                            if i + 1 < nkb:
                                for ch in range(2):
                                    qk(i + 1, zb[(i + 1) % 2][ch], ch, True)
                            if near:
                                for ch in range(2):
                                    m = 2 * hd + ch
                                    pt, ptb = pd_t[ch][i % 2], pd_b[ch][i % 2]
                                    x0 = q0 - k0 + 384
                                    A("dve", lambda ch=ch, m=m, x0=x0, zz=zz: nc.vector.tensor_tensor(
                                        out=la_t[ch][:], in0=psum[zz[ch]][:], in1=btoe[:, m, x0:x0 + 512], op=ALU.add),
                                        reads=[psb[zz[ch]], btb], writes=[la_b[ch]])
                                    A("act", lambda ch=ch, pt=pt: nc.scalar.activation(out=pt[:], in_=la_t[ch][:], func=AF.Exp),
                                      reads=[la_b[ch]], writes=[ptb])
                            else:
                                A("act", lambda i=i: nc.scalar.activation(out=pdw[i % 2][:], in_=pbig[i % 2][:], func=AF.Exp),
                                  reads=[psb[zz[0]], psb[zz[1]]], writes=[pdw_b[i % 2]])
                            for ch in range(2):
                                pt, ptb = pd_t[ch][i % 2], pd_b[ch][i % 2]
                                A("pe", lambda ch=ch, kb=kb, i=i, pt=pt: nc.tensor.matmul(psum[OB[ch]][:], V2[:, kb, :], pt[:],
                                                                                        start=(i == 0), stop=(i == nkb - 1)),
                                  reads=[pb, ptb], writes=[psb[OB[ch]]])
                                if ch == 0:
                                    A("pe", lambda i=i, pt=pt: nc.tensor.matmul(psum[SB_[0]][:], ones[:], pt[:],
                                                                                start=(i == 0), stop=(i == nkb - 1)),
                                      reads=[ptb, b_const], writes=[psb[SB_[0]]])
                                else:
                                    if i == 0:
                                        A("dve", lambda pt=pt: nc.vector.tensor_copy(out=rsum[1][:], in_=pt[:]),
                                          reads=[ptb], writes=[rsum_b[1]])
                                    else:
                                        A("dve", lambda pt=pt: nc.vector.tensor_tensor(out=rsum[1][:], in0=pt[:], in1=rsum[1][:], op=ALU.add),
                                          reads=[ptb, rsum_b[1]], writes=[rsum_b[1]])
                        for ch in range(1, 2):
                            A("dve", lambda ch=ch: nc.vector.tensor_copy(out=rsbf[ch][:], in_=rsum[ch][:]),
                              reads=[rsum_b[ch]], writes=[rsbf_b[ch]])
                            A("pe", lambda ch=ch: nc.tensor.matmul(psum[SB_[ch]][:], ones[:], rsbf[ch][:], start=True, stop=True),
                              reads=[rsbf_b[ch], b_const], writes=[psb[SB_[ch]]])
                        A("act", lambda: nc.scalar.activation(out=rw_t[:], in_=pbig[2][:], func=AF.Ln),
                          reads=[psb[4], psb[5]], writes=[r_b[0], r_b[1]])
                        A("act", lambda: nc.scalar.activation(out=rw_t[:], in_=rw_t[:], func=AF.Exp, scale=-1.0),
                          reads=[r_b[0], r_b[1]], writes=[r_b[0], r_b[1]])
                        for ch in range(2):
                            A("dve", lambda ch=ch: nc.vector.tensor_tensor(out=r_t[ch][:], in0=psum[OB[ch]][:], in1=r_t[ch][:], op=ALU.mult),
                              reads=[psb[OB[ch]], r_b[ch]], writes=[r_b[ch]])
                        A("dve", lambda: nc.vector.scalar_tensor_tensor(out=o_t[:], in0=r_t[1][:], scalar=neglam[:, l:l + 1], in1=r_t[0][:],
                                                                        op0=ALU.mult, op1=ALU.add),
                          reads=[r_b[0], r_b[1], b_const], writes=[o_b])
                        def tail(q0=q0, qc=qc, u=u, hd=hd):
                            A("act", lambda: nc.scalar.activation(out=sq2[:], in_=o_t[:], func=AF.Square), reads=[o_b], writes=[sq2b])
                            A("pe", lambda: nc.tensor.matmul(psum[5][:], ones[:], sq2[:], start=True, stop=True),
                              reads=[sq2b, b_const], writes=[psb[5]])
                            A("act", lambda: nc.scalar.activation(out=ln2[:], in_=psum[5][:], func=AF.Ln, bias=128.0 * EPS, scale=1.0),
                              reads=[psb[5]], writes=[ln2b])
                            A("act", lambda: nc.scalar.activation(out=rs2[:], in_=ln2[:], func=AF.Exp, scale=-0.5), reads=[ln2b], writes=[rs2b])
                            A("dve", lambda: nc.vector.scalar_tensor_tensor(out=mixo[:], in0=o_t[:], scalar=gsub2[:, l:l + 1], in1=rs2[:],
                                                                            op0=ALU.mult, op1=ALU.mult),
                              reads=[o_b, rs2b, b_const], writes=[mixob])
                            row = 256 + 128 * hd
                            A("sp", lambda: nc.sync.dma_start(out=mix_scr[row:row + 128, q0:q0 + 512], in_=mixo[:]),
                              reads=[mixob], writes=[mix_bu[u][qc]], dma=True)
                        pending.append(tail)
                if pending:
                    pending.pop()()

        for k in (6, 7):
            cc_n[0] += 1
            A("pool", lambda k=k: nc.gpsimd.collective_compute(
                "AllGather", ALU.bypass, replica_groups=[[0, 1], [2, 3], [4, 5], [6, 7]],
                ins=[mix_loc.ap()[64 * k:64 * k + 64, :].opt()], outs=[mix_full.ap()[128 * k:128 * k + 128, :].opt()]),
              reads=mix_bu[3], writes=[mixfull_b], dma=True, cc=(cc_sem, cc_n[0]))
        P.barrier()
        ffn_scope = ExitStack()
        w_up_sb = sb("w_up_sb", [128, 8, 2 * DFF], BF16, ffn_scope); wub = [Buf(), Buf()]
        stg_f = [sb(f"stg3b_{i}", [128, 1408], F32, ffn_scope) for i in range(2)]; stgb_f = [Buf(), Buf()]
        up_pieces = weight_pieces(w_up_sb, wub, w_up[l], 8, 2 * DFF, stg_f, stgb_f, 1408)
        with ExitStack() as ps_:
            w_o_sb = sb("w_o_sb", [128, 8, 1024], BF16, ps_); wb = [Buf(), Buf()]
            stg = [sb(f"stg3a_{i}", [128, 1024], F32, ps_) for i in range(2)]; stgb = [Buf(), Buf()]
            load_weight(w_o_sb, wb, w_o[l], 8, 1024, stg, stgb, 1024)
            A("sp", lambda: nc.sync.dma_start(out=mix_mine.ap(), in_=mix_full.ap()[:, bass.ds(r_h, H)]),
              reads=[mixfull_b], writes=[mixmine_b], dma=True)
            hbs = [sb(f"p3a_h{i}", [128, 8, 512], F32, ps_) for i in range(2)]; hbb = [Buf(), Buf()]
            mxs = [sb(f"p3a_m{i}", [128, 8, 512], BF16, ps_) for i in range(2)]; mxb = [Buf(), Buf()]
            sq = sb("p3a_sq", [128, 8, 512], BF16, ps_); sqb = Buf()
            lnv = sb("p3a_lnv", [128, 512], F32, ps_); lnvb = Buf()
            rstd = sb("p3a_rstd", [128, 512], F32, ps_); rstdb = Buf()
            xns = [sb(f"p3a_xn{i}", [128, 8, 512], BF16, ps_) for i in range(1)] * 2; xnb = [Buf()] * 2

            def ld(blk):
                A("sp", lambda blk=blk: nc.sync.dma_start(out=hbs[blk % 2][:], in_=hview(h_src, blk * 512, 512)),
                  reads=[h_src_b[blk]], writes=[hbb[blk % 2]], dma=True)
                A("sp", lambda blk=blk: nc.sync.dma_start(out=mxs[blk % 2][:], in_=hview(mix_mine.ap(), blk * 512, 512)),
                  reads=[mixmine_b], writes=[mxb[blk % 2]], dma=True)
            ld(0)
            for blk in range(NBh):
                if blk + 1 < NBh:
                    ld(blk + 1)
                hb, hb_b, mx, mx_b = hbs[blk % 2], hbb[blk % 2], mxs[blk % 2], mxb[blk % 2]
                for oc in range(8):
                    pi = next_ps()
                    for c in range(8):
                        A("pe", lambda c=c, oc=oc, pi=pi: nc.tensor.matmul(psum[pi][:], w_o_sb[:, c, oc * 128:(oc + 1) * 128], mx[:, c, :],
                                                                         start=(c == 0), stop=(c == 7)),
                          reads=[*wb, mx_b], writes=[psb[pi]], sig=(c == 7))
                    A("dve", lambda oc=oc, pi=pi: nc.vector.tensor_tensor(out=hb[:, oc, :], in0=psum[pi][:], in1=hb[:, oc, :], op=ALU.add),
                      reads=[psb[pi], hb_b], writes=[hb_b])
                A("sp", lambda blk=blk, hb=hb: nc.sync.dma_start(out=hview(hA, blk * 512, 512), in_=hb[:]),
                  reads=[hb_b], writes=[hA_b[blk]], dma=True)
                xn, xn_b = xns[blk % 2], xnb[blk % 2]
                norm_block(hb, hb_b, 512, lambda c: gvec32[:, (l * 3 + 1) * 8 + c:(l * 3 + 1) * 8 + c + 1],
                           sq, sqb, lnv, lnvb, rstd, rstdb, xn, xn_b)
                A("sp", lambda blk=blk, xn=xn: nc.sync.dma_start(out=hview(xn2_scr, blk * 512, 512), in_=xn[:]),
                  reads=[xn_b], writes=[xn2_b[blk]], dma=True)
                per = (len(up_pieces) + NBh - 1) // NBh
                for pc in up_pieces[blk * per:(blk + 1) * per]:
                    pc()
                if blk == NBh - 1:
                    A("sp", lambda xn=xn: nc.sync.dma_start(out=tail_loc.ap().rearrange("(c p) t -> p c t", p=128), in_=xn[:, :, 510:512]),
                      reads=[xn_b], writes=[tail_b], dma=True)
        cc_n[0] += 1
        A("pool", lambda: nc.gpsimd.collective_compute(
            "AllGather", ALU.bypass, replica_groups=[[0, 1], [2, 3], [4, 5], [6, 7]],
            ins=[tail_loc.ap().opt()], outs=[tail_full.ap().opt()]),
          reads=[tail_b], writes=[tailfull_b], dma=True, cc=(cc_sem, cc_n[0]))

        P.barrier()
        with ExitStack() as ps_:
            w_dn_sb = sb("w_dn_sb", [128, NFC, 1024], BF16, ps_); wdb = [Buf(), Buf()]
            load_weight(w_dn_sb, wdb, w_down[l], NFC, 1024, stg_f, stgb_f, 1024)
            NT = 256
            xbs = [sb(f"p3b_x{i}", [128, 8, NT + 2], BF16, ps_) for i in range(2)]; xbb = [Buf(), Buf()]
            hbs = [sb(f"p3b_h{i}", [128, 8, NT], F32, ps_) for i in range(2)]; hbb = [Buf(), Buf()]
            act_t = sb("p3b_act", [128, NFC, NT], BF16, ps_); actb = [Buf() for _ in range(NFC)]
            yg = [sb(f"p3b_yg{i}", [128, NT], F32, ps_) for i in range(2)]; ygb = [Buf(), Buf()]
            yv = [sb(f"p3b_yv{i}", [128, NT], F32, ps_) for i in range(2)]; yvb = [Buf(), Buf()]
            gg = [sb(f"p3b_gg{i}", [128, NT], F32, ps_) for i in range(2)]; ggb = [Buf(), Buf()]
            nblk = H // NT
            tl = sb("p3b_tl", [128, 8, 2], BF16, ps_); tlb = Buf()

            def ld(b):
                t0 = b * NT
                x_, xb_ = xbs[b % 2], xbb[b % 2]
                rb = [xn2_b[t0 // 512]] + ([xn2_b[(t0 - 2) // 512]] if t0 > 0 else [])
                if b == 0:
                    A("sp", lambda: nc.sync.dma_start(out=tl[:], in_=tail_full.ap()[0:D, :].rearrange("(c p) t -> p c t", p=128)),
                      reads=[tailfull_b], writes=[tlb], dma=True)
                    A("dve", lambda x_=x_: nc.vector.tensor_scalar(out=x_[:, :, 0:2], in0=tl[:], scalar1=hmask[:, 0:1], scalar2=None,
                                                                   op0=ALU.mult), reads=[tlb, b_const], writes=[xb_])
                    A("sp", lambda x_=x_: nc.sync.dma_start(out=x_[:, :, 2:NT + 2], in_=hview(xn2_scr, 0, NT)),
                      reads=rb, writes=[xb_], dma=True)
                else:
                    A("sp", lambda x_=x_, t0=t0: nc.sync.dma_start(out=x_[:], in_=hview(xn2_scr, t0 - 2, NT + 2)),
                      reads=rb, writes=[xb_], dma=True)
                A("sp", lambda b=b, t0=t0: nc.sync.dma_start(out=hbs[b % 2][:], in_=hview(hA, t0, NT)),
                  reads=[hA_b[t0 // 512]], writes=[hbb[b % 2]], dma=True)
            ld(0)
            cwo = lambda k, j: (l * 3 + k) * 44 + j
            for b in range(nblk):
                if b + 1 < nblk:
                    ld(b + 1)
                x_, xb_, hb, hb_b = xbs[b % 2], xbb[b % 2], hbs[b % 2], hbb[b % 2]
                for i in range(NFC):
                    pg, pv = next_ps(), next_ps()
                    for (pi, cbase) in ((pg, 128 * i), (pv, DFF + 128 * i)):
                        for c in range(8):
                            A("pe", lambda c=c, pi=pi, cbase=cbase: nc.tensor.matmul(
                                psum[pi][:, :NT + 2], w_up_sb[:, c, cbase:cbase + 128], x_[:, c, :], start=(c == 0), stop=(c == 7)),
                                reads=[*wub, xb_], writes=[psb[pi]], sig=(c == 7))
                    s = i % 2
                    for (pi, y, yb, j) in ((pg, yg[s], ygb[s], i), (pv, yv[s], yvb[s], NFC + i)):
                        A("act", lambda pi=pi, y=y, j=j: nc.scalar.activation(
                            out=y[:], in_=psum[pi][:, 2:NT + 2], func=AF.Identity,
                            bias=cb[:, l * 44 + j:l * 44 + j + 1], scale=cw[:, cwo(2, j):cwo(2, j) + 1]),
                            reads=[psb[pi], b_const], writes=[yb])
                        A("dve", lambda pi=pi, y=y, j=j: nc.vector.scalar_tensor_tensor(
                            out=y[:], in0=psum[pi][:, 1:NT + 1], scalar=cw[:, cwo(1, j):cwo(1, j) + 1], in1=y[:],
                            op0=ALU.mult, op1=ALU.add), reads=[psb[pi], yb, b_const], writes=[yb])
                        A("dve", lambda pi=pi, y=y, j=j: nc.vector.scalar_tensor_tensor(
                            out=y[:], in0=psum[pi][:, 0:NT], scalar=cw[:, cwo(0, j):cwo(0, j) + 1], in1=y[:],
                            op0=ALU.mult, op1=ALU.add), reads=[psb[pi], yb, b_const], writes=[yb])
                    A("act", lambda s=s: nc.scalar.activation(out=gg[s][:], in_=yg[s][:], func=AF.Gelu_apprx_tanh),
                      reads=[ygb[s]], writes=[ggb[s]])
                    A("pool", lambda s=s, i=i: nc.gpsimd.tensor_tensor(out=act_t[:, i, :], in0=gg[s][:], in1=yv[s][:], op=ALU.mult),
                      reads=[ggb[s], yvb[s]], writes=[actb[i]])
                for oc in range(8):
                    pi = next_ps()
                    for i in range(NFC):
                        A("pe", lambda i=i, oc=oc, pi=pi: nc.tensor.matmul(psum[pi][:, :NT], w_dn_sb[:, i, oc * 128:(oc + 1) * 128], act_t[:, i, :],
                                                                         start=(i == 0), stop=(i == NFC - 1)),
                          reads=[*wdb, actb[i]], writes=[psb[pi]], sig=(i == NFC - 1))
                    A("dve", lambda oc=oc, pi=pi: nc.vector.tensor_tensor(out=hb[:, oc, :], in0=psum[pi][:, :NT], in1=hb[:, oc, :], op=ALU.add),
                      reads=[psb[pi], hb_b], writes=[hb_b])
                A("sp", lambda b=b, hb=hb: nc.sync.dma_start(out=hview(hB, b * NT, NT), in_=hb[:]),
                  reads=[hb_b], writes=[hB_b[(b * NT) // 512]], dma=True)

        P.barrier()
        ffn_scope.close()
        with ExitStack() as ps_:
            w_pg_sb = sb("w_pg_sb", [128, 8, 1024], BF16, ps_); wgb = [Buf(), Buf()]
            w_pp_sb = sb("w_pp_sb", [128, 2, 1024], BF16, ps_); wpb = [Buf(), Buf()]
            stg = [sb(f"stg3c_{i}", [128, 1024], F32, ps_) for i in range(2)]; stgb = [Buf(), Buf()]
            load_weight(w_pg_sb, wgb, w_pg[l], 8, 1024, stg, stgb, 1024)
            load_weight(w_pp_sb, wpb, w_pp[l], 2, 1024, stg, stgb, 1024)
            hbs = [sb(f"p3c_h{i}", [128, 8, 512], F32, ps_) for i in range(2)]; hbb = [Buf(), Buf()]
            pfs = [sb(f"p3c_pf{i}", [128, 2, 512], F32, ps_) for i in range(2)]; pfb = [Buf(), Buf()]
            pbf = sb("p3c_pb", [128, 2, 512], BF16, ps_); pbb = Buf()
            sq = sb("p3c_sq", [128, 8, 512], BF16, ps_); sqb = Buf()
            lnv = sb("p3c_lnv", [128, 512], F32, ps_); lnvb = Buf()
            rstd = sb("p3c_rstd", [128, 512], F32, ps_); rstdb = Buf()
            xn = sb("p3c_xn", [128, 8, 512], BF16, ps_); xnb = Buf()
            sg = [sb(f"p3c_sg{i}", [128, 512], F32, ps_) for i in range(2)]; sgb = [Buf(), Buf()]
            outs = [sb(f"p3c_o{i}", [128, 8, 512], F32, ps_) for i in range(2)] if last else None
            outb = [Buf(), Buf()]
            dst, dst_b = (hC, hC_b)

            def ld(blk):
                A("sp", lambda blk=blk: nc.sync.dma_start(out=hbs[blk % 2][:], in_=hview(hB, blk * 512, 512)),
                  reads=[hB_b[blk]], writes=[hbb[blk % 2]], dma=True)
                A("sp", lambda blk=blk: nc.sync.dma_start(out=pfs[blk % 2][:], in_=pT[l][:, blk * 512:(blk + 1) * 512].rearrange("(c p) t -> p c t", p=128)),
                  writes=[pfb[blk % 2]], dma=True)
            ld(0)
            for blk in range(NBh):
                if blk + 1 < NBh:
                    ld(blk + 1)
                hb, hb_b = hbs[blk % 2], hbb[blk % 2]
                norm_block(hb, hb_b, 512, lambda c: gvec32[:, (l * 3 + 2) * 8 + c:(l * 3 + 2) * 8 + c + 1],
                           sq, sqb, lnv, lnvb, rstd, rstdb, xn, xnb)
                A("pool", lambda blk=blk: nc.gpsimd.tensor_copy(out=pbf[:], in_=pfs[blk % 2][:]), reads=[pfb[blk % 2]], writes=[pbb])
                for oc in range(8):
                    pg, pp = next_ps(), next_ps()
                    for c in range(8):
                        A("pe", lambda c=c, oc=oc, pg=pg: nc.tensor.matmul(psum[pg][:], w_pg_sb[:, c, oc * 128:(oc + 1) * 128], xn[:, c, :],
                                                                         start=(c == 0), stop=(c == 7)),
                          reads=[*wgb, xnb], writes=[psb[pg]], sig=(c == 7))
                    for c in range(2):
                        A("pe", lambda c=c, oc=oc, pp=pp: nc.tensor.matmul(psum[pp][:], w_pp_sb[:, c, oc * 128:(oc + 1) * 128], pbf[:, c, :],
                                                                         start=(c == 0), stop=(c == 1)),
                          reads=[*wpb, pbb], writes=[psb[pp]], sig=(c == 1))
                    s = oc % 2
                    A("act", lambda s=s, pg=pg: nc.scalar.activation(out=sg[s][:], in_=psum[pg][:], func=AF.Sigmoid),
                      reads=[psb[pg]], writes=[sgb[s]])
                    A("dve", lambda s=s, pp=pp: nc.vector.tensor_tensor(out=sg[s][:], in0=psum[pp][:], in1=sg[s][:], op=ALU.mult),
                      reads=[psb[pp], sgb[s]], writes=[sgb[s]])
                    A("dve", lambda s=s, oc=oc: nc.vector.tensor_tensor(out=hb[:, oc, :], in0=sg[s][:], in1=hb[:, oc, :], op=ALU.add),
                      reads=[sgb[s], hb_b], writes=[hb_b])
                if not last:
                    A("sp", lambda blk=blk, hb=hb: nc.sync.dma_start(out=hview(dst, blk * 512, 512), in_=hb[:]),
                      reads=[hb_b], writes=[dst_b[blk]], dma=True)
                else:
                    o_, ob_ = outs[blk % 2], outb[blk % 2]
                    norm_block(hb, hb_b, 512, lambda c: gfin32[:, c:c + 1], sq, sqb, lnv, lnvb, rstd, rstdb, None, None,
                               out_f32=o_, outb=ob_)
                    A("sp", lambda blk=blk, o_=o_: nc.sync.dma_start(out=hview(outT, blk * 512, 512), in_=o_[:]),
                      reads=[ob_], writes=[dst_b[blk]], dma=True)
        h_src, h_src_b = hC, hC_b

    P.barrier()
    A("pool", None, reads=hC_b)
    A("sp", None, reads=hC_b)
    n, nw = len(P.meta), (P.n_wait, getattr(P, 'n_standalone', 0), dict(P.cnt))
    es.close()
    return nc, (n, nw)


def _rel_bucket_np(dist):
    max_exact = 16
    d = np.maximum(dist, 1).astype(np.float32)
    large = max_exact + (np.log(d / np.float32(max_exact)) / np.float32(math.log(128 / max_exact))
                         * np.float32(32 - max_exact)).astype(np.int32)
    large = np.minimum(large, 31)
    return np.where(dist < max_exact, dist, large)


def host_prep(T, L, x_b, p_b, w, rank=0):
    f = np.float32
    m = {}
    m["xT"] = np.ascontiguousarray(x_b.T)
    m["pT"] = np.ascontiguousarray(np.transpose(p_b, (0, 2, 1)))
    for k_src, k_dst in (("w_up", "w_up"), ("w_down", "w_down"),
                         ("w_ple_gate", "w_pg"), ("w_ple_proj", "w_pp")):
        m[k_dst] = np.ascontiguousarray(w[k_src][:L])
    r = rank
    dh = [2 * r, 2 * r + 1]
    cols = []
    for g in range(2):
        sbp = [2 * g, 2 * g + 1]
        dhg = [2 * g, 2 * g + 1]
        for u in sbp:
            cols += list(range(128 * u, 128 * u + 128))
        for u in sbp:
            cols += list(range(512 + 128 * u, 512 + 128 * u + 128))
        for d_ in dhg:
            cols += list(range(1536 + 128 * d_, 1536 + 128 * d_ + 128))
        for d_ in dhg:
            cols += list(range(2048 + 128 * d_, 2048 + 128 * d_ + 128))
    for g in range(2):
        for u in (2 * g, 2 * g + 1):
            cols += list(range(1024 + 128 * u, 1024 + 128 * u + 128))
        for d_ in (2 * g, 2 * g + 1):
            cols += list(range(2560 + 128 * d_, 2560 + 128 * d_ + 128))
    m["w_in"] = np.ascontiguousarray(w["w_in"][:L][:, :, cols])
    m["hmask"] = np.full((128, 1), float(rank), np.float32)
    def orig_row(rr, rho):
        return 256 * rr + rho if rho < 256 else 512 + 256 * rr + (rho - 256)
    rows = [orig_row(rr, 64 * k + i) for k in range(8) for rr in range(2) for i in range(64)]
    m["w_o"] = np.ascontiguousarray(w["w_o"][:L][:, rows, :])
    g = np.stack([w["g_attn"][:L], w["g_ffn"][:L], w["g_ple"][:L]], axis=1)
    m["gvec"] = np.ascontiguousarray(g.reshape(L, 3, 8, 128).transpose(3, 0, 1, 2).reshape(128, L * 3 * 8))
    m["gfin"] = np.ascontiguousarray(w["g_final"].reshape(8, 128).T)
    m["convw"] = np.ascontiguousarray(w["conv_w"][:L].reshape(L, 3, 44, 128).transpose(3, 0, 1, 2).reshape(128, L * 3 * 44))
    m["convb"] = np.ascontiguousarray(w["conv_b"][:L].reshape(L, 44, 128).transpose(2, 0, 1).reshape(128, L * 44))
    m["gsub"] = np.ascontiguousarray(w["g_subln"][:L].T)
    lam = np.stack([w["lambda_q1"][:L], w["lambda_k1"][:L], w["lambda_q2"][:L], w["lambda_k2"][:L]], axis=1)
    m["lamv"] = np.ascontiguousarray(np.broadcast_to(lam.reshape(1, L * 4 * 64), (128, L * 4 * 64)))
    kl = np.arange(128)[:, None]
    xx = np.arange(1024)[None, :]
    dd = xx - 384 - kl
    idx = _rel_bucket_np(np.maximum(dd, 0))
    maps = [2 * d_ + j for d_ in dh for j in range(2)]
    bt = np.transpose(w["rel_bias"][:, maps][idx], (2, 0, 1)).astype(f)
    bt = np.where((dd < 0)[None], f(NEG), bt)
    m["btoe"] = np.ascontiguousarray(bt)
    negw = np.zeros((2, 128, 1024), f)
    xq = xx - 512
    negw[0] = np.where(xq <= kl, NEG, 0.0)
    negw[1] = np.where(xq < kl, NEG, 0.0)
    m["negw"] = negw
    cst = np.zeros((3, 128, 128), f)
    cst[0] = np.eye(128, dtype=f)
    jj = np.arange(128)[:, None]; ss = np.arange(128)[None, :]
    cst[1] = np.where(jj >= ss, -1.0, 0.0)
    cst[2] = 1.0
    m["cst"] = cst
    return {k: np.ascontiguousarray(v, dtype=f) for k, v in m.items()}


_CACHE = {}


def kernel(**inputs):
    x = np.asarray(inputs["x"], np.float32)
    p = np.asarray(inputs["p"], np.float32)
    w = {k: np.asarray(v, np.float32) for k, v in inputs.items() if k not in ("x", "p")}
    B, T, _ = x.shape
    L = p.shape[0]
    H = T // 2
    key = (T, L)
    if key not in _CACHE:
        _CACHE[key] = build_program(T, L)[0]
    nc = _CACHE[key]
    n_cores = 8
    maps = []
    per = {}
    for core in range(n_cores):
        b = (core * B) // n_cores
        r = core % 2
        maps.append(host_prep(T, L, x[b, r * H:(r + 1) * H], p[:, b, r * H:(r + 1) * H], w, rank=r))
    res = run_bass_kernel_spmd(nc, maps, core_ids=list(range(n_cores)))
    out = np.empty((B, T, D), np.float32)
    for core in range(n_cores):
        b, r = (core * B) // n_cores, core % 2
        out[b, r * H:(r + 1) * H] = res.results[core]["outT"].T
    return out
```

```python
import math
from contextlib import ExitStack
import numpy as np
import concourse.bass as bass
import concourse.mybir as mybir
from concourse.bass_utils import run_bass_kernel_spmd

F32 = mybir.dt.float32
BF16 = mybir.dt.bfloat16
AF = mybir.ActivationFunctionType
ALU = mybir.AluOpType
AX = mybir.AxisListType

D = 1024
DFF = 2816
NFC = 22
PLE = 256
EPS = 1e-6
NEG = -30000.0
SEM_LIM = 30000
N_DMA_SEM = 8


class Buf:
    __slots__ = ("w", "wd", "r", "rd")

    def __init__(self):
        self.w = None
        self.wd = []
        self.r = {}
        self.rd = []


class Prog:
    def __init__(self, nc, es):
        self.nc = nc
        self.es = es
        self.eng = {"pe": nc.tensor, "act": nc.scalar, "dve": nc.vector,
                    "pool": nc.gpsimd, "sp": nc.sync}
        self.meta = []
        self.cnt = {e: 0 for e in self.eng}
        self.sems = {e: [] for e in self.eng}
        self.dsems = {e: [es.enter_context(nc.semaphore(f"d_{e}_{j}")) for j in range(N_DMA_SEM)]
                      for e in ("sp", "pool")}
        self.dcount = {"sp": 0, "pool": 0}
        self.waited = {e: {p: 0 for p in self.eng} for e in self.eng}
        self.dwaited = {e: {} for e in self.eng}
        self.n_wait = 0

    def _sem(self, eng, g):
        j, v = (g - 1) // SEM_LIM, (g - 1) % SEM_LIM + 1
        lst = self.sems[eng]
        while len(lst) <= j:
            lst.append(self.es.enter_context(self.nc.semaphore(f"s_{eng}_{len(lst)}")))
        return lst[j], v

    def barrier(self):
        for eng, E in self.eng.items():
            for p in self.eng:
                g = self.cnt[p]
                if g == 0 or self.waited[eng][p] >= g:
                    continue
                self.waited[eng][p] = g
                s, v = self._sem(p, g)
                E.wait_ge(s, v)
                self.n_wait += 1
            for q in ("sp", "pool"):
                k = self.dcount[q]
                for j in range(min(k, N_DMA_SEM)):
                    last_k = ((k - 1 - j) // N_DMA_SEM) * N_DMA_SEM + j
                    v = 16 * (last_k // N_DMA_SEM + 1)
                    s = self.dsems[q][j]
                    if self.dwaited[eng].get(id(s), 0) < v:
                        self.dwaited[eng][id(s)] = v
                        E.wait_ge(s, v)
                        self.n_wait += 1

    def add(self, eng, fn, reads=(), writes=(), dma=False, sig=True, cc=None):
        i = len(self.meta)
        deps = set()
        for b in reads:
            if b.w is not None:
                deps.add(b.w)
            deps.update(b.wd)
        for b in writes:
            if b.w is not None:
                deps.add(b.w)
            if not dma or b.r or b.rd:
                deps.update(b.wd)
            deps.update(b.r.values())
            deps.update(b.rd)
        for b in writes:
            if dma:
                if b.r or b.rd:
                    b.wd = []
                b.wd.append(i)
            else:
                b.w = i
                b.wd = []
            b.r = {}
            b.rd = []
        for b in reads:
            if dma:
                b.rd.append(i)
            else:
                b.r[eng] = i
        deps.discard(i)
        E = self.eng[eng]
        need = {}
        waits = []
        for d in deps:
            deng, ddma, info = self.meta[d]
            if ddma:
                s, v = info
                key = id(s)
                if self.dwaited[eng].get(key, 0) < v:
                    self.dwaited[eng][key] = v
                    waits.append((s, v))
                continue
            if deng == eng and eng == "pe" and not dma:
                continue
            if info < 0:
                g = -info
                assert self.cnt[deng] >= g, "dependency on an unsignalled op whose group is not closed"
                info = g
            if info > need.get(deng, 0):
                need[deng] = info
        for deng, g in need.items():
            if self.waited[eng][deng] >= g:
                continue
            self.waited[eng][deng] = g
            waits.append(self._sem(deng, g))
        if dma and cc is None:
            k = self.dcount[eng]
            self.dcount[eng] += 1
            s = self.dsems[eng][k % N_DMA_SEM]
            v = 16 * (k // N_DMA_SEM + 1)
            if k >= N_DMA_SEM:
                key = id(s)
                if self.dwaited[eng].get(key, 0) < v - 16:
                    self.dwaited[eng][key] = v - 16
                    waits.append((s, v - 16))
        self.n_wait += len(waits)
        if cc is not None:
            for (ws, wv) in waits:
                E.wait_ge(ws, wv)
            fn().then_inc(cc[0], 1)
            self.meta.append((eng, True, cc))
            return i
        if fn is None:
            for (ws, wv) in waits:
                E.wait_ge(ws, wv)
            self.meta.append((eng, False, self.cnt[eng]))
            return i
        for (ws, wv) in waits[:-1]:
            E.wait_ge(ws, wv)
            self.n_standalone = getattr(self, 'n_standalone', 0) + 1
        inst = fn()
        if waits:
            inst._wait_ge(*waits[-1])
        if dma:
            inst.then_inc(s, 16)
            self.meta.append((eng, True, (s, v)))
        elif sig:
            self.cnt[eng] += 1
            g = self.cnt[eng]
            s, v = self._sem(eng, g)
            inst.then_inc(s, 1)
            self.meta.append((eng, False, g))
        else:
            self.meta.append((eng, False, -(self.cnt[eng] + 1)))
        return i


def build_program(T, L, taps=False):
    nc = bass.Bass("TRN2", target_bir_lowering=False)
    es = ExitStack()
    P = Prog(nc, es)
    NB = T // 512
    NKB = T // 128
    H = T // 2
    NBh = H // 512
    rank = nc.sync.snap(nc.sync.partition_id() % 2, min_val=0, max_val=1)
    r_qk = rank * 2048
    r_v = rank * (2 * H)
    r_h = rank * H
    VP = min(2048, H)
    npv = H // VP

    def din(name, shape, dt=F32):
        return nc.dram_tensor(name, list(shape), dt, kind="ExternalInput").ap()

    def dscr(name, shape, dt):
        kind = "ExternalOutput" if taps else "Internal"
        return nc.dram_tensor(name, list(shape), dt, kind=kind).ap()

    xT = din("xT", [D, H])
    pT = din("pT", [L, PLE, H])
    w_in = din("w_in", [L, D, 3072])
    w_o = din("w_o", [L, D, D])
    w_up = din("w_up", [L, D, 2 * DFF])
    w_down = din("w_down", [L, DFF, D])
    w_pg = din("w_pg", [L, D, D])
    w_pp = din("w_pp", [L, PLE, D])
    gvec_d = din("gvec", [128, L * 3 * 8])
    gfin_d = din("gfin", [128, 8])
    cw_d = din("convw", [128, L * 3 * 44])
    cb_d = din("convb", [128, L * 44])
    gsub_d = din("gsub", [128, L])
    lamv_d = din("lamv", [128, L * 4 * 64])
    btoe_d = din("btoe", [4, 128, 1024])
    negw_d = din("negw", [2, 128, 1024])
    cst_d = din("cst", [3, 128, 128])
    hmask_d = din("hmask", [128, 1])
    outT = nc.dram_tensor("outT", [D, H], F32, kind="ExternalOutput").ap()

    qk_loc = nc.dram_tensor("qk_loc", [NBh * 2048, 512], BF16)
    v_loc = nc.dram_tensor("v_loc", [NBh * 1024, 512], BF16)
    qk_scr, v_scr = qk_loc.ap(), v_loc.ap()
    qk_full = nc.dram_tensor("qk_full", [NBh * 4096, 512], BF16)
    v_full = nc.dram_tensor("v_full", [NBh * 2048, 512], BF16)
    halves = [(0, NBh // 2), (NBh // 2, NBh)] if NBh >= 2 else [(0, NBh)]
    qkfull_b, vfull_b = [Buf() for _ in halves], [Buf() for _ in halves]

    def half_of(blk):
        return 0 if (len(halves) == 1 or blk < NBh // 2) else 1
    qk_mine = nc.dram_tensor("qk_mine", [NBh * 2048, 512], BF16)
    v_mine = nc.dram_tensor("v_mine", [NBh * 1024, 512], BF16)
    mix_mine = nc.dram_tensor("mix_mine", [D, H], BF16)
    qkmine_b, vmine_b, mixmine_b = Buf(), Buf(), Buf()
    tail_loc = nc.dram_tensor("tail_loc", [D, 2], BF16)
    tail_full = nc.dram_tensor("tail_full", [2 * D, 2], BF16)
    tail_b, tailfull_b = Buf(), Buf()
    mix_loc = nc.dram_tensor("mix_loc", [512, T], BF16)
    mix_full = nc.dram_tensor("mix_full", [D, T], BF16)
    mix_scr = mix_loc.ap()
    mixfull_b = Buf()
    cc_sem = es.enter_context(nc.semaphore("cc_sem"))
    cc_n = [0]
    xn2_scr = dscr("xn2_scr", [D, H], BF16)
    hA = dscr("hA", [D, H], F32)
    hB = dscr("hB", [D, H], F32)
    hC = dscr("hC", [D, H], F32)

    def blkbufs(n=NBh):
        return [Buf() for _ in range(n)]
    qk_b, v_b, xn2_b, hA_b, hB_b, hC_b = (blkbufs() for _ in range(6))
    mix_bu = [blkbufs(NB) for _ in range(4)]

    uid = [0]

    def sb(name, shape, dt, stack=None):
        uid[0] += 1
        return (stack or es).enter_context(nc.sbuf_tensor(f"t{uid[0]}_{name}", list(shape), dt))

    pbig = [es.enter_context(nc.psum_tensor(f"psw{i}", [128, 1024], F32)) for i in range(4)]
    psum = [pbig[i // 2][:, (i % 2) * 512:(i % 2) * 512 + 512] for i in range(8)]
    psb = [Buf() for _ in range(8)]
    ident = sb("ident", [128, 128], BF16)
    ntri = sb("ntri", [128, 128], BF16)
    ones = sb("ones", [128, 128], BF16)
    gvec = sb("gvec_s", [128, L * 3 * 8], F32)
    gvec32 = sb("gvec32", [128, L * 3 * 8], F32)
    gfin = sb("gfin_s", [128, 8], F32)
    gfin32 = sb("gfin32", [128, 8], F32)
    cw = sb("cw_s", [128, L * 3 * 44], F32)
    cb = sb("cb_s", [128, L * 44], F32)
    gsub = sb("gsub_s", [128, L], F32)
    gsub2 = sb("gsub2", [128, L], F32)
    lamv = sb("lamv_s", [128, L * 4 * 64], F32)
    lamt = sb("lamt", [128, 64], F32)
    lams = sb("lams", [128, 2 * L], F32)
    neglam = sb("neglam", [128, L], F32)
    cstage = sb("cstage", [128, 3, 128], F32)
    hmask = sb("hmask_s", [128, 1], F32)
    b_const = Buf()

    def A(eng, fn, reads=(), writes=(), dma=False, sig=True, cc=None):
        return P.add(eng, fn, reads, writes, dma, sig, cc)

    A("sp", lambda: nc.sync.dma_start(out=cstage[:], in_=cst_d.rearrange("k p n -> p k n")),
      writes=[b_const], dma=True)
    for dst, src in ((gvec, gvec_d), (gfin, gfin_d), (cw, cw_d), (cb, cb_d), (gsub, gsub_d), (lamv, lamv_d), (hmask, hmask_d)):
        A("sp", lambda dst=dst, src=src: nc.sync.dma_start(out=dst[:], in_=src), writes=[b_const], dma=True)
    A("dve", lambda: nc.vector.tensor_copy(out=ident[:], in_=cstage[:, 0, :]), reads=[b_const], writes=[b_const])
    A("dve", lambda: nc.vector.tensor_copy(out=ntri[:], in_=cstage[:, 1, :]), reads=[b_const], writes=[b_const])
    A("dve", lambda: nc.vector.tensor_copy(out=ones[:], in_=cstage[:, 2, :]), reads=[b_const], writes=[b_const])
    A("dve", lambda: nc.vector.tensor_scalar(out=gvec32[:], in0=gvec[:], scalar1=32.0, scalar2=None, op0=ALU.mult),
      reads=[b_const], writes=[b_const])
    A("dve", lambda: nc.vector.tensor_scalar(out=gfin32[:], in0=gfin[:], scalar1=32.0, scalar2=None, op0=ALU.mult),
      reads=[b_const], writes=[b_const])
    for l in range(L):
        li = 0.8 - 0.6 * math.exp(-0.3 * l)
        for j in range(2):
            o = (l * 4 + 2 * j) * 64
            A("dve", lambda o=o: nc.vector.tensor_tensor(out=lamt[:], in0=lamv[:, o:o + 64], in1=lamv[:, o + 64:o + 128],
                                                         op=ALU.mult), reads=[b_const], writes=[b_const])
            A("dve", lambda l=l, j=j: nc.vector.reduce_sum(out=lams[:, 2 * l + j:2 * l + j + 1], in_=lamt[:], axis=AX.X),
              reads=[b_const], writes=[b_const])
        A("act", lambda l=l: nc.scalar.activation(out=lams[:, 2 * l:2 * l + 2], in_=lams[:, 2 * l:2 * l + 2], func=AF.Exp),
          reads=[b_const], writes=[b_const])
        A("dve", lambda l=l, li=li: nc.vector.scalar_tensor_tensor(
            out=neglam[:, l:l + 1], in0=lams[:, 2 * l + 1:2 * l + 2], scalar=-li, in1=lams[:, 2 * l:2 * l + 1],
            op0=ALU.add, op1=ALU.subtract), reads=[b_const], writes=[b_const])
        A("dve", lambda l=l, li=li: nc.vector.tensor_scalar(
            out=gsub2[:, l:l + 1], in0=gsub[:, l:l + 1], scalar1=(1.0 - li) * math.sqrt(128.0), scalar2=None, op0=ALU.mult),
            reads=[b_const], writes=[b_const])

    rr = {"ps": 0}

    def next_ps():
        i = rr["ps"] % 8
        rr["ps"] += 1
        return i

    def load_weight(dst, dstbuf, src2d, nchunk, ncols, stg, stgb, colstep):
        k = 0
        for c in range(nchunk):
            for c0 in range(0, ncols, colstep):
                w = min(colstep, ncols - c0)
                s, sbuf_ = stg[k % 2], stgb[k % 2]
                A("sp", lambda s=s, c=c, c0=c0, w=w: nc.sync.dma_start(
                    out=s[:, :w], in_=src2d[c * 128:(c + 1) * 128, c0:c0 + w]), writes=[sbuf_], dma=True)
                if k % 2 == 0:
                    A("pool", lambda s=s, c=c, c0=c0, w=w: nc.gpsimd.tensor_copy(out=dst[:, c, c0:c0 + w], in_=s[:, :w]),
                      reads=[sbuf_], writes=[dstbuf[0]])
                else:
                    A("act", lambda s=s, c=c, c0=c0, w=w: nc.scalar.activation(out=dst[:, c, c0:c0 + w], in_=s[:, :w], func=AF.Copy),
                      reads=[sbuf_], writes=[dstbuf[1]])
                k += 1

    def weight_pieces(dst, dstbuf, src2d, nchunk, ncols, stg, stgb, colstep):
        out = []
        k = 0
        for c in range(nchunk):
            for c0 in range(0, ncols, colstep):
                w = min(colstep, ncols - c0)
                s_, sb_ = stg[k % 2], stgb[k % 2]

                def piece(s_=s_, sb_=sb_, c=c, c0=c0, w=w, k=k):
                    A("sp", lambda: nc.sync.dma_start(out=s_[:, :w], in_=src2d[c * 128:(c + 1) * 128, c0:c0 + w]),
                      writes=[sb_], dma=True)
                    if k % 2 == 0:
                        A("pool", lambda: nc.gpsimd.tensor_copy(out=dst[:, c, c0:c0 + w], in_=s_[:, :w]),
                          reads=[sb_], writes=[dstbuf[0]])
                    else:
                        A("act", lambda: nc.scalar.activation(out=dst[:, c, c0:c0 + w], in_=s_[:, :w], func=AF.Copy),
                          reads=[sb_], writes=[dstbuf[1]])
                out.append(piece)
                k += 1
        return out

    def norm_block(hb, hbb, n, gcol, sq, sqb, lnv, lnvb, rstd, rstdb, xn, xnb, xc0=0, out_f32=None, outb=None, part=None):
        if part in (None, 0):
            A("act", lambda: nc.scalar.activation(out=sq[:, :, :n], in_=hb[:, :, :n], func=AF.Square),
              reads=[hbb], writes=[sqb])
        if part == 0:
            return
        pi = next_ps()
        for c in range(8):
            A("pe", lambda c=c, pi=pi: nc.tensor.matmul(psum[pi][:, :n], ones[:], sq[:, c, :n], start=(c == 0), stop=(c == 7)),
              reads=[sqb, b_const], writes=[psb[pi]], sig=(c == 7))
        A("act", lambda pi=pi: nc.scalar.activation(out=lnv[:, :n], in_=psum[pi][:, :n], func=AF.Ln, bias=1024.0 * EPS, scale=1.0),
          reads=[psb[pi]], writes=[lnvb])
        A("act", lambda: nc.scalar.activation(out=rstd[:, :n], in_=lnv[:, :n], func=AF.Exp, scale=-0.5),
          reads=[lnvb], writes=[rstdb])
        for c in range(8):
            if out_f32 is None:
                A("dve", lambda c=c: nc.vector.scalar_tensor_tensor(
                    out=xn[:, c, xc0:xc0 + n], in0=hb[:, c, :n], scalar=gcol(c), in1=rstd[:, :n], op0=ALU.mult, op1=ALU.mult),
                    reads=[hbb, rstdb, b_const], writes=[xnb])
            else:
                A("dve", lambda c=c: nc.vector.scalar_tensor_tensor(
                    out=out_f32[:, c, :n], in0=hb[:, c, :n], scalar=gcol(c), in1=rstd[:, :n], op0=ALU.mult, op1=ALU.mult),
                    reads=[hbb, rstdb, b_const], writes=[outb])

    def hview(ap2d, t0, n):
        return ap2d[:, t0:t0 + n].rearrange("(c p) t -> p c t", p=128)

    h_src, h_src_b = xT, [Buf() for _ in range(NBh)]

    for l in range(L):
        last = (l == L - 1)
        def pick_half(hi, b0, b1):
            A("sp", lambda: nc.sync.dma_start(
                out=qk_mine.ap().rearrange("(b j o w) t -> b j o (w t)", j=2, o=1, w=1024)[b0:b1],
                in_=qk_full.ap().rearrange("(b j g w) t -> b j g (w t)", j=2, g=2, w=1024)[b0:b1, :, bass.ds(rank, 1), :]),
              reads=[qkfull_b[hi]], writes=[qkmine_b], dma=True)
            A("sp", lambda: nc.sync.dma_start(
                out=v_mine.ap().rearrange("(b j o w) c -> b j o (w c)", j=2, o=1, w=512)[b0:b1],
                in_=v_full.ap().rearrange("(b j g w) c -> b j g (w c)", j=2, g=2, w=512)[b0:b1, :, bass.ds(rank, 1), :]),
              reads=[vfull_b[hi]], writes=[vmine_b], dma=True)

        P.barrier()
        with ExitStack() as ps_:
            w_in_sb = sb("w_in_sb", [128, 8, 3072], BF16, ps_)
            wb = [Buf(), Buf()]
            stg = [sb(f"stg1_{i}", [128, 1536], F32, ps_) for i in range(2)]
            stgb = [Buf(), Buf()]
            load_weight(w_in_sb, wb, w_in[l], 8, 3072, stg, stgb, 1536)
            hbs = [sb(f"p1_h{i}", [128, 8, 512], F32, ps_) for i in range(2)]
            hbb = [Buf(), Buf()]
            sq = sb("p1_sq", [128, 8, 512], BF16, ps_); sqb = Buf()
            lnv = sb("p1_lnv", [128, 512], F32, ps_); lnvb = Buf()
            rstd = sb("p1_rstd", [128, 512], F32, ps_); rstdb = Buf()
            xns_ = [sb(f"p1_xn{i}", [128, 8, 512], BF16, ps_) for i in range(2)]; xnbs_ = [Buf(), Buf()]
            qst = [sb(f"p1_qst{i}", [128, 4, 512], BF16, ps_) for i in range(2)]
            qstb = [Buf(), Buf()]
            vst = [sb(f"p1_vst{i}", [128, 1024], BF16, ps_) for i in range(2)]
            vstb = [Buf(), Buf()]
            qcols = [128 * j for j in range(16)]
            vcols = [2048, 2560]

            def ld(blk):
                A("sp", lambda blk=blk: nc.sync.dma_start(out=hbs[blk % 2][:], in_=hview(h_src, blk * 512, 512)),
                  reads=[h_src_b[blk]], writes=[hbb[blk % 2]], dma=True)
            ld(0)
            ev = 0
            g1col = lambda c: gvec32[:, (l * 3 + 0) * 8 + c:(l * 3 + 0) * 8 + c + 1]

            def p1_norm(b, part):
                norm_block(hbs[b % 2], hbb[b % 2], 512, g1col, sq, sqb, lnv, lnvb, rstd, rstdb,
                           xns_[b % 2], xnbs_[b % 2], part=part)
            p1_norm(0, None)
            for blk in range(NBh):
                if blk + 1 < NBh:
                    ld(blk + 1)
                xn, xnb = xns_[blk % 2], xnbs_[blk % 2]
                t0 = blk * 512
                for j4 in range(4):
                    st, stb = qst[j4 % 2], qstb[j4 % 2]
                    for jj in range(4):
                        j = j4 * 4 + jj
                        col = qcols[j]
                        pi = next_ps()
                        for c in range(8):
                            A("pe", lambda c=c, pi=pi, col=col: nc.tensor.matmul(
                                psum[pi][:], w_in_sb[:, c, col:col + 128], xn[:, c, :], start=(c == 0), stop=(c == 7)),
                                reads=[*wb, xnb], writes=[psb[pi]], sig=(c == 7))
                        scale = 0.125 if (j % 8) in (0, 1, 4, 5) else 1.0
                        if ev % 2 == 0:
                            A("act", lambda pi=pi, st=st, jj=jj, scale=scale: nc.scalar.activation(
                                out=st[:, jj, :], in_=psum[pi][:], func=AF.Copy, scale=scale), reads=[psb[pi]], writes=[stb])
                        else:
                            A("dve", lambda pi=pi, st=st, jj=jj, scale=scale: nc.vector.tensor_scalar(
                                out=st[:, jj, :], in0=psum[pi][:], scalar1=scale, scalar2=None, op0=ALU.mult),
                                reads=[psb[pi]], writes=[stb])
                        ev += 1
                    A("sp", lambda st=st, j4=j4, t0=t0: nc.sync.dma_start(
                        out=qk_scr[blk * 2048 + j4 * 512:blk * 2048 + (j4 + 1) * 512, :].rearrange("(c p) t -> p c t", p=128), in_=st[:]),
                        reads=[stb], writes=[qk_b[blk]], dma=True)
                if blk + 1 < NBh:
                    p1_norm(blk + 1, 0)
                for s in range(4):
                    if s == 2 and blk + 1 < NBh:
                        p1_norm(blk + 1, 1)
                    st, stb = vst[s % 2], vstb[s % 2]
                    for half in range(2):
                        pi = next_ps()
                        vc = vcols[half]
                        for c in range(8):
                            A("pe", lambda c=c, pi=pi, s=s, vc=vc: nc.tensor.matmul(
                                psum[pi][:], xn[:, c, s * 128:(s + 1) * 128], w_in_sb[:, c, vc:vc + 512],
                                start=(c == 0), stop=(c == 7)), reads=[*wb, xnb], writes=[psb[pi]], sig=(c == 7))
                        if ev % 2 == 0:
                            A("act", lambda pi=pi, st=st, half=half: nc.scalar.activation(
                                out=st[:, half * 512:(half + 1) * 512], in_=psum[pi][:], func=AF.Copy), reads=[psb[pi]], writes=[stb])
                        else:
                            A("dve", lambda pi=pi, st=st, half=half: nc.vector.tensor_copy(
                                out=st[:, half * 512:(half + 1) * 512], in_=psum[pi][:]), reads=[psb[pi]], writes=[stb])
                        ev += 1
                    for g in range(2):
                        r0 = blk * 1024 + g * 512 + s * 128
                        A("sp", lambda st=st, r0=r0, g=g: nc.sync.dma_start(
                            out=v_scr[r0:r0 + 128, :], in_=st[:, g * 512:(g + 1) * 512]),
                            reads=[stb], writes=[v_b[blk]], dma=True)
                cc_n[0] += 1
                A("pool", lambda blk=blk: nc.gpsimd.collective_compute(
                    "AllGather", ALU.bypass, replica_groups=[[0, 1], [2, 3], [4, 5], [6, 7]],
                    ins=[qk_loc.ap()[blk * 2048:(blk + 1) * 2048, :].opt()], outs=[qk_full.ap()[blk * 4096:(blk + 1) * 4096, :].opt()]),
                  reads=[qk_b[blk]], writes=[qkfull_b[half_of(blk)]], dma=True, cc=(cc_sem, cc_n[0]))
                cc_n[0] += 1
                A("pool", lambda blk=blk: nc.gpsimd.collective_compute(
                    "AllGather", ALU.bypass, replica_groups=[[0, 1], [2, 3], [4, 5], [6, 7]],
                    ins=[v_loc.ap()[blk * 1024:(blk + 1) * 1024, :].opt()], outs=[v_full.ap()[blk * 2048:(blk + 1) * 2048, :].opt()]),
                  reads=[v_b[blk]], writes=[vfull_b[half_of(blk)]], dma=True, cc=(cc_sem, cc_n[0]))
                for hi, (b0, b1) in enumerate(halves):
                    if blk == min(b1 + 1, NBh - 1):
                        pick_half(hi, b0, b1)

        P.barrier()
        with ExitStack() as ps_:
            btoe = sb("btoe", [128, 4, 1024], F32, ps_); btb = Buf()
            negw = sb("negw", [128, 2, 1024], BF16, ps_)
            negst = sb("negst", [128, 2, 1024], F32, ps_)
            A("sp", lambda: nc.sync.dma_start(out=btoe[:], in_=btoe_d.rearrange("m p x -> p m x")), writes=[btb], dma=True)
            A("sp", lambda: nc.sync.dma_start(out=negst[:], in_=negw_d.rearrange("m p x -> p m x")), writes=[btb], dma=True)
            A("dve", lambda: nc.vector.tensor_copy(out=negw[:], in_=negst[:]), reads=[btb], writes=[btb])
            c31 = sb("c31", [128, 4], F32, ps_)
            A("dve", lambda: nc.vector.tensor_copy(out=c31[:], in_=btoe[:, :, 1023]), reads=[btb], writes=[btb])
            for m_ in range(4):
                A("dve", lambda m_=m_: nc.vector.tensor_scalar(out=btoe[:, m_, :], in0=btoe[:, m_, :], scalar1=c31[:, m_:m_ + 1],
                                                               scalar2=None, op0=ALU.subtract), reads=[btb], writes=[btb])
            qt2 = [sb(f"qt2_{i}", [128, T], BF16, ps_) for i in range(2)]
            kt2 = [sb(f"kt2_{i}", [128, T], BF16, ps_) for i in range(2)]
            v2 = [sb(f"v2_{i}", [128, NKB, 128], BF16, ps_) for i in range(2)]
            pairb = [Buf(), Buf()]
            e_t = [sb(f"e_t{i}", [128, 512], F32, ps_) for i in range(2)]; e_b = [Buf(), Buf()]
            sp_t = [sb(f"sp_t{i}", [128, 512], BF16, ps_) for i in range(2)]; sp_b = [Buf(), Buf()]
            la_t = [sb(f"la_t{i}", [128, 512], F32, ps_) for i in range(2)]; la_b = [Buf(), Buf()]
            a_t = [sb(f"a_t{i}", [128, 512], BF16, ps_) for i in range(2)]; a_b = [Buf(), Buf()]
            tc_t = [sb(f"tc_t{i}", [128, 512], F32, ps_) for i in range(2)]; tc_b = [Buf(), Buf()]
            ost = [sb(f"ost{i}", [128, 512], BF16, ps_) for i in range(2)]; ost_b = [Buf(), Buf()]
            pdw = [sb(f"pdw{i}", [128, 1024], BF16, ps_) for i in range(2)]
            pd_t = [[pdw[i][:, 512 * c:512 * c + 512] for i in range(2)] for c in range(2)]
            pdw_b = [Buf(), Buf()]
            pd_b = [[pdw_b[0], pdw_b[1]], [pdw_b[0], pdw_b[1]]]
            rsum = [sb(f"rsum{c}", [128, 512], F32, ps_) for c in range(2)]; rsum_b = [Buf(), Buf()]
            rsbf = [sb(f"rsbf{c}", [128, 512], BF16, ps_) for c in range(2)]; rsbf_b = [Buf(), Buf()]
            rw_t = sb("rw_t", [128, 1024], F32, ps_)
            r_t = [rw_t[:, 0:512], rw_t[:, 512:1024]]; r_b = [Buf(), Buf()]
            o_t = sb("o_t", [128, 512], F32, ps_); o_b = Buf()
            sq2 = sb("sq2", [128, 512], BF16, ps_); sq2b = Buf()
            ln2 = sb("ln2", [128, 512], F32, ps_); ln2b = Buf()
            rs2 = sb("rs2", [128, 512], F32, ps_); rs2b = Buf()
            mixo = sb("mixo", [128, 512], BF16, ps_); mixob = Buf()
            ew_t = sb("ew_t", [128, 1024], F32, ps_); ew_b = Buf()
            spw_ts = [sb(f"spw_t{i}", [128, 1024], BF16, ps_) for i in range(2)]; spw_bs = [Buf(), Buf()]
            law_ts = [sb(f"law_t{i}", [128, 1024], F32, ps_) for i in range(2)]; law_bs = [Buf(), Buf()]
            aw_ts = [sb(f"aw_t{i}", [128, 1024], BF16, ps_) for i in range(2)]; aw_bs = [Buf(), Buf()]
            tcw_t = sb("tcw_t", [128, 1024], F32, ps_); tcw_b = Buf()
            ostw = sb("ostw", [128, 1024], BF16, ps_); ostw_b = Buf()
            zbig = [pbig[0], pbig[1]]
            sbig, obig = pbig[2], pbig[3]
            ZB, LB, SB_, OB = (0, 1), (2, 3), (4, 5), (6, 7)

            def load_pair(u, slot):
                if u < 2:
                    qc_, kc_, vcol = u, 2 + u, 128 * u
                else:
                    qc_, kc_, vcol = 4 + (u - 2), 6 + (u - 2), 256 + 128 * (u - 2)
                qk4 = qk_mine.ap().rearrange("(b j w) t -> b j w t", j=2, w=1024)
                for (dst, c_) in ((qt2[slot], qc_), (kt2[slot], kc_)):
                    for j in range(2):
                        A("sp", lambda dst=dst, c_=c_, j=j: nc.sync.dma_start(
                            out=dst[:, j * H:(j + 1) * H].rearrange("p (b t) -> p b t", t=512),
                            in_=qk4[:, j, c_ * 128:(c_ + 1) * 128, :].rearrange("b p t -> p b t")),
                            reads=[qkmine_b], writes=[pairb[slot]], dma=True)
                for j in range(2):
                    for b_ in range(NBh):
                        kb0 = (j * H + b_ * 512) // 128
                        r0 = b_ * 1024 + j * 512
                        A("sp", lambda kb0=kb0, r0=r0: nc.sync.dma_start(
                            out=v2[slot][:, kb0:kb0 + 4, :],
                            in_=v_mine.ap()[r0:r0 + 512, vcol:vcol + 128].rearrange("(kb p) d -> p kb d", p=128)),
                            reads=[vmine_b], writes=[pairb[slot]], dma=True)

            def mix_exchange(u):
                for k in (2 * u, 2 * u + 1):
                    cc_n[0] += 1
                    A("pool", lambda k=k: nc.gpsimd.collective_compute(
                        "AllGather", ALU.bypass, replica_groups=[[0, 1], [2, 3], [4, 5], [6, 7]],
                        ins=[mix_loc.ap()[64 * k:64 * k + 64, :].opt()], outs=[mix_full.ap()[128 * k:128 * k + 128, :].opt()]),
                      reads=mix_bu[u], writes=[mixfull_b], dma=True, cc=(cc_sem, cc_n[0]))

            pending = []
            load_pair(0, 0)
            for u in range(4):
                if u >= 1:
                    mix_exchange(u - 1)
                slot = u % 2
                if u + 1 < 4:
                    load_pair(u + 1, (u + 1) % 2)
                QT, KT, V2, pb = qt2[slot], kt2[slot], v2[slot], pairb[slot]
                is_sb = u < 2
                for qc in range(NB):
                    q0 = qc * 512
                    nkb = 4 * (qc + 1)
                    order = list(range(nkb - 1, -1, -1))

                    def qk(i, bank, ch, last_stop):
                        kb = order[i]
                        k0 = kb * 128
                        diag = k0 >= q0
                        lo = 64 * ch
                        A("pe", lambda: nc.tensor.matmul(psum[bank][:], KT[lo:lo + 64, k0:k0 + 128], QT[lo:lo + 64, q0:q0 + 512],
                                                         start=True, stop=(last_stop and not (diag and is_sb))),
                          reads=[pb], writes=[psb[bank]])
                        if diag and is_sb:
                            c0 = k0 - q0
                            A("pe", lambda: nc.tensor.matmul(psum[bank][:], ident[:], negw[:, 0, 512 - c0:1024 - c0],
                                                             start=False, stop=last_stop),
                              reads=[btb, b_const], writes=[psb[bank]])

                    if is_sb:
                        def zqk(i):
                            kb = order[i]
                            k0 = kb * 128
                            diag = k0 >= q0
                            zt, zk = zbig[i % 2], 2 * (i % 2)
                            for ch in range(2):
                                lo = 64 * ch
                                A("pe", lambda lo=lo, ch=ch: nc.tensor.matmul(
                                    zt[:, 512 * ch:512 * ch + 512], KT[lo:lo + 64, k0:k0 + 128], QT[lo:lo + 64, q0:q0 + 512],
                                    start=True, stop=not diag), reads=[pb], writes=[psb[zk + ch]], sig=(ch == 1 and not diag))
                            if diag:
                                c0 = k0 - q0
                                for ch in range(2):
                                    A("pe", lambda ch=ch: nc.tensor.matmul(
                                        zt[:, 512 * ch:512 * ch + 512], ident[:], negw[:, 0, 512 - c0:1024 - c0],
                                        start=False, stop=True), reads=[btb, b_const], writes=[psb[zk + ch]], sig=(ch == 1))

                        def act_A(i):
                            A("act", lambda: nc.scalar.activation(out=aw_ts[i % 2][:], in_=law_ts[i % 2][:], func=AF.Exp),
                              reads=[law_bs[i % 2]], writes=[aw_bs[i % 2]])

                        def pe_PV(i):
                            kbp = order[i]
                            for ch in range(2):
                                A("pe", lambda ch=ch: nc.tensor.matmul(
                                    obig[0:64, 512 * ch:512 * ch + 512], V2[:, kbp, 64 * ch:64 * ch + 64], aw_ts[i % 2][:, 512 * ch:512 * ch + 512],
                                    start=(i == 0), stop=(i == nkb - 1)), reads=[pb, aw_bs[i % 2]], writes=[psb[6 + ch]], sig=(ch == 1))

                        zqk(0)
                        for i in range(nkb):
                            zt, zk = zbig[i % 2], 2 * (i % 2)
                            zpair = [psb[zk], psb[zk + 1]]
                            A("act", lambda zt=zt: nc.scalar.activation(out=ew_t[:], in_=zt[:], func=AF.Exp),
                              reads=zpair, writes=[ew_b])
                            if i + 1 < nkb:
                                zqk(i + 1)
                            spw_t, spw_b = spw_ts[i % 2], spw_bs[i % 2]
                            law_t, law_b = law_ts[i % 2], law_bs[i % 2]
                            A("act", lambda spw_t=spw_t: nc.scalar.activation(out=spw_t[:], in_=ew_t[:], func=AF.Ln, bias=1.0, scale=1.0),
                              reads=[ew_b], writes=[spw_b])
                            for ch in range(2):
                                A("pe", lambda ch=ch, zt=zt, spw_t=spw_t: nc.tensor.matmul(
                                    zt[:, 512 * ch:512 * ch + 512], ntri[:], spw_t[:, 512 * ch:512 * ch + 512],
                                    start=False, stop=True, skip_group_check=True),
                                  reads=[spw_b, b_const], writes=[psb[zk + ch]], sig=(ch == 1))
                            if i + 1 < nkb:
                                for ch in range(2):
                                    A("pe", lambda ch=ch, spw_t=spw_t: nc.tensor.matmul(
                                        sbig[:, 512 * ch:512 * ch + 512], ones[:], spw_t[:, 512 * ch:512 * ch + 512],
                                        start=True, stop=True), reads=[spw_b, b_const], writes=[psb[4 + ch]], sig=(ch == 1))
                            if i >= 1:
                                act_A(i - 1)
                                pe_PV(i - 1)
                            if i == 0:
                                A("dve", lambda zt=zt, law_t=law_t: nc.vector.tensor_copy(out=law_t[:], in_=zt[:]),
                                  reads=zpair, writes=[law_b])
                            else:
                                A("dve", lambda zt=zt, law_t=law_t: nc.vector.tensor_tensor(out=law_t[:], in0=zt[:], in1=tcw_t[:], op=ALU.subtract),
                                  reads=zpair + [tcw_b], writes=[law_b])
                            if i + 1 < nkb:
                                if i == 0:
                                    A("dve", lambda: nc.vector.tensor_copy(out=tcw_t[:], in_=sbig[:]),
                                      reads=[psb[4], psb[5]], writes=[tcw_b])
                                else:
                                    A("dve", lambda: nc.vector.tensor_tensor(out=tcw_t[:], in0=sbig[:], in1=tcw_t[:], op=ALU.add),
                                      reads=[psb[4], psb[5], tcw_b], writes=[tcw_b])
                        act_A(nkb - 1)
                        pe_PV(nkb - 1)
                        A("dve", lambda: nc.vector.tensor_copy(out=ostw[0:64, :], in_=obig[0:64, :]),
                          reads=[psb[6], psb[7]], writes=[ostw_b])
                        A("sp", lambda: nc.sync.dma_start(
                            out=mix_scr[128 * u:128 * u + 128, q0:q0 + 512].rearrange("(c p) t -> p c t", p=64),
                            in_=ostw[0:64, :].rearrange("p (c t) -> p c t", c=2)),
                          reads=[ostw_b], writes=[mix_bu[u][qc]], dma=True)
                    else:
                        hd = u - 2
                        zb = [(0, 1), (2, 3)]
                        for ch in range(2):
                            qk(0, zb[0][ch], ch, True)
                        for i in range(nkb):
                            if i == 2 and pending:
                                pending.pop()()
                            kb = order[i]
                            k0 = kb * 128
                            near = (k0 >= q0 - 128)
                            zz = zb[i % 2]
                            if i + 1 < nkb:
                                for ch in range(2):
                                    qk(i + 1, zb[(i + 1) % 2][ch], ch, True)
                            if near:
                                for ch in range(2):
                                    m = 2 * hd + ch
                                    pt, ptb = pd_t[ch][i % 2], pd_b[ch][i % 2]
                                    x0 = q0 - k0 + 384
                                    A("dve", lambda ch=ch, m=m, x0=x0, zz=zz: nc.vector.tensor_tensor(
                                        out=la_t[ch][:], in0=psum[zz[ch]][:], in1=btoe[:, m, x0:x0 + 512], op=ALU.add),
                                        reads=[psb[zz[ch]], btb], writes=[la_b[ch]])
                                    A("act", lambda ch=ch, pt=pt: nc.scalar.activation(out=pt[:], in_=la_t[ch][:], func=AF.Exp),
                                      reads=[la_b[ch]], writes=[ptb])
                            else:
                                A("act", lambda i=i: nc.scalar.activation(out=pdw[i % 2][:], in_=pbig[i % 2][:], func=AF.Exp),
                                  reads=[psb[zz[0]], psb[zz[1]]], writes=[pdw_b[i % 2]])
                            for ch in range(2):
                                pt, ptb = pd_t[ch][i % 2], pd_b[ch][i % 2]
                                A("pe", lambda ch=ch, kb=kb, i=i, pt=pt: nc.tensor.matmul(psum[OB[ch]][:], V2[:, kb, :], pt[:],
                                                                                        start=(i == 0), stop=(i == nkb - 1)),
                                  reads=[pb, ptb], writes=[psb[OB[ch]]])
                                if ch == 0:
                                    A("pe", lambda i=i, pt=pt: nc.tensor.matmul(psum[SB_[0]][:], ones[:], pt[:],
                                                                                start=(i == 0), stop=(i == nkb - 1)),
                                      reads=[ptb, b_const], writes=[psb[SB_[0]]])
                                else:
                                    if i == 0:
                                        A("dve", lambda pt=pt: nc.vector.tensor_copy(out=rsum[1][:], in_=pt[:]),
                                          reads=[ptb], writes=[rsum_b[1]])
                                    else:
                                        A("dve", lambda pt=pt: nc.vector.tensor_tensor(out=rsum[1][:], in0=pt[:], in1=rsum[1][:], op=ALU.add),
                                          reads=[ptb, rsum_b[1]], writes=[rsum_b[1]])
                        for ch in range(1, 2):
                            A("dve", lambda ch=ch: nc.vector.tensor_copy(out=rsbf[ch][:], in_=rsum[ch][:]),
                              reads=[rsum_b[ch]], writes=[rsbf_b[ch]])
                            A("pe", lambda ch=ch: nc.tensor.matmul(psum[SB_[ch]][:], ones[:], rsbf[ch][:], start=True, stop=True),
                              reads=[rsbf_b[ch], b_const], writes=[psb[SB_[ch]]])
                        A("act", lambda: nc.scalar.activation(out=rw_t[:], in_=pbig[2][:], func=AF.Ln),
                          reads=[psb[4], psb[5]], writes=[r_b[0], r_b[1]])
                        A("act", lambda: nc.scalar.activation(out=rw_t[:], in_=rw_t[:], func=AF.Exp, scale=-1.0),
                          reads=[r_b[0], r_b[1]], writes=[r_b[0], r_b[1]])
                        for ch in range(2):
                            A("dve", lambda ch=ch: nc.vector.tensor_tensor(out=r_t[ch][:], in0=psum[OB[ch]][:], in1=r_t[ch][:], op=ALU.mult),
                              reads=[psb[OB[ch]], r_b[ch]], writes=[r_b[ch]])
                        A("dve", lambda: nc.vector.scalar_tensor_tensor(out=o_t[:], in0=r_t[1][:], scalar=neglam[:, l:l + 1], in1=r_t[0][:],
                                                                        op0=ALU.mult, op1=ALU.add),
                          reads=[r_b[0], r_b[1], b_const], writes=[o_b])
                        def tail(q0=q0, qc=qc, u=u, hd=hd):
                            A("act", lambda: nc.scalar.activation(out=sq2[:], in_=o_t[:], func=AF.Square), reads=[o_b], writes=[sq2b])
                            A("pe", lambda: nc.tensor.matmul(psum[5][:], ones[:], sq2[:], start=True, stop=True),
                              reads=[sq2b, b_const], writes=[psb[5]])
                            A("act", lambda: nc.scalar.activation(out=ln2[:], in_=psum[5][:], func=AF.Ln, bias=128.0 * EPS, scale=1.0),
                              reads=[psb[5]], writes=[ln2b])
                            A("act", lambda: nc.scalar.activation(out=rs2[:], in_=ln2[:], func=AF.Exp, scale=-0.5), reads=[ln2b], writes=[rs2b])
                            A("dve", lambda: nc.vector.scalar_tensor_tensor(out=mixo[:], in0=o_t[:], scalar=gsub2[:, l:l + 1], in1=rs2[:],
                                                                            op0=ALU.mult, op1=ALU.mult),
                              reads=[o_b, rs2b, b_const], writes=[mixob])
                            row = 256 + 128 * hd
                            A("sp", lambda: nc.sync.dma_start(out=mix_scr[row:row + 128, q0:q0 + 512], in_=mixo[:]),
                              reads=[mixob], writes=[mix_bu[u][qc]], dma=True)
                        pending.append(tail)
                if pending:
                    pending.pop()()

        for k in (6, 7):
            cc_n[0] += 1
            A("pool", lambda k=k: nc.gpsimd.collective_compute(
                "AllGather", ALU.bypass, replica_groups=[[0, 1], [2, 3], [4, 5], [6, 7]],
                ins=[mix_loc.ap()[64 * k:64 * k + 64, :].opt()], outs=[mix_full.ap()[128 * k:128 * k + 128, :].opt()]),
              reads=mix_bu[3], writes=[mixfull_b], dma=True, cc=(cc_sem, cc_n[0]))
        P.barrier()
        ffn_scope = ExitStack()
        w_up_sb = sb("w_up_sb", [128, 8, 2 * DFF], BF16, ffn_scope); wub = [Buf(), Buf()]
        stg_f = [sb(f"stg3b_{i}", [128, 1408], F32, ffn_scope) for i in range(2)]; stgb_f = [Buf(), Buf()]
        up_pieces = weight_pieces(w_up_sb, wub, w_up[l], 8, 2 * DFF, stg_f, stgb_f, 1408)
        with ExitStack() as ps_:
            w_o_sb = sb("w_o_sb", [128, 8, 1024], BF16, ps_); wb = [Buf(), Buf()]
            stg = [sb(f"stg3a_{i}", [128, 1024], F32, ps_) for i in range(2)]; stgb = [Buf(), Buf()]
            load_weight(w_o_sb, wb, w_o[l], 8, 1024, stg, stgb, 1024)
            A("sp", lambda: nc.sync.dma_start(out=mix_mine.ap(), in_=mix_full.ap()[:, bass.ds(r_h, H)]),
              reads=[mixfull_b], writes=[mixmine_b], dma=True)
            hbs = [sb(f"p3a_h{i}", [128, 8, 512], F32, ps_) for i in range(2)]; hbb = [Buf(), Buf()]
            mxs = [sb(f"p3a_m{i}", [128, 8, 512], BF16, ps_) for i in range(2)]; mxb = [Buf(), Buf()]
            sq = sb("p3a_sq", [128, 8, 512], BF16, ps_); sqb = Buf()
            lnv = sb("p3a_lnv", [128, 512], F32, ps_); lnvb = Buf()
            rstd = sb("p3a_rstd", [128, 512], F32, ps_); rstdb = Buf()
            xns = [sb(f"p3a_xn{i}", [128, 8, 512], BF16, ps_) for i in range(1)] * 2; xnb = [Buf()] * 2

            def ld(blk):
                A("sp", lambda blk=blk: nc.sync.dma_start(out=hbs[blk % 2][:], in_=hview(h_src, blk * 512, 512)),
                  reads=[h_src_b[blk]], writes=[hbb[blk % 2]], dma=True)
                A("sp", lambda blk=blk: nc.sync.dma_start(out=mxs[blk % 2][:], in_=hview(mix_mine.ap(), blk * 512, 512)),
                  reads=[mixmine_b], writes=[mxb[blk % 2]], dma=True)
            ld(0)
            for blk in range(NBh):
                if blk + 1 < NBh:
                    ld(blk + 1)
                hb, hb_b, mx, mx_b = hbs[blk % 2], hbb[blk % 2], mxs[blk % 2], mxb[blk % 2]
                for oc in range(8):
                    pi = next_ps()
                    for c in range(8):
                        A("pe", lambda c=c, oc=oc, pi=pi: nc.tensor.matmul(psum[pi][:], w_o_sb[:, c, oc * 128:(oc + 1) * 128], mx[:, c, :],
                                                                         start=(c == 0), stop=(c == 7)),
                          reads=[*wb, mx_b], writes=[psb[pi]], sig=(c == 7))
                    A("dve", lambda oc=oc, pi=pi: nc.vector.tensor_tensor(out=hb[:, oc, :], in0=psum[pi][:], in1=hb[:, oc, :], op=ALU.add),
                      reads=[psb[pi], hb_b], writes=[hb_b])
                A("sp", lambda blk=blk, hb=hb: nc.sync.dma_start(out=hview(hA, blk * 512, 512), in_=hb[:]),
                  reads=[hb_b], writes=[hA_b[blk]], dma=True)
                xn, xn_b = xns[blk % 2], xnb[blk % 2]
                norm_block(hb, hb_b, 512, lambda c: gvec32[:, (l * 3 + 1) * 8 + c:(l * 3 + 1) * 8 + c + 1],
                           sq, sqb, lnv, lnvb, rstd, rstdb, xn, xn_b)
                A("sp", lambda blk=blk, xn=xn: nc.sync.dma_start(out=hview(xn2_scr, blk * 512, 512), in_=xn[:]),
                  reads=[xn_b], writes=[xn2_b[blk]], dma=True)
                per = (len(up_pieces) + NBh - 1) // NBh
                for pc in up_pieces[blk * per:(blk + 1) * per]:
                    pc()
                if blk == NBh - 1:
                    A("sp", lambda xn=xn: nc.sync.dma_start(out=tail_loc.ap().rearrange("(c p) t -> p c t", p=128), in_=xn[:, :, 510:512]),
                      reads=[xn_b], writes=[tail_b], dma=True)
        cc_n[0] += 1
        A("pool", lambda: nc.gpsimd.collective_compute(
            "AllGather", ALU.bypass, replica_groups=[[0, 1], [2, 3], [4, 5], [6, 7]],
            ins=[tail_loc.ap().opt()], outs=[tail_full.ap().opt()]),
          reads=[tail_b], writes=[tailfull_b], dma=True, cc=(cc_sem, cc_n[0]))

        P.barrier()
        with ExitStack() as ps_:
            w_dn_sb = sb("w_dn_sb", [128, NFC, 1024], BF16, ps_); wdb = [Buf(), Buf()]
            load_weight(w_dn_sb, wdb, w_down[l], NFC, 1024, stg_f, stgb_f, 1024)
            NT = 256
            xbs = [sb(f"p3b_x{i}", [128, 8, NT + 2], BF16, ps_) for i in range(2)]; xbb = [Buf(), Buf()]
            hbs = [sb(f"p3b_h{i}", [128, 8, NT], F32, ps_) for i in range(2)]; hbb = [Buf(), Buf()]
            act_t = sb("p3b_act", [128, NFC, NT], BF16, ps_); actb = [Buf() for _ in range(NFC)]
            yg = [sb(f"p3b_yg{i}", [128, NT], F32, ps_) for i in range(2)]; ygb = [Buf(), Buf()]
            yv = [sb(f"p3b_yv{i}", [128, NT], F32, ps_) for i in range(2)]; yvb = [Buf(), Buf()]
            gg = [sb(f"p3b_gg{i}", [128, NT], F32, ps_) for i in range(2)]; ggb = [Buf(), Buf()]
            nblk = H // NT
            tl = sb("p3b_tl", [128, 8, 2], BF16, ps_); tlb = Buf()

            def ld(b):
                t0 = b * NT
                x_, xb_ = xbs[b % 2], xbb[b % 2]
                rb = [xn2_b[t0 // 512]] + ([xn2_b[(t0 - 2) // 512]] if t0 > 0 else [])
                if b == 0:
                    A("sp", lambda: nc.sync.dma_start(out=tl[:], in_=tail_full.ap()[0:D, :].rearrange("(c p) t -> p c t", p=128)),
                      reads=[tailfull_b], writes=[tlb], dma=True)
                    A("dve", lambda x_=x_: nc.vector.tensor_scalar(out=x_[:, :, 0:2], in0=tl[:], scalar1=hmask[:, 0:1], scalar2=None,
                                                                   op0=ALU.mult), reads=[tlb, b_const], writes=[xb_])
                    A("sp", lambda x_=x_: nc.sync.dma_start(out=x_[:, :, 2:NT + 2], in_=hview(xn2_scr, 0, NT)),
                      reads=rb, writes=[xb_], dma=True)
                else:
                    A("sp", lambda x_=x_, t0=t0: nc.sync.dma_start(out=x_[:], in_=hview(xn2_scr, t0 - 2, NT + 2)),
                      reads=rb, writes=[xb_], dma=True)
                A("sp", lambda b=b, t0=t0: nc.sync.dma_start(out=hbs[b % 2][:], in_=hview(hA, t0, NT)),
                  reads=[hA_b[t0 // 512]], writes=[hbb[b % 2]], dma=True)
            ld(0)
            cwo = lambda k, j: (l * 3 + k) * 44 + j
            for b in range(nblk):
                if b + 1 < nblk:
                    ld(b + 1)
                x_, xb_, hb, hb_b = xbs[b % 2], xbb[b % 2], hbs[b % 2], hbb[b % 2]
                for i in range(NFC):
                    pg, pv = next_ps(), next_ps()
                    for (pi, cbase) in ((pg, 128 * i), (pv, DFF + 128 * i)):
                        for c in range(8):
                            A("pe", lambda c=c, pi=pi, cbase=cbase: nc.tensor.matmul(
                                psum[pi][:, :NT + 2], w_up_sb[:, c, cbase:cbase + 128], x_[:, c, :], start=(c == 0), stop=(c == 7)),
                                reads=[*wub, xb_], writes=[psb[pi]], sig=(c == 7))
                    s = i % 2
                    for (pi, y, yb, j) in ((pg, yg[s], ygb[s], i), (pv, yv[s], yvb[s], NFC + i)):
                        A("act", lambda pi=pi, y=y, j=j: nc.scalar.activation(
                            out=y[:], in_=psum[pi][:, 2:NT + 2], func=AF.Identity,
                            bias=cb[:, l * 44 + j:l * 44 + j + 1], scale=cw[:, cwo(2, j):cwo(2, j) + 1]),
                            reads=[psb[pi], b_const], writes=[yb])
                        A("dve", lambda pi=pi, y=y, j=j: nc.vector.scalar_tensor_tensor(
                            out=y[:], in0=psum[pi][:, 1:NT + 1], scalar=cw[:, cwo(1, j):cwo(1, j) + 1], in1=y[:],
                            op0=ALU.mult, op1=ALU.add), reads=[psb[pi], yb, b_const], writes=[yb])
                        A("dve", lambda pi=pi, y=y, j=j: nc.vector.scalar_tensor_tensor(
                            out=y[:], in0=psum[pi][:, 0:NT], scalar=cw[:, cwo(0, j):cwo(0, j) + 1], in1=y[:],
                            op0=ALU.mult, op1=ALU.add), reads=[psb[pi], yb, b_const], writes=[yb])
                    A("act", lambda s=s: nc.scalar.activation(out=gg[s][:], in_=yg[s][:], func=AF.Gelu_apprx_tanh),
                      reads=[ygb[s]], writes=[ggb[s]])
                    A("pool", lambda s=s, i=i: nc.gpsimd.tensor_tensor(out=act_t[:, i, :], in0=gg[s][:], in1=yv[s][:], op=ALU.mult),
                      reads=[ggb[s], yvb[s]], writes=[actb[i]])
                for oc in range(8):
                    pi = next_ps()
                    for i in range(NFC):
                        A("pe", lambda i=i, oc=oc, pi=pi: nc.tensor.matmul(psum[pi][:, :NT], w_dn_sb[:, i, oc * 128:(oc + 1) * 128], act_t[:, i, :],
                                                                         start=(i == 0), stop=(i == NFC - 1)),
                          reads=[*wdb, actb[i]], writes=[psb[pi]], sig=(i == NFC - 1))
                    A("dve", lambda oc=oc, pi=pi: nc.vector.tensor_tensor(out=hb[:, oc, :], in0=psum[pi][:, :NT], in1=hb[:, oc, :], op=ALU.add),
                      reads=[psb[pi], hb_b], writes=[hb_b])
                A("sp", lambda b=b, hb=hb: nc.sync.dma_start(out=hview(hB, b * NT, NT), in_=hb[:]),
                  reads=[hb_b], writes=[hB_b[(b * NT) // 512]], dma=True)

        P.barrier()
        ffn_scope.close()
        with ExitStack() as ps_:
            w_pg_sb = sb("w_pg_sb", [128, 8, 1024], BF16, ps_); wgb = [Buf(), Buf()]
            w_pp_sb = sb("w_pp_sb", [128, 2, 1024], BF16, ps_); wpb = [Buf(), Buf()]
            stg = [sb(f"stg3c_{i}", [128, 1024], F32, ps_) for i in range(2)]; stgb = [Buf(), Buf()]
            load_weight(w_pg_sb, wgb, w_pg[l], 8, 1024, stg, stgb, 1024)
            load_weight(w_pp_sb, wpb, w_pp[l], 2, 1024, stg, stgb, 1024)
            hbs = [sb(f"p3c_h{i}", [128, 8, 512], F32, ps_) for i in range(2)]; hbb = [Buf(), Buf()]
            pfs = [sb(f"p3c_pf{i}", [128, 2, 512], F32, ps_) for i in range(2)]; pfb = [Buf(), Buf()]
            pbf = sb("p3c_pb", [128, 2, 512], BF16, ps_); pbb = Buf()
            sq = sb("p3c_sq", [128, 8, 512], BF16, ps_); sqb = Buf()
            lnv = sb("p3c_lnv", [128, 512], F32, ps_); lnvb = Buf()
            rstd = sb("p3c_rstd", [128, 512], F32, ps_); rstdb = Buf()
            xn = sb("p3c_xn", [128, 8, 512], BF16, ps_); xnb = Buf()
            sg = [sb(f"p3c_sg{i}", [128, 512], F32, ps_) for i in range(2)]; sgb = [Buf(), Buf()]
            outs = [sb(f"p3c_o{i}", [128, 8, 512], F32, ps_) for i in range(2)] if last else None
            outb = [Buf(), Buf()]
            dst, dst_b = (hC, hC_b)

            def ld(blk):
                A("sp", lambda blk=blk: nc.sync.dma_start(out=hbs[blk % 2][:], in_=hview(hB, blk * 512, 512)),
                  reads=[hB_b[blk]], writes=[hbb[blk % 2]], dma=True)
                A("sp", lambda blk=blk: nc.sync.dma_start(out=pfs[blk % 2][:], in_=pT[l][:, blk * 512:(blk + 1) * 512].rearrange("(c p) t -> p c t", p=128)),
                  writes=[pfb[blk % 2]], dma=True)
            ld(0)
            for blk in range(NBh):
                if blk + 1 < NBh:
                    ld(blk + 1)
                hb, hb_b = hbs[blk % 2], hbb[blk % 2]
                norm_block(hb, hb_b, 512, lambda c: gvec32[:, (l * 3 + 2) * 8 + c:(l * 3 + 2) * 8 + c + 1],
                           sq, sqb, lnv, lnvb, rstd, rstdb, xn, xnb)
                A("pool", lambda blk=blk: nc.gpsimd.tensor_copy(out=pbf[:], in_=pfs[blk % 2][:]), reads=[pfb[blk % 2]], writes=[pbb])
                for oc in range(8):
                    pg, pp = next_ps(), next_ps()
                    for c in range(8):
                        A("pe", lambda c=c, oc=oc, pg=pg: nc.tensor.matmul(psum[pg][:], w_pg_sb[:, c, oc * 128:(oc + 1) * 128], xn[:, c, :],
                                                                         start=(c == 0), stop=(c == 7)),
                          reads=[*wgb, xnb], writes=[psb[pg]], sig=(c == 7))
                    for c in range(2):
                        A("pe", lambda c=c, oc=oc, pp=pp: nc.tensor.matmul(psum[pp][:], w_pp_sb[:, c, oc * 128:(oc + 1) * 128], pbf[:, c, :],
                                                                         start=(c == 0), stop=(c == 1)),
                          reads=[*wpb, pbb], writes=[psb[pp]], sig=(c == 1))
                    s = oc % 2
                    A("act", lambda s=s, pg=pg: nc.scalar.activation(out=sg[s][:], in_=psum[pg][:], func=AF.Sigmoid),
                      reads=[psb[pg]], writes=[sgb[s]])
                    A("dve", lambda s=s, pp=pp: nc.vector.tensor_tensor(out=sg[s][:], in0=psum[pp][:], in1=sg[s][:], op=ALU.mult),
                      reads=[psb[pp], sgb[s]], writes=[sgb[s]])
                    A("dve", lambda s=s, oc=oc: nc.vector.tensor_tensor(out=hb[:, oc, :], in0=sg[s][:], in1=hb[:, oc, :], op=ALU.add),
                      reads=[sgb[s], hb_b], writes=[hb_b])
                if not last:
                    A("sp", lambda blk=blk, hb=hb: nc.sync.dma_start(out=hview(dst, blk * 512, 512), in_=hb[:]),
                      reads=[hb_b], writes=[dst_b[blk]], dma=True)
                else:
                    o_, ob_ = outs[blk % 2], outb[blk % 2]
                    norm_block(hb, hb_b, 512, lambda c: gfin32[:, c:c + 1], sq, sqb, lnv, lnvb, rstd, rstdb, None, None,
                               out_f32=o_, outb=ob_)
                    A("sp", lambda blk=blk, o_=o_: nc.sync.dma_start(out=hview(outT, blk * 512, 512), in_=o_[:]),
                      reads=[ob_], writes=[dst_b[blk]], dma=True)
        h_src, h_src_b = hC, hC_b

    P.barrier()
    A("pool", None, reads=hC_b)
    A("sp", None, reads=hC_b)
    n, nw = len(P.meta), (P.n_wait, getattr(P, 'n_standalone', 0), dict(P.cnt))
    es.close()
    return nc, (n, nw)


def _rel_bucket_np(dist):
    max_exact = 16
    d = np.maximum(dist, 1).astype(np.float32)
    large = max_exact + (np.log(d / np.float32(max_exact)) / np.float32(math.log(128 / max_exact))
                         * np.float32(32 - max_exact)).astype(np.int32)
    large = np.minimum(large, 31)
    return np.where(dist < max_exact, dist, large)


def host_prep(T, L, x_b, p_b, w, rank=0):
    f = np.float32
    m = {}
    m["xT"] = np.ascontiguousarray(x_b.T)
    m["pT"] = np.ascontiguousarray(np.transpose(p_b, (0, 2, 1)))
    for k_src, k_dst in (("w_up", "w_up"), ("w_down", "w_down"),
                         ("w_ple_gate", "w_pg"), ("w_ple_proj", "w_pp")):
        m[k_dst] = np.ascontiguousarray(w[k_src][:L])
    r = rank
    dh = [2 * r, 2 * r + 1]
    cols = []
    for g in range(2):
        sbp = [2 * g, 2 * g + 1]
        dhg = [2 * g, 2 * g + 1]
        for u in sbp:
            cols += list(range(128 * u, 128 * u + 128))
        for u in sbp:
            cols += list(range(512 + 128 * u, 512 + 128 * u + 128))
        for d_ in dhg:
            cols += list(range(1536 + 128 * d_, 1536 + 128 * d_ + 128))
        for d_ in dhg:
            cols += list(range(2048 + 128 * d_, 2048 + 128 * d_ + 128))
    for g in range(2):
        for u in (2 * g, 2 * g + 1):
            cols += list(range(1024 + 128 * u, 1024 + 128 * u + 128))
        for d_ in (2 * g, 2 * g + 1):
            cols += list(range(2560 + 128 * d_, 2560 + 128 * d_ + 128))
    m["w_in"] = np.ascontiguousarray(w["w_in"][:L][:, :, cols])
    m["hmask"] = np.full((128, 1), float(rank), np.float32)
    def orig_row(rr, rho):
        return 256 * rr + rho if rho < 256 else 512 + 256 * rr + (rho - 256)
    rows = [orig_row(rr, 64 * k + i) for k in range(8) for rr in range(2) for i in range(64)]
    m["w_o"] = np.ascontiguousarray(w["w_o"][:L][:, rows, :])
    g = np.stack([w["g_attn"][:L], w["g_ffn"][:L], w["g_ple"][:L]], axis=1)
    m["gvec"] = np.ascontiguousarray(g.reshape(L, 3, 8, 128).transpose(3, 0, 1, 2).reshape(128, L * 3 * 8))
    m["gfin"] = np.ascontiguousarray(w["g_final"].reshape(8, 128).T)
    m["convw"] = np.ascontiguousarray(w["conv_w"][:L].reshape(L, 3, 44, 128).transpose(3, 0, 1, 2).reshape(128, L * 3 * 44))
    m["convb"] = np.ascontiguousarray(w["conv_b"][:L].reshape(L, 44, 128).transpose(2, 0, 1).reshape(128, L * 44))
    m["gsub"] = np.ascontiguousarray(w["g_subln"][:L].T)
    lam = np.stack([w["lambda_q1"][:L], w["lambda_k1"][:L], w["lambda_q2"][:L], w["lambda_k2"][:L]], axis=1)
    m["lamv"] = np.ascontiguousarray(np.broadcast_to(lam.reshape(1, L * 4 * 64), (128, L * 4 * 64)))
    kl = np.arange(128)[:, None]
    xx = np.arange(1024)[None, :]
    dd = xx - 384 - kl
    idx = _rel_bucket_np(np.maximum(dd, 0))
    maps = [2 * d_ + j for d_ in dh for j in range(2)]
    bt = np.transpose(w["rel_bias"][:, maps][idx], (2, 0, 1)).astype(f)
    bt = np.where((dd < 0)[None], f(NEG), bt)
    m["btoe"] = np.ascontiguousarray(bt)
    negw = np.zeros((2, 128, 1024), f)
    xq = xx - 512
    negw[0] = np.where(xq <= kl, NEG, 0.0)
    negw[1] = np.where(xq < kl, NEG, 0.0)
    m["negw"] = negw
    cst = np.zeros((3, 128, 128), f)
    cst[0] = np.eye(128, dtype=f)
    jj = np.arange(128)[:, None]; ss = np.arange(128)[None, :]
    cst[1] = np.where(jj >= ss, -1.0, 0.0)
    cst[2] = 1.0
    m["cst"] = cst
    return {k: np.ascontiguousarray(v, dtype=f) for k, v in m.items()}


_CACHE = {}


def kernel(**inputs):
    x = np.asarray(inputs["x"], np.float32)
    p = np.asarray(inputs["p"], np.float32)
    w = {k: np.asarray(v, np.float32) for k, v in inputs.items() if k not in ("x", "p")}
    B, T, _ = x.shape
    L = p.shape[0]
    H = T // 2
    key = (T, L)
    if key not in _CACHE:
        _CACHE[key] = build_program(T, L)[0]
    nc = _CACHE[key]
    n_cores = 8
    maps = []
    per = {}
    for core in range(n_cores):
        b = (core * B) // n_cores
        r = core % 2
        maps.append(host_prep(T, L, x[b, r * H:(r + 1) * H], p[:, b, r * H:(r + 1) * H], w, rank=r))
    res = run_bass_kernel_spmd(nc, maps, core_ids=list(range(n_cores)))
    out = np.empty((B, T, D), np.float32)
    for core in range(n_cores):
        b, r = (core * B) // n_cores, core % 2
        out[b, r * H:(r + 1) * H] = res.results[core]["outT"].T
    return out
```

```python
import math
from contextlib import ExitStack
import numpy as np
import concourse.bass as bass
import concourse.mybir as mybir
from concourse.bass_utils import run_bass_kernel_spmd

F32 = mybir.dt.float32
BF16 = mybir.dt.bfloat16
AF = mybir.ActivationFunctionType
ALU = mybir.AluOpType
AX = mybir.AxisListType

D = 1024
DFF = 2816
NFC = 22
PLE = 256
EPS = 1e-6
NEG = -30000.0
SEM_LIM = 30000
N_DMA_SEM = 8


class Buf:
    __slots__ = ("w", "wd", "r", "rd")

    def __init__(self):
        self.w = None
        self.wd = []
        self.r = {}
        self.rd = []


class Prog:
    def __init__(self, nc, es):
        self.nc = nc
        self.es = es
        self.eng = {"pe": nc.tensor, "act": nc.scalar, "dve": nc.vector,
                    "pool": nc.gpsimd, "sp": nc.sync}
        self.meta = []
        self.cnt = {e: 0 for e in self.eng}
        self.sems = {e: [] for e in self.eng}
        self.dsems = {e: [es.enter_context(nc.semaphore(f"d_{e}_{j}")) for j in range(N_DMA_SEM)]
                      for e in ("sp", "pool")}
        self.dcount = {"sp": 0, "pool": 0}
        self.waited = {e: {p: 0 for p in self.eng} for e in self.eng}
        self.dwaited = {e: {} for e in self.eng}
        self.n_wait = 0

    def _sem(self, eng, g):
        j, v = (g - 1) // SEM_LIM, (g - 1) % SEM_LIM + 1
        lst = self.sems[eng]
        while len(lst) <= j:
            lst.append(self.es.enter_context(self.nc.semaphore(f"s_{eng}_{len(lst)}")))
        return lst[j], v

    def barrier(self):
        for eng, E in self.eng.items():
            for p in self.eng:
                g = self.cnt[p]
                if g == 0 or self.waited[eng][p] >= g:
                    continue
                self.waited[eng][p] = g
                s, v = self._sem(p, g)
                E.wait_ge(s, v)
                self.n_wait += 1
            for q in ("sp", "pool"):
                k = self.dcount[q]
                for j in range(min(k, N_DMA_SEM)):
                    last_k = ((k - 1 - j) // N_DMA_SEM) * N_DMA_SEM + j
                    v = 16 * (last_k // N_DMA_SEM + 1)
                    s = self.dsems[q][j]
                    if self.dwaited[eng].get(id(s), 0) < v:
                        self.dwaited[eng][id(s)] = v
                        E.wait_ge(s, v)
                        self.n_wait += 1

    def add(self, eng, fn, reads=(), writes=(), dma=False, sig=True, cc=None):
        i = len(self.meta)
        deps = set()
        for b in reads:
            if b.w is not None:
                deps.add(b.w)
            deps.update(b.wd)
        for b in writes:
            if b.w is not None:
                deps.add(b.w)
            if not dma or b.r or b.rd:
                deps.update(b.wd)
            deps.update(b.r.values())
            deps.update(b.rd)
        for b in writes:
            if dma:
                if b.r or b.rd:
                    b.wd = []
                b.wd.append(i)
            else:
                b.w = i
                b.wd = []
            b.r = {}
            b.rd = []
        for b in reads:
            if dma:
                b.rd.append(i)
            else:
                b.r[eng] = i
        deps.discard(i)
        E = self.eng[eng]
        need = {}
        waits = []
        for d in deps:
            deng, ddma, info = self.meta[d]
            if ddma:
                s, v = info
                key = id(s)
                if self.dwaited[eng].get(key, 0) < v:
                    self.dwaited[eng][key] = v
                    waits.append((s, v))
                continue
            if deng == eng and eng == "pe" and not dma:
                continue
            if info < 0:
                g = -info
                assert self.cnt[deng] >= g, "dependency on an unsignalled op whose group is not closed"
                info = g
            if info > need.get(deng, 0):
                need[deng] = info
        for deng, g in need.items():
            if self.waited[eng][deng] >= g:
                continue
            self.waited[eng][deng] = g
            waits.append(self._sem(deng, g))
        if dma and cc is None:
            k = self.dcount[eng]
            self.dcount[eng] += 1
            s = self.dsems[eng][k % N_DMA_SEM]
            v = 16 * (k // N_DMA_SEM + 1)
            if k >= N_DMA_SEM:
                key = id(s)
                if self.dwaited[eng].get(key, 0) < v - 16:
                    self.dwaited[eng][key] = v - 16
                    waits.append((s, v - 16))
        self.n_wait += len(waits)
        if cc is not None:
            for (ws, wv) in waits:
                E.wait_ge(ws, wv)
            fn().then_inc(cc[0], 1)
            self.meta.append((eng, True, cc))
            return i
        if fn is None:
            for (ws, wv) in waits:
                E.wait_ge(ws, wv)
            self.meta.append((eng, False, self.cnt[eng]))
            return i
        for (ws, wv) in waits[:-1]:
            E.wait_ge(ws, wv)
            self.n_standalone = getattr(self, 'n_standalone', 0) + 1
        inst = fn()
        if waits:
            inst._wait_ge(*waits[-1])
        if dma:
            inst.then_inc(s, 16)
            self.meta.append((eng, True, (s, v)))
        elif sig:
            self.cnt[eng] += 1
            g = self.cnt[eng]
            s, v = self._sem(eng, g)
            inst.then_inc(s, 1)
            self.meta.append((eng, False, g))
        else:
            self.meta.append((eng, False, -(self.cnt[eng] + 1)))
        return i


def build_program(T, L, taps=False):
    nc = bass.Bass("TRN2", target_bir_lowering=False)
    es = ExitStack()
    P = Prog(nc, es)
    NB = T // 512
    NKB = T // 128
    H = T // 2
    NBh = H // 512
    rank = nc.sync.snap(nc.sync.partition_id() % 2, min_val=0, max_val=1)
    r_qk = rank * 2048
    r_v = rank * (2 * H)
    r_h = rank * H
    VP = min(2048, H)
    npv = H // VP

    def din(name, shape, dt=F32):
        return nc.dram_tensor(name, list(shape), dt, kind="ExternalInput").ap()

    def dscr(name, shape, dt):
        kind = "ExternalOutput" if taps else "Internal"
        return nc.dram_tensor(name, list(shape), dt, kind=kind).ap()

    xT = din("xT", [D, H])
    pT = din("pT", [L, PLE, H])
    w_in = din("w_in", [L, D, 3072])
    w_o = din("w_o", [L, D, D])
    w_up = din("w_up", [L, D, 2 * DFF])
    w_down = din("w_down", [L, DFF, D])
    w_pg = din("w_pg", [L, D, D])
    w_pp = din("w_pp", [L, PLE, D])
    gvec_d = din("gvec", [128, L * 3 * 8])
    gfin_d = din("gfin", [128, 8])
    cw_d = din("convw", [128, L * 3 * 44])
    cb_d = din("convb", [128, L * 44])
    gsub_d = din("gsub", [128, L])
    lamv_d = din("lamv", [128, L * 4 * 64])
    btoe_d = din("btoe", [4, 128, 1024])
    negw_d = din("negw", [2, 128, 1024])
    cst_d = din("cst", [3, 128, 128])
    hmask_d = din("hmask", [128, 1])
    outT = nc.dram_tensor("outT", [D, H], F32, kind="ExternalOutput").ap()

    qk_loc = nc.dram_tensor("qk_loc", [NBh * 2048, 512], BF16)
    v_loc = nc.dram_tensor("v_loc", [NBh * 1024, 512], BF16)
    qk_scr, v_scr = qk_loc.ap(), v_loc.ap()
    qk_full = nc.dram_tensor("qk_full", [NBh * 4096, 512], BF16)
    v_full = nc.dram_tensor("v_full", [NBh * 2048, 512], BF16)
    halves = [(0, NBh // 2), (NBh // 2, NBh)] if NBh >= 2 else [(0, NBh)]
    qkfull_b, vfull_b = [Buf() for _ in halves], [Buf() for _ in halves]

    def half_of(blk):
        return 0 if (len(halves) == 1 or blk < NBh // 2) else 1
    qk_mine = nc.dram_tensor("qk_mine", [NBh * 2048, 512], BF16)
    v_mine = nc.dram_tensor("v_mine", [NBh * 1024, 512], BF16)
    mix_mine = nc.dram_tensor("mix_mine", [D, H], BF16)
    qkmine_b, vmine_b, mixmine_b = Buf(), Buf(), Buf()
    tail_loc = nc.dram_tensor("tail_loc", [D, 2], BF16)
    tail_full = nc.dram_tensor("tail_full", [2 * D, 2], BF16)
    tail_b, tailfull_b = Buf(), Buf()
    mix_loc = nc.dram_tensor("mix_loc", [512, T], BF16)
    mix_full = nc.dram_tensor("mix_full", [D, T], BF16)
    mix_scr = mix_loc.ap()
    mixfull_b = Buf()
    cc_sem = es.enter_context(nc.semaphore("cc_sem"))
    cc_n = [0]
    xn2_scr = dscr("xn2_scr", [D, H], BF16)
    hA = dscr("hA", [D, H], F32)
    hB = dscr("hB", [D, H], F32)
    hC = dscr("hC", [D, H], F32)

    def blkbufs(n=NBh):
        return [Buf() for _ in range(n)]
    qk_b, v_b, xn2_b, hA_b, hB_b, hC_b = (blkbufs() for _ in range(6))
    mix_bu = [blkbufs(NB) for _ in range(4)]

    uid = [0]

    def sb(name, shape, dt, stack=None):
        uid[0] += 1
        return (stack or es).enter_context(nc.sbuf_tensor(f"t{uid[0]}_{name}", list(shape), dt))

    pbig = [es.enter_context(nc.psum_tensor(f"psw{i}", [128, 1024], F32)) for i in range(4)]
    psum = [pbig[i // 2][:, (i % 2) * 512:(i % 2) * 512 + 512] for i in range(8)]
    psb = [Buf() for _ in range(8)]
    ident = sb("ident", [128, 128], BF16)
    ntri = sb("ntri", [128, 128], BF16)
    ones = sb("ones", [128, 128], BF16)
    gvec = sb("gvec_s", [128, L * 3 * 8], F32)
    gvec32 = sb("gvec32", [128, L * 3 * 8], F32)
    gfin = sb("gfin_s", [128, 8], F32)
    gfin32 = sb("gfin32", [128, 8], F32)
    cw = sb("cw_s", [128, L * 3 * 44], F32)
    cb = sb("cb_s", [128, L * 44], F32)
    gsub = sb("gsub_s", [128, L], F32)
    gsub2 = sb("gsub2", [128, L], F32)
    lamv = sb("lamv_s", [128, L * 4 * 64], F32)
    lamt = sb("lamt", [128, 64], F32)
    lams = sb("lams", [128, 2 * L], F32)
    neglam = sb("neglam", [128, L], F32)
    cstage = sb("cstage", [128, 3, 128], F32)
    hmask = sb("hmask_s", [128, 1], F32)
    b_const = Buf()

    def A(eng, fn, reads=(), writes=(), dma=False, sig=True, cc=None):
        return P.add(eng, fn, reads, writes, dma, sig, cc)

    A("sp", lambda: nc.sync.dma_start(out=cstage[:], in_=cst_d.rearrange("k p n -> p k n")),
      writes=[b_const], dma=True)
    for dst, src in ((gvec, gvec_d), (gfin, gfin_d), (cw, cw_d), (cb, cb_d), (gsub, gsub_d), (lamv, lamv_d), (hmask, hmask_d)):
        A("sp", lambda dst=dst, src=src: nc.sync.dma_start(out=dst[:], in_=src), writes=[b_const], dma=True)
    A("dve", lambda: nc.vector.tensor_copy(out=ident[:], in_=cstage[:, 0, :]), reads=[b_const], writes=[b_const])
    A("dve", lambda: nc.vector.tensor_copy(out=ntri[:], in_=cstage[:, 1, :]), reads=[b_const], writes=[b_const])
    A("dve", lambda: nc.vector.tensor_copy(out=ones[:], in_=cstage[:, 2, :]), reads=[b_const], writes=[b_const])
    A("dve", lambda: nc.vector.tensor_scalar(out=gvec32[:], in0=gvec[:], scalar1=32.0, scalar2=None, op0=ALU.mult),
      reads=[b_const], writes=[b_const])
    A("dve", lambda: nc.vector.tensor_scalar(out=gfin32[:], in0=gfin[:], scalar1=32.0, scalar2=None, op0=ALU.mult),
      reads=[b_const], writes=[b_const])
    for l in range(L):
        li = 0.8 - 0.6 * math.exp(-0.3 * l)
        for j in range(2):
            o = (l * 4 + 2 * j) * 64
            A("dve", lambda o=o: nc.vector.tensor_tensor(out=lamt[:], in0=lamv[:, o:o + 64], in1=lamv[:, o + 64:o + 128],
                                                         op=ALU.mult), reads=[b_const], writes=[b_const])
            A("dve", lambda l=l, j=j: nc.vector.reduce_sum(out=lams[:, 2 * l + j:2 * l + j + 1], in_=lamt[:], axis=AX.X),
              reads=[b_const], writes=[b_const])
        A("act", lambda l=l: nc.scalar.activation(out=lams[:, 2 * l:2 * l + 2], in_=lams[:, 2 * l:2 * l + 2], func=AF.Exp),
          reads=[b_const], writes=[b_const])
        A("dve", lambda l=l, li=li: nc.vector.scalar_tensor_tensor(
            out=neglam[:, l:l + 1], in0=lams[:, 2 * l + 1:2 * l + 2], scalar=-li, in1=lams[:, 2 * l:2 * l + 1],
            op0=ALU.add, op1=ALU.subtract), reads=[b_const], writes=[b_const])
        A("dve", lambda l=l, li=li: nc.vector.tensor_scalar(
            out=gsub2[:, l:l + 1], in0=gsub[:, l:l + 1], scalar1=(1.0 - li) * math.sqrt(128.0), scalar2=None, op0=ALU.mult),
            reads=[b_const], writes=[b_const])

    rr = {"ps": 0}

    def next_ps():
        i = rr["ps"] % 8
        rr["ps"] += 1
        return i

    def load_weight(dst, dstbuf, src2d, nchunk, ncols, stg, stgb, colstep):
        k = 0
        for c in range(nchunk):
            for c0 in range(0, ncols, colstep):
                w = min(colstep, ncols - c0)
                s, sbuf_ = stg[k % 2], stgb[k % 2]
                A("sp", lambda s=s, c=c, c0=c0, w=w: nc.sync.dma_start(
                    out=s[:, :w], in_=src2d[c * 128:(c + 1) * 128, c0:c0 + w]), writes=[sbuf_], dma=True)
                if k % 2 == 0:
                    A("pool", lambda s=s, c=c, c0=c0, w=w: nc.gpsimd.tensor_copy(out=dst[:, c, c0:c0 + w], in_=s[:, :w]),
                      reads=[sbuf_], writes=[dstbuf[0]])
                else:
                    A("act", lambda s=s, c=c, c0=c0, w=w: nc.scalar.activation(out=dst[:, c, c0:c0 + w], in_=s[:, :w], func=AF.Copy),
                      reads=[sbuf_], writes=[dstbuf[1]])
                k += 1

    def weight_pieces(dst, dstbuf, src2d, nchunk, ncols, stg, stgb, colstep):
        out = []
        k = 0
        for c in range(nchunk):
            for c0 in range(0, ncols, colstep):
                w = min(colstep, ncols - c0)
                s_, sb_ = stg[k % 2], stgb[k % 2]

                def piece(s_=s_, sb_=sb_, c=c, c0=c0, w=w, k=k):
                    A("sp", lambda: nc.sync.dma_start(out=s_[:, :w], in_=src2d[c * 128:(c + 1) * 128, c0:c0 + w]),
                      writes=[sb_], dma=True)
                    if k % 2 == 0:
                        A("pool", lambda: nc.gpsimd.tensor_copy(out=dst[:, c, c0:c0 + w], in_=s_[:, :w]),
                          reads=[sb_], writes=[dstbuf[0]])
                    else:
                        A("act", lambda: nc.scalar.activation(out=dst[:, c, c0:c0 + w], in_=s_[:, :w], func=AF.Copy),
                          reads=[sb_], writes=[dstbuf[1]])
                out.append(piece)
                k += 1
        return out

    def norm_block(hb, hbb, n, gcol, sq, sqb, lnv, lnvb, rstd, rstdb, xn, xnb, xc0=0, out_f32=None, outb=None):
        A("act", lambda: nc.scalar.activation(out=sq[:, :, :n], in_=hb[:, :, :n], func=AF.Square),
          reads=[hbb], writes=[sqb])
        pi = next_ps()
        for c in range(8):
            A("pe", lambda c=c, pi=pi: nc.tensor.matmul(psum[pi][:, :n], ones[:], sq[:, c, :n], start=(c == 0), stop=(c == 7)),
              reads=[sqb, b_const], writes=[psb[pi]], sig=(c == 7))
        A("act", lambda pi=pi: nc.scalar.activation(out=lnv[:, :n], in_=psum[pi][:, :n], func=AF.Ln, bias=1024.0 * EPS, scale=1.0),
          reads=[psb[pi]], writes=[lnvb])
        A("act", lambda: nc.scalar.activation(out=rstd[:, :n], in_=lnv[:, :n], func=AF.Exp, scale=-0.5),
          reads=[lnvb], writes=[rstdb])
        for c in range(8):
            if out_f32 is None:
                A("dve", lambda c=c: nc.vector.scalar_tensor_tensor(
                    out=xn[:, c, xc0:xc0 + n], in0=hb[:, c, :n], scalar=gcol(c), in1=rstd[:, :n], op0=ALU.mult, op1=ALU.mult),
                    reads=[hbb, rstdb, b_const], writes=[xnb])
            else:
                A("dve", lambda c=c: nc.vector.scalar_tensor_tensor(
                    out=out_f32[:, c, :n], in0=hb[:, c, :n], scalar=gcol(c), in1=rstd[:, :n], op0=ALU.mult, op1=ALU.mult),
                    reads=[hbb, rstdb, b_const], writes=[outb])

    def hview(ap2d, t0, n):
        return ap2d[:, t0:t0 + n].rearrange("(c p) t -> p c t", p=128)

    h_src, h_src_b = xT, [Buf() for _ in range(NBh)]

    for l in range(L):
        last = (l == L - 1)
        def pick_half(hi, b0, b1):
            A("sp", lambda: nc.sync.dma_start(
                out=qk_mine.ap().rearrange("(b j o w) t -> b j o (w t)", j=2, o=1, w=1024)[b0:b1],
                in_=qk_full.ap().rearrange("(b j g w) t -> b j g (w t)", j=2, g=2, w=1024)[b0:b1, :, bass.ds(rank, 1), :]),
              reads=[qkfull_b[hi]], writes=[qkmine_b], dma=True)
            A("sp", lambda: nc.sync.dma_start(
                out=v_mine.ap().rearrange("(b j o w) c -> b j o (w c)", j=2, o=1, w=512)[b0:b1],
                in_=v_full.ap().rearrange("(b j g w) c -> b j g (w c)", j=2, g=2, w=512)[b0:b1, :, bass.ds(rank, 1), :]),
              reads=[vfull_b[hi]], writes=[vmine_b], dma=True)

        P.barrier()
        with ExitStack() as ps_:
            w_in_sb = sb("w_in_sb", [128, 8, 3072], BF16, ps_)
            wb = [Buf(), Buf()]
            stg = [sb(f"stg1_{i}", [128, 1536], F32, ps_) for i in range(2)]
            stgb = [Buf(), Buf()]
            load_weight(w_in_sb, wb, w_in[l], 8, 3072, stg, stgb, 1536)
            hbs = [sb(f"p1_h{i}", [128, 8, 512], F32, ps_) for i in range(2)]
            hbb = [Buf(), Buf()]
            sq = sb("p1_sq", [128, 8, 512], BF16, ps_); sqb = Buf()
            lnv = sb("p1_lnv", [128, 512], F32, ps_); lnvb = Buf()
            rstd = sb("p1_rstd", [128, 512], F32, ps_); rstdb = Buf()
            xn = sb("p1_xn", [128, 8, 512], BF16, ps_); xnb = Buf()
            qst = [sb(f"p1_qst{i}", [128, 4, 512], BF16, ps_) for i in range(2)]
            qstb = [Buf(), Buf()]
            vst = [sb(f"p1_vst{i}", [128, 1024], BF16, ps_) for i in range(2)]
            vstb = [Buf(), Buf()]
            qcols = [128 * j for j in range(16)]
            vcols = [2048, 2560]

            def ld(blk):
                A("sp", lambda blk=blk: nc.sync.dma_start(out=hbs[blk % 2][:], in_=hview(h_src, blk * 512, 512)),
                  reads=[h_src_b[blk]], writes=[hbb[blk % 2]], dma=True)
            ld(0)
            ev = 0
            for blk in range(NBh):
                if blk + 1 < NBh:
                    ld(blk + 1)
                hb, hb_b = hbs[blk % 2], hbb[blk % 2]
                norm_block(hb, hb_b, 512, lambda c: gvec32[:, (l * 3 + 0) * 8 + c:(l * 3 + 0) * 8 + c + 1],
                           sq, sqb, lnv, lnvb, rstd, rstdb, xn, xnb)
                t0 = blk * 512
                for j4 in range(4):
                    st, stb = qst[j4 % 2], qstb[j4 % 2]
                    for jj in range(4):
                        j = j4 * 4 + jj
                        col = qcols[j]
                        pi = next_ps()
                        for c in range(8):
                            A("pe", lambda c=c, pi=pi, col=col: nc.tensor.matmul(
                                psum[pi][:], w_in_sb[:, c, col:col + 128], xn[:, c, :], start=(c == 0), stop=(c == 7)),
                                reads=[*wb, xnb], writes=[psb[pi]], sig=(c == 7))
                        scale = 0.125 if (j % 8) in (0, 1, 4, 5) else 1.0
                        if ev % 2 == 0:
                            A("act", lambda pi=pi, st=st, jj=jj, scale=scale: nc.scalar.activation(
                                out=st[:, jj, :], in_=psum[pi][:], func=AF.Copy, scale=scale), reads=[psb[pi]], writes=[stb])
                        else:
                            A("dve", lambda pi=pi, st=st, jj=jj, scale=scale: nc.vector.tensor_scalar(
                                out=st[:, jj, :], in0=psum[pi][:], scalar1=scale, scalar2=None, op0=ALU.mult),
                                reads=[psb[pi]], writes=[stb])
                        ev += 1
                    A("sp", lambda st=st, j4=j4, t0=t0: nc.sync.dma_start(
                        out=qk_scr[blk * 2048 + j4 * 512:blk * 2048 + (j4 + 1) * 512, :].rearrange("(c p) t -> p c t", p=128), in_=st[:]),
                        reads=[stb], writes=[qk_b[blk]], dma=True)
                for s in range(4):
                    st, stb = vst[s % 2], vstb[s % 2]
                    for half in range(2):
                        pi = next_ps()
                        vc = vcols[half]
                        for c in range(8):
                            A("pe", lambda c=c, pi=pi, s=s, vc=vc: nc.tensor.matmul(
                                psum[pi][:], xn[:, c, s * 128:(s + 1) * 128], w_in_sb[:, c, vc:vc + 512],
                                start=(c == 0), stop=(c == 7)), reads=[*wb, xnb], writes=[psb[pi]], sig=(c == 7))
                        if ev % 2 == 0:
                            A("act", lambda pi=pi, st=st, half=half: nc.scalar.activation(
                                out=st[:, half * 512:(half + 1) * 512], in_=psum[pi][:], func=AF.Copy), reads=[psb[pi]], writes=[stb])
                        else:
                            A("dve", lambda pi=pi, st=st, half=half: nc.vector.tensor_copy(
                                out=st[:, half * 512:(half + 1) * 512], in_=psum[pi][:]), reads=[psb[pi]], writes=[stb])
                        ev += 1
                    for g in range(2):
                        r0 = blk * 1024 + g * 512 + s * 128
                        A("sp", lambda st=st, r0=r0, g=g: nc.sync.dma_start(
                            out=v_scr[r0:r0 + 128, :], in_=st[:, g * 512:(g + 1) * 512]),
                            reads=[stb], writes=[v_b[blk]], dma=True)
                cc_n[0] += 1
                A("pool", lambda blk=blk: nc.gpsimd.collective_compute(
                    "AllGather", ALU.bypass, replica_groups=[[0, 1], [2, 3], [4, 5], [6, 7]],
                    ins=[qk_loc.ap()[blk * 2048:(blk + 1) * 2048, :].opt()], outs=[qk_full.ap()[blk * 4096:(blk + 1) * 4096, :].opt()]),
                  reads=[qk_b[blk]], writes=[qkfull_b[half_of(blk)]], dma=True, cc=(cc_sem, cc_n[0]))
                cc_n[0] += 1
                A("pool", lambda blk=blk: nc.gpsimd.collective_compute(
                    "AllGather", ALU.bypass, replica_groups=[[0, 1], [2, 3], [4, 5], [6, 7]],
                    ins=[v_loc.ap()[blk * 1024:(blk + 1) * 1024, :].opt()], outs=[v_full.ap()[blk * 2048:(blk + 1) * 2048, :].opt()]),
                  reads=[v_b[blk]], writes=[vfull_b[half_of(blk)]], dma=True, cc=(cc_sem, cc_n[0]))
                for hi, (b0, b1) in enumerate(halves):
                    if blk == min(b1 + 1, NBh - 1):
                        pick_half(hi, b0, b1)

        P.barrier()
        with ExitStack() as ps_:
            btoe = sb("btoe", [128, 4, 1024], F32, ps_); btb = Buf()
            negw = sb("negw", [128, 2, 1024], BF16, ps_)
            negst = sb("negst", [128, 2, 1024], F32, ps_)
            A("sp", lambda: nc.sync.dma_start(out=btoe[:], in_=btoe_d.rearrange("m p x -> p m x")), writes=[btb], dma=True)
            A("sp", lambda: nc.sync.dma_start(out=negst[:], in_=negw_d.rearrange("m p x -> p m x")), writes=[btb], dma=True)
            A("dve", lambda: nc.vector.tensor_copy(out=negw[:], in_=negst[:]), reads=[btb], writes=[btb])
            c31 = sb("c31", [128, 4], F32, ps_)
            A("dve", lambda: nc.vector.tensor_copy(out=c31[:], in_=btoe[:, :, 1023]), reads=[btb], writes=[btb])
            for m_ in range(4):
                A("dve", lambda m_=m_: nc.vector.tensor_scalar(out=btoe[:, m_, :], in0=btoe[:, m_, :], scalar1=c31[:, m_:m_ + 1],
                                                               scalar2=None, op0=ALU.subtract), reads=[btb], writes=[btb])
            qt2 = [sb(f"qt2_{i}", [128, T], BF16, ps_) for i in range(2)]
            kt2 = [sb(f"kt2_{i}", [128, T], BF16, ps_) for i in range(2)]
            v2 = [sb(f"v2_{i}", [128, NKB, 128], BF16, ps_) for i in range(2)]
            pairb = [Buf(), Buf()]
            e_t = [sb(f"e_t{i}", [128, 512], F32, ps_) for i in range(2)]; e_b = [Buf(), Buf()]
            sp_t = [sb(f"sp_t{i}", [128, 512], BF16, ps_) for i in range(2)]; sp_b = [Buf(), Buf()]
            la_t = [sb(f"la_t{i}", [128, 512], F32, ps_) for i in range(2)]; la_b = [Buf(), Buf()]
            a_t = [sb(f"a_t{i}", [128, 512], BF16, ps_) for i in range(2)]; a_b = [Buf(), Buf()]
            tc_t = [sb(f"tc_t{i}", [128, 512], F32, ps_) for i in range(2)]; tc_b = [Buf(), Buf()]
            ost = [sb(f"ost{i}", [128, 512], BF16, ps_) for i in range(2)]; ost_b = [Buf(), Buf()]
            pdw = [sb(f"pdw{i}", [128, 1024], BF16, ps_) for i in range(2)]
            pd_t = [[pdw[i][:, 512 * c:512 * c + 512] for i in range(2)] for c in range(2)]
            pdw_b = [Buf(), Buf()]
            pd_b = [[pdw_b[0], pdw_b[1]], [pdw_b[0], pdw_b[1]]]
            rsum = [sb(f"rsum{c}", [128, 512], F32, ps_) for c in range(2)]; rsum_b = [Buf(), Buf()]
            rsbf = [sb(f"rsbf{c}", [128, 512], BF16, ps_) for c in range(2)]; rsbf_b = [Buf(), Buf()]
            rw_t = sb("rw_t", [128, 1024], F32, ps_)
            r_t = [rw_t[:, 0:512], rw_t[:, 512:1024]]; r_b = [Buf(), Buf()]
            o_t = sb("o_t", [128, 512], F32, ps_); o_b = Buf()
            sq2 = sb("sq2", [128, 512], BF16, ps_); sq2b = Buf()
            ln2 = sb("ln2", [128, 512], F32, ps_); ln2b = Buf()
            rs2 = sb("rs2", [128, 512], F32, ps_); rs2b = Buf()
            mixo = sb("mixo", [128, 512], BF16, ps_); mixob = Buf()
            ew_t = sb("ew_t", [128, 1024], F32, ps_); ew_b = Buf()
            spw_ts = [sb(f"spw_t{i}", [128, 1024], BF16, ps_) for i in range(2)]; spw_bs = [Buf(), Buf()]
            law_ts = [sb(f"law_t{i}", [128, 1024], F32, ps_) for i in range(2)]; law_bs = [Buf(), Buf()]
            aw_ts = [sb(f"aw_t{i}", [128, 1024], BF16, ps_) for i in range(2)]; aw_bs = [Buf(), Buf()]
            tcw_t = sb("tcw_t", [128, 1024], F32, ps_); tcw_b = Buf()
            ostw = sb("ostw", [128, 1024], BF16, ps_); ostw_b = Buf()
            zbig = [pbig[0], pbig[1]]
            sbig, obig = pbig[2], pbig[3]
            ZB, LB, SB_, OB = (0, 1), (2, 3), (4, 5), (6, 7)

            def load_pair(u, slot):
                if u < 2:
                    qc_, kc_, vcol = u, 2 + u, 128 * u
                else:
                    qc_, kc_, vcol = 4 + (u - 2), 6 + (u - 2), 256 + 128 * (u - 2)
                qk4 = qk_mine.ap().rearrange("(b j w) t -> b j w t", j=2, w=1024)
                for (dst, c_) in ((qt2[slot], qc_), (kt2[slot], kc_)):
                    for j in range(2):
                        A("sp", lambda dst=dst, c_=c_, j=j: nc.sync.dma_start(
                            out=dst[:, j * H:(j + 1) * H].rearrange("p (b t) -> p b t", t=512),
                            in_=qk4[:, j, c_ * 128:(c_ + 1) * 128, :].rearrange("b p t -> p b t")),
                            reads=[qkmine_b], writes=[pairb[slot]], dma=True)
                for j in range(2):
                    for b_ in range(NBh):
                        kb0 = (j * H + b_ * 512) // 128
                        r0 = b_ * 1024 + j * 512
                        A("sp", lambda kb0=kb0, r0=r0: nc.sync.dma_start(
                            out=v2[slot][:, kb0:kb0 + 4, :],
                            in_=v_mine.ap()[r0:r0 + 512, vcol:vcol + 128].rearrange("(kb p) d -> p kb d", p=128)),
                            reads=[vmine_b], writes=[pairb[slot]], dma=True)

            def mix_exchange(u):
                for k in (2 * u, 2 * u + 1):
                    cc_n[0] += 1
                    A("pool", lambda k=k: nc.gpsimd.collective_compute(
                        "AllGather", ALU.bypass, replica_groups=[[0, 1], [2, 3], [4, 5], [6, 7]],
                        ins=[mix_loc.ap()[64 * k:64 * k + 64, :].opt()], outs=[mix_full.ap()[128 * k:128 * k + 128, :].opt()]),
                      reads=mix_bu[u], writes=[mixfull_b], dma=True, cc=(cc_sem, cc_n[0]))

            pending = []
            load_pair(0, 0)
            for u in range(4):
                if u >= 1:
                    mix_exchange(u - 1)
                slot = u % 2
                if u + 1 < 4:
                    load_pair(u + 1, (u + 1) % 2)
                QT, KT, V2, pb = qt2[slot], kt2[slot], v2[slot], pairb[slot]
                is_sb = u < 2
                for qc in range(NB):
                    q0 = qc * 512
                    nkb = 4 * (qc + 1)
                    order = list(range(nkb - 1, -1, -1))

                    def qk(i, bank, ch, last_stop):
                        kb = order[i]
                        k0 = kb * 128
                        diag = k0 >= q0
                        lo = 64 * ch
                        A("pe", lambda: nc.tensor.matmul(psum[bank][:], KT[lo:lo + 64, k0:k0 + 128], QT[lo:lo + 64, q0:q0 + 512],
                                                         start=True, stop=(last_stop and not (diag and is_sb))),
                          reads=[pb], writes=[psb[bank]])
                        if diag and is_sb:
                            c0 = k0 - q0
                            A("pe", lambda: nc.tensor.matmul(psum[bank][:], ident[:], negw[:, 0, 512 - c0:1024 - c0],
                                                             start=False, stop=last_stop),
                              reads=[btb, b_const], writes=[psb[bank]])

                    if is_sb:
                        def zqk(i):
                            kb = order[i]
                            k0 = kb * 128
                            diag = k0 >= q0
                            zt, zk = zbig[i % 2], 2 * (i % 2)
                            for ch in range(2):
                                lo = 64 * ch
                                A("pe", lambda lo=lo, ch=ch: nc.tensor.matmul(
                                    zt[:, 512 * ch:512 * ch + 512], KT[lo:lo + 64, k0:k0 + 128], QT[lo:lo + 64, q0:q0 + 512],
                                    start=True, stop=not diag), reads=[pb], writes=[psb[zk + ch]], sig=(ch == 1 and not diag))
                            if diag:
                                c0 = k0 - q0
                                for ch in range(2):
                                    A("pe", lambda ch=ch: nc.tensor.matmul(
                                        zt[:, 512 * ch:512 * ch + 512], ident[:], negw[:, 0, 512 - c0:1024 - c0],
                                        start=False, stop=True), reads=[btb, b_const], writes=[psb[zk + ch]], sig=(ch == 1))

                        def c0_of(i):
                            k0_ = order[i] * 128
                            return (k0_ - q0) if k0_ > q0 else 0

                        def cols(t, c0):
                            return t[:] if c0 == 0 else t[:].rearrange("p (c t) -> p c t", c=2)[:, :, c0:512]

                        def zero_left(t, tb, c0):
                            if c0 > 0:
                                A("pool", lambda: nc.gpsimd.memset(t[:].rearrange("p (c t) -> p c t", c=2)[:, :, 0:c0], 0.0), writes=[tb])

                        def act_A(i):
                            c0 = c0_of(i)
                            zero_left(aw_ts[i % 2], aw_bs[i % 2], c0)
                            A("act", lambda: nc.scalar.activation(out=cols(aw_ts[i % 2], c0), in_=cols(law_ts[i % 2], c0), func=AF.Exp),
                              reads=[law_bs[i % 2]], writes=[aw_bs[i % 2]])

                        def pe_PV(i):
                            kbp = order[i]
                            for ch in range(2):
                                A("pe", lambda ch=ch: nc.tensor.matmul(
                                    obig[0:64, 512 * ch:512 * ch + 512], V2[:, kbp, 64 * ch:64 * ch + 64], aw_ts[i % 2][:, 512 * ch:512 * ch + 512],
                                    start=(i == 0), stop=(i == nkb - 1)), reads=[pb, aw_bs[i % 2]], writes=[psb[6 + ch]], sig=(ch == 1))

                        zqk(0)
                        for i in range(nkb):
                            zt, zk = zbig[i % 2], 2 * (i % 2)
                            zpair = [psb[zk], psb[zk + 1]]
                            c0i = c0_of(i)
                            A("act", lambda zt=zt, c0i=c0i: nc.scalar.activation(out=cols(ew_t, c0i), in_=cols(zt, c0i), func=AF.Exp),
                              reads=zpair, writes=[ew_b])
                            if i + 1 < nkb:
                                zqk(i + 1)
                            spw_t, spw_b = spw_ts[i % 2], spw_bs[i % 2]
                            law_t, law_b = law_ts[i % 2], law_bs[i % 2]
                            zero_left(spw_t, spw_b, c0i)
                            A("act", lambda spw_t=spw_t, c0i=c0i: nc.scalar.activation(out=cols(spw_t, c0i), in_=cols(ew_t, c0i), func=AF.Ln, bias=1.0, scale=1.0),
                              reads=[ew_b], writes=[spw_b])
                            for ch in range(2):
                                A("pe", lambda ch=ch, zt=zt, spw_t=spw_t: nc.tensor.matmul(
                                    zt[:, 512 * ch:512 * ch + 512], ntri[:], spw_t[:, 512 * ch:512 * ch + 512],
                                    start=False, stop=True, skip_group_check=True),
                                  reads=[spw_b, b_const], writes=[psb[zk + ch]], sig=(ch == 1))
                            if i + 1 < nkb:
                                for ch in range(2):
                                    A("pe", lambda ch=ch, spw_t=spw_t: nc.tensor.matmul(
                                        sbig[:, 512 * ch:512 * ch + 512], ones[:], spw_t[:, 512 * ch:512 * ch + 512],
                                        start=True, stop=True), reads=[spw_b, b_const], writes=[psb[4 + ch]], sig=(ch == 1))
                            if i >= 1:
                                act_A(i - 1)
                                pe_PV(i - 1)
                            if i == 0:
                                A("dve", lambda zt=zt, law_t=law_t: nc.vector.tensor_copy(out=law_t[:], in_=zt[:]),
                                  reads=zpair, writes=[law_b])
                            else:
                                A("dve", lambda zt=zt, law_t=law_t: nc.vector.tensor_tensor(out=law_t[:], in0=zt[:], in1=tcw_t[:], op=ALU.subtract),
                                  reads=zpair + [tcw_b], writes=[law_b])
                            if i + 1 < nkb:
                                if i == 0:
                                    A("dve", lambda: nc.vector.tensor_copy(out=tcw_t[:], in_=sbig[:]),
                                      reads=[psb[4], psb[5]], writes=[tcw_b])
                                else:
                                    A("dve", lambda: nc.vector.tensor_tensor(out=tcw_t[:], in0=sbig[:], in1=tcw_t[:], op=ALU.add),
                                      reads=[psb[4], psb[5], tcw_b], writes=[tcw_b])
                        act_A(nkb - 1)
                        pe_PV(nkb - 1)
                        A("dve", lambda: nc.vector.tensor_copy(out=ostw[0:64, :], in_=obig[0:64, :]),
                          reads=[psb[6], psb[7]], writes=[ostw_b])
                        A("sp", lambda: nc.sync.dma_start(
                            out=mix_scr[128 * u:128 * u + 128, q0:q0 + 512].rearrange("(c p) t -> p c t", p=64),
                            in_=ostw[0:64, :].rearrange("p (c t) -> p c t", c=2)),
                          reads=[ostw_b], writes=[mix_bu[u][qc]], dma=True)
                    else:
                        hd = u - 2
                        zb = [(0, 1), (2, 3)]
                        for ch in range(2):
                            qk(0, zb[0][ch], ch, True)
                        for i in range(nkb):
                            if i == 2 and pending:
                                pending.pop()()
                            kb = order[i]
                            k0 = kb * 128
                            near = (k0 >= q0 - 128)
                            zz = zb[i % 2]
                            if i + 1 < nkb:
                                for ch in range(2):
                                    qk(i + 1, zb[(i + 1) % 2][ch], ch, True)
                            if near:
                                for ch in range(2):
                                    m = 2 * hd + ch
                                    pt, ptb = pd_t[ch][i % 2], pd_b[ch][i % 2]
                                    x0 = q0 - k0 + 384
                                    A("dve", lambda ch=ch, m=m, x0=x0, zz=zz: nc.vector.tensor_tensor(
                                        out=la_t[ch][:], in0=psum[zz[ch]][:], in1=btoe[:, m, x0:x0 + 512], op=ALU.add),
                                        reads=[psb[zz[ch]], btb], writes=[la_b[ch]])
                                    A("act", lambda ch=ch, pt=pt: nc.scalar.activation(out=pt[:], in_=la_t[ch][:], func=AF.Exp),
                                      reads=[la_b[ch]], writes=[ptb])
                            else:
                                A("act", lambda i=i: nc.scalar.activation(out=pdw[i % 2][:], in_=pbig[i % 2][:], func=AF.Exp),
                                  reads=[psb[zz[0]], psb[zz[1]]], writes=[pdw_b[i % 2]])
                            for ch in range(2):
                                pt, ptb = pd_t[ch][i % 2], pd_b[ch][i % 2]
                                A("pe", lambda ch=ch, kb=kb, i=i, pt=pt: nc.tensor.matmul(psum[OB[ch]][:], V2[:, kb, :], pt[:],
                                                                                        start=(i == 0), stop=(i == nkb - 1)),
                                  reads=[pb, ptb], writes=[psb[OB[ch]]])
                                if ch == 0:
                                    A("pe", lambda i=i, pt=pt: nc.tensor.matmul(psum[SB_[0]][:], ones[:], pt[:],
                                                                                start=(i == 0), stop=(i == nkb - 1)),
                                      reads=[ptb, b_const], writes=[psb[SB_[0]]])
                                else:
                                    if i == 0:
                                        A("dve", lambda pt=pt: nc.vector.tensor_copy(out=rsum[1][:], in_=pt[:]),
                                          reads=[ptb], writes=[rsum_b[1]])
                                    else:
                                        A("dve", lambda pt=pt: nc.vector.tensor_tensor(out=rsum[1][:], in0=pt[:], in1=rsum[1][:], op=ALU.add),
                                          reads=[ptb, rsum_b[1]], writes=[rsum_b[1]])
                        for ch in range(1, 2):
                            A("dve", lambda ch=ch: nc.vector.tensor_copy(out=rsbf[ch][:], in_=rsum[ch][:]),
                              reads=[rsum_b[ch]], writes=[rsbf_b[ch]])
                            A("pe", lambda ch=ch: nc.tensor.matmul(psum[SB_[ch]][:], ones[:], rsbf[ch][:], start=True, stop=True),
                              reads=[rsbf_b[ch], b_const], writes=[psb[SB_[ch]]])
                        A("act", lambda: nc.scalar.activation(out=rw_t[:], in_=pbig[2][:], func=AF.Ln),
                          reads=[psb[4], psb[5]], writes=[r_b[0], r_b[1]])
                        A("act", lambda: nc.scalar.activation(out=rw_t[:], in_=rw_t[:], func=AF.Exp, scale=-1.0),
                          reads=[r_b[0], r_b[1]], writes=[r_b[0], r_b[1]])
                        for ch in range(2):
                            A("dve", lambda ch=ch: nc.vector.tensor_tensor(out=r_t[ch][:], in0=psum[OB[ch]][:], in1=r_t[ch][:], op=ALU.mult),
                              reads=[psb[OB[ch]], r_b[ch]], writes=[r_b[ch]])
                        A("dve", lambda: nc.vector.scalar_tensor_tensor(out=o_t[:], in0=r_t[1][:], scalar=neglam[:, l:l + 1], in1=r_t[0][:],
                                                                        op0=ALU.mult, op1=ALU.add),
                          reads=[r_b[0], r_b[1], b_const], writes=[o_b])
                        def tail(q0=q0, qc=qc, u=u, hd=hd):
                            A("act", lambda: nc.scalar.activation(out=sq2[:], in_=o_t[:], func=AF.Square), reads=[o_b], writes=[sq2b])
                            A("pe", lambda: nc.tensor.matmul(psum[5][:], ones[:], sq2[:], start=True, stop=True),
                              reads=[sq2b, b_const], writes=[psb[5]])
                            A("act", lambda: nc.scalar.activation(out=ln2[:], in_=psum[5][:], func=AF.Ln, bias=128.0 * EPS, scale=1.0),
                              reads=[psb[5]], writes=[ln2b])
                            A("act", lambda: nc.scalar.activation(out=rs2[:], in_=ln2[:], func=AF.Exp, scale=-0.5), reads=[ln2b], writes=[rs2b])
                            A("dve", lambda: nc.vector.scalar_tensor_tensor(out=mixo[:], in0=o_t[:], scalar=gsub2[:, l:l + 1], in1=rs2[:],
                                                                            op0=ALU.mult, op1=ALU.mult),
                              reads=[o_b, rs2b, b_const], writes=[mixob])
                            row = 256 + 128 * hd
                            A("sp", lambda: nc.sync.dma_start(out=mix_scr[row:row + 128, q0:q0 + 512], in_=mixo[:]),
                              reads=[mixob], writes=[mix_bu[u][qc]], dma=True)
                        pending.append(tail)
                if pending:
                    pending.pop()()

        for k in (6, 7):
            cc_n[0] += 1
            A("pool", lambda k=k: nc.gpsimd.collective_compute(
                "AllGather", ALU.bypass, replica_groups=[[0, 1], [2, 3], [4, 5], [6, 7]],
                ins=[mix_loc.ap()[64 * k:64 * k + 64, :].opt()], outs=[mix_full.ap()[128 * k:128 * k + 128, :].opt()]),
              reads=mix_bu[3], writes=[mixfull_b], dma=True, cc=(cc_sem, cc_n[0]))
        P.barrier()
        ffn_scope = ExitStack()
        w_up_sb = sb("w_up_sb", [128, 8, 2 * DFF], BF16, ffn_scope); wub = [Buf(), Buf()]
        stg_f = [sb(f"stg3b_{i}", [128, 1408], F32, ffn_scope) for i in range(2)]; stgb_f = [Buf(), Buf()]
        up_pieces = weight_pieces(w_up_sb, wub, w_up[l], 8, 2 * DFF, stg_f, stgb_f, 1408)
        with ExitStack() as ps_:
            w_o_sb = sb("w_o_sb", [128, 8, 1024], BF16, ps_); wb = [Buf(), Buf()]
            stg = [sb(f"stg3a_{i}", [128, 1024], F32, ps_) for i in range(2)]; stgb = [Buf(), Buf()]
            load_weight(w_o_sb, wb, w_o[l], 8, 1024, stg, stgb, 1024)
            A("sp", lambda: nc.sync.dma_start(out=mix_mine.ap(), in_=mix_full.ap()[:, bass.ds(r_h, H)]),
              reads=[mixfull_b], writes=[mixmine_b], dma=True)
            hbs = [sb(f"p3a_h{i}", [128, 8, 512], F32, ps_) for i in range(2)]; hbb = [Buf(), Buf()]
            mxs = [sb(f"p3a_m{i}", [128, 8, 512], BF16, ps_) for i in range(2)]; mxb = [Buf(), Buf()]
            sq = sb("p3a_sq", [128, 8, 512], BF16, ps_); sqb = Buf()
            lnv = sb("p3a_lnv", [128, 512], F32, ps_); lnvb = Buf()
            rstd = sb("p3a_rstd", [128, 512], F32, ps_); rstdb = Buf()
            xns = [sb(f"p3a_xn{i}", [128, 8, 512], BF16, ps_) for i in range(1)] * 2; xnb = [Buf()] * 2

            def ld(blk):
                A("sp", lambda blk=blk: nc.sync.dma_start(out=hbs[blk % 2][:], in_=hview(h_src, blk * 512, 512)),
                  reads=[h_src_b[blk]], writes=[hbb[blk % 2]], dma=True)
                A("sp", lambda blk=blk: nc.sync.dma_start(out=mxs[blk % 2][:], in_=hview(mix_mine.ap(), blk * 512, 512)),
                  reads=[mixmine_b], writes=[mxb[blk % 2]], dma=True)
            ld(0)
            for blk in range(NBh):
                if blk + 1 < NBh:
                    ld(blk + 1)
                hb, hb_b, mx, mx_b = hbs[blk % 2], hbb[blk % 2], mxs[blk % 2], mxb[blk % 2]
                for oc in range(8):
                    pi = next_ps()
                    for c in range(8):
                        A("pe", lambda c=c, oc=oc, pi=pi: nc.tensor.matmul(psum[pi][:], w_o_sb[:, c, oc * 128:(oc + 1) * 128], mx[:, c, :],
                                                                         start=(c == 0), stop=(c == 7)),
                          reads=[*wb, mx_b], writes=[psb[pi]], sig=(c == 7))
                    A("dve", lambda oc=oc, pi=pi: nc.vector.tensor_tensor(out=hb[:, oc, :], in0=psum[pi][:], in1=hb[:, oc, :], op=ALU.add),
                      reads=[psb[pi], hb_b], writes=[hb_b])
                A("sp", lambda blk=blk, hb=hb: nc.sync.dma_start(out=hview(hA, blk * 512, 512), in_=hb[:]),
                  reads=[hb_b], writes=[hA_b[blk]], dma=True)
                xn, xn_b = xns[blk % 2], xnb[blk % 2]
                norm_block(hb, hb_b, 512, lambda c: gvec32[:, (l * 3 + 1) * 8 + c:(l * 3 + 1) * 8 + c + 1],
                           sq, sqb, lnv, lnvb, rstd, rstdb, xn, xn_b)
                A("sp", lambda blk=blk, xn=xn: nc.sync.dma_start(out=hview(xn2_scr, blk * 512, 512), in_=xn[:]),
                  reads=[xn_b], writes=[xn2_b[blk]], dma=True)
                per = (len(up_pieces) + NBh - 1) // NBh
                for pc in up_pieces[blk * per:(blk + 1) * per]:
                    pc()
                if blk == NBh - 1:
                    A("sp", lambda xn=xn: nc.sync.dma_start(out=tail_loc.ap().rearrange("(c p) t -> p c t", p=128), in_=xn[:, :, 510:512]),
                      reads=[xn_b], writes=[tail_b], dma=True)
        cc_n[0] += 1
        A("pool", lambda: nc.gpsimd.collective_compute(
            "AllGather", ALU.bypass, replica_groups=[[0, 1], [2, 3], [4, 5], [6, 7]],
            ins=[tail_loc.ap().opt()], outs=[tail_full.ap().opt()]),
          reads=[tail_b], writes=[tailfull_b], dma=True, cc=(cc_sem, cc_n[0]))

        P.barrier()
        with ExitStack() as ps_:
            w_dn_sb = sb("w_dn_sb", [128, NFC, 1024], BF16, ps_); wdb = [Buf(), Buf()]
            load_weight(w_dn_sb, wdb, w_down[l], NFC, 1024, stg_f, stgb_f, 1024)
            NT = 256
            xbs = [sb(f"p3b_x{i}", [128, 8, NT + 2], BF16, ps_) for i in range(2)]; xbb = [Buf(), Buf()]
            hbs = [sb(f"p3b_h{i}", [128, 8, NT], F32, ps_) for i in range(2)]; hbb = [Buf(), Buf()]
            act_t = sb("p3b_act", [128, NFC, NT], BF16, ps_); actb = [Buf() for _ in range(NFC)]
            yg = [sb(f"p3b_yg{i}", [128, NT], F32, ps_) for i in range(2)]; ygb = [Buf(), Buf()]
            yv = [sb(f"p3b_yv{i}", [128, NT], F32, ps_) for i in range(2)]; yvb = [Buf(), Buf()]
            gg = [sb(f"p3b_gg{i}", [128, NT], F32, ps_) for i in range(2)]; ggb = [Buf(), Buf()]
            nblk = H // NT
            tl = sb("p3b_tl", [128, 8, 2], BF16, ps_); tlb = Buf()

            def ld(b):
                t0 = b * NT
                x_, xb_ = xbs[b % 2], xbb[b % 2]
                rb = [xn2_b[t0 // 512]] + ([xn2_b[(t0 - 2) // 512]] if t0 > 0 else [])
                if b == 0:
                    A("sp", lambda: nc.sync.dma_start(out=tl[:], in_=tail_full.ap()[0:D, :].rearrange("(c p) t -> p c t", p=128)),
                      reads=[tailfull_b], writes=[tlb], dma=True)
                    A("dve", lambda x_=x_: nc.vector.tensor_scalar(out=x_[:, :, 0:2], in0=tl[:], scalar1=hmask[:, 0:1], scalar2=None,
                                                                   op0=ALU.mult), reads=[tlb, b_const], writes=[xb_])
                    A("sp", lambda x_=x_: nc.sync.dma_start(out=x_[:, :, 2:NT + 2], in_=hview(xn2_scr, 0, NT)),
                      reads=rb, writes=[xb_], dma=True)
                else:
                    A("sp", lambda x_=x_, t0=t0: nc.sync.dma_start(out=x_[:], in_=hview(xn2_scr, t0 - 2, NT + 2)),
                      reads=rb, writes=[xb_], dma=True)
                A("sp", lambda b=b, t0=t0: nc.sync.dma_start(out=hbs[b % 2][:], in_=hview(hA, t0, NT)),
                  reads=[hA_b[t0 // 512]], writes=[hbb[b % 2]], dma=True)
            ld(0)
            cwo = lambda k, j: (l * 3 + k) * 44 + j
            for b in range(nblk):
                if b + 1 < nblk:
                    ld(b + 1)
                x_, xb_, hb, hb_b = xbs[b % 2], xbb[b % 2], hbs[b % 2], hbb[b % 2]
                for i in range(NFC):
                    pg, pv = next_ps(), next_ps()
                    for (pi, cbase) in ((pg, 128 * i), (pv, DFF + 128 * i)):
                        for c in range(8):
                            A("pe", lambda c=c, pi=pi, cbase=cbase: nc.tensor.matmul(
                                psum[pi][:, :NT + 2], w_up_sb[:, c, cbase:cbase + 128], x_[:, c, :], start=(c == 0), stop=(c == 7)),
                                reads=[*wub, xb_], writes=[psb[pi]], sig=(c == 7))
                    s = i % 2
                    for (pi, y, yb, j) in ((pg, yg[s], ygb[s], i), (pv, yv[s], yvb[s], NFC + i)):
                        A("act", lambda pi=pi, y=y, j=j: nc.scalar.activation(
                            out=y[:], in_=psum[pi][:, 2:NT + 2], func=AF.Identity,
                            bias=cb[:, l * 44 + j:l * 44 + j + 1], scale=cw[:, cwo(2, j):cwo(2, j) + 1]),
                            reads=[psb[pi], b_const], writes=[yb])
                        A("dve", lambda pi=pi, y=y, j=j: nc.vector.scalar_tensor_tensor(
                            out=y[:], in0=psum[pi][:, 1:NT + 1], scalar=cw[:, cwo(1, j):cwo(1, j) + 1], in1=y[:],
                            op0=ALU.mult, op1=ALU.add), reads=[psb[pi], yb, b_const], writes=[yb])
                        A("dve", lambda pi=pi, y=y, j=j: nc.vector.scalar_tensor_tensor(
                            out=y[:], in0=psum[pi][:, 0:NT], scalar=cw[:, cwo(0, j):cwo(0, j) + 1], in1=y[:],
                            op0=ALU.mult, op1=ALU.add), reads=[psb[pi], yb, b_const], writes=[yb])
                    A("act", lambda s=s: nc.scalar.activation(out=gg[s][:], in_=yg[s][:], func=AF.Gelu_apprx_tanh),
                      reads=[ygb[s]], writes=[ggb[s]])
                    A("pool", lambda s=s, i=i: nc.gpsimd.tensor_tensor(out=act_t[:, i, :], in0=gg[s][:], in1=yv[s][:], op=ALU.mult),
                      reads=[ggb[s], yvb[s]], writes=[actb[i]])
                for oc in range(8):
                    pi = next_ps()
                    for i in range(NFC):
                        A("pe", lambda i=i, oc=oc, pi=pi: nc.tensor.matmul(psum[pi][:, :NT], w_dn_sb[:, i, oc * 128:(oc + 1) * 128], act_t[:, i, :],
                                                                         start=(i == 0), stop=(i == NFC - 1)),
                          reads=[*wdb, actb[i]], writes=[psb[pi]], sig=(i == NFC - 1))
                    A("dve", lambda oc=oc, pi=pi: nc.vector.tensor_tensor(out=hb[:, oc, :], in0=psum[pi][:, :NT], in1=hb[:, oc, :], op=ALU.add),
                      reads=[psb[pi], hb_b], writes=[hb_b])
                A("sp", lambda b=b, hb=hb: nc.sync.dma_start(out=hview(hB, b * NT, NT), in_=hb[:]),
                  reads=[hb_b], writes=[hB_b[(b * NT) // 512]], dma=True)

        P.barrier()
        ffn_scope.close()
        with ExitStack() as ps_:
            w_pg_sb = sb("w_pg_sb", [128, 8, 1024], BF16, ps_); wgb = [Buf(), Buf()]
            w_pp_sb = sb("w_pp_sb", [128, 2, 1024], BF16, ps_); wpb = [Buf(), Buf()]
            stg = [sb(f"stg3c_{i}", [128, 1024], F32, ps_) for i in range(2)]; stgb = [Buf(), Buf()]
            load_weight(w_pg_sb, wgb, w_pg[l], 8, 1024, stg, stgb, 1024)
            load_weight(w_pp_sb, wpb, w_pp[l], 2, 1024, stg, stgb, 1024)
            hbs = [sb(f"p3c_h{i}", [128, 8, 512], F32, ps_) for i in range(2)]; hbb = [Buf(), Buf()]
            pfs = [sb(f"p3c_pf{i}", [128, 2, 512], F32, ps_) for i in range(2)]; pfb = [Buf(), Buf()]
            pbf = sb("p3c_pb", [128, 2, 512], BF16, ps_); pbb = Buf()
            sq = sb("p3c_sq", [128, 8, 512], BF16, ps_); sqb = Buf()
            lnv = sb("p3c_lnv", [128, 512], F32, ps_); lnvb = Buf()
            rstd = sb("p3c_rstd", [128, 512], F32, ps_); rstdb = Buf()
            xn = sb("p3c_xn", [128, 8, 512], BF16, ps_); xnb = Buf()
            sg = [sb(f"p3c_sg{i}", [128, 512], F32, ps_) for i in range(2)]; sgb = [Buf(), Buf()]
            outs = [sb(f"p3c_o{i}", [128, 8, 512], F32, ps_) for i in range(2)] if last else None
            outb = [Buf(), Buf()]
            dst, dst_b = (hC, hC_b)

            def ld(blk):
                A("sp", lambda blk=blk: nc.sync.dma_start(out=hbs[blk % 2][:], in_=hview(hB, blk * 512, 512)),
                  reads=[hB_b[blk]], writes=[hbb[blk % 2]], dma=True)
                A("sp", lambda blk=blk: nc.sync.dma_start(out=pfs[blk % 2][:], in_=pT[l][:, blk * 512:(blk + 1) * 512].rearrange("(c p) t -> p c t", p=128)),
                  writes=[pfb[blk % 2]], dma=True)
            ld(0)
            for blk in range(NBh):
                if blk + 1 < NBh:
                    ld(blk + 1)
                hb, hb_b = hbs[blk % 2], hbb[blk % 2]
                norm_block(hb, hb_b, 512, lambda c: gvec32[:, (l * 3 + 2) * 8 + c:(l * 3 + 2) * 8 + c + 1],
                           sq, sqb, lnv, lnvb, rstd, rstdb, xn, xnb)
                A("pool", lambda blk=blk: nc.gpsimd.tensor_copy(out=pbf[:], in_=pfs[blk % 2][:]), reads=[pfb[blk % 2]], writes=[pbb])
                for oc in range(8):
                    pg, pp = next_ps(), next_ps()
                    for c in range(8):
                        A("pe", lambda c=c, oc=oc, pg=pg: nc.tensor.matmul(psum[pg][:], w_pg_sb[:, c, oc * 128:(oc + 1) * 128], xn[:, c, :],
                                                                         start=(c == 0), stop=(c == 7)),
                          reads=[*wgb, xnb], writes=[psb[pg]], sig=(c == 7))
                    for c in range(2):
                        A("pe", lambda c=c, oc=oc, pp=pp: nc.tensor.matmul(psum[pp][:], w_pp_sb[:, c, oc * 128:(oc + 1) * 128], pbf[:, c, :],
                                                                         start=(c == 0), stop=(c == 1)),
                          reads=[*wpb, pbb], writes=[psb[pp]], sig=(c == 1))
                    s = oc % 2
                    A("act", lambda s=s, pg=pg: nc.scalar.activation(out=sg[s][:], in_=psum[pg][:], func=AF.Sigmoid),
                      reads=[psb[pg]], writes=[sgb[s]])
                    A("dve", lambda s=s, pp=pp: nc.vector.tensor_tensor(out=sg[s][:], in0=psum[pp][:], in1=sg[s][:], op=ALU.mult),
                      reads=[psb[pp], sgb[s]], writes=[sgb[s]])
                    A("dve", lambda s=s, oc=oc: nc.vector.tensor_tensor(out=hb[:, oc, :], in0=sg[s][:], in1=hb[:, oc, :], op=ALU.add),
                      reads=[sgb[s], hb_b], writes=[hb_b])
                if not last:
                    A("sp", lambda blk=blk, hb=hb: nc.sync.dma_start(out=hview(dst, blk * 512, 512), in_=hb[:]),
                      reads=[hb_b], writes=[dst_b[blk]], dma=True)
                else:
                    o_, ob_ = outs[blk % 2], outb[blk % 2]
                    norm_block(hb, hb_b, 512, lambda c: gfin32[:, c:c + 1], sq, sqb, lnv, lnvb, rstd, rstdb, None, None,
                               out_f32=o_, outb=ob_)
                    A("sp", lambda blk=blk, o_=o_: nc.sync.dma_start(out=hview(outT, blk * 512, 512), in_=o_[:]),
                      reads=[ob_], writes=[dst_b[blk]], dma=True)
        h_src, h_src_b = hC, hC_b

    P.barrier()
    A("pool", None, reads=hC_b)
    A("sp", None, reads=hC_b)
    n, nw = len(P.meta), (P.n_wait, getattr(P, 'n_standalone', 0), dict(P.cnt))
    es.close()
    return nc, (n, nw)


def _rel_bucket_np(dist):
    max_exact = 16
    d = np.maximum(dist, 1).astype(np.float32)
    large = max_exact + (np.log(d / np.float32(max_exact)) / np.float32(math.log(128 / max_exact))
                         * np.float32(32 - max_exact)).astype(np.int32)
    large = np.minimum(large, 31)
    return np.where(dist < max_exact, dist, large)


def host_prep(T, L, x_b, p_b, w, rank=0):
    f = np.float32
    m = {}
    m["xT"] = np.ascontiguousarray(x_b.T)
    m["pT"] = np.ascontiguousarray(np.transpose(p_b, (0, 2, 1)))
    for k_src, k_dst in (("w_up", "w_up"), ("w_down", "w_down"),
                         ("w_ple_gate", "w_pg"), ("w_ple_proj", "w_pp")):
        m[k_dst] = np.ascontiguousarray(w[k_src][:L])
    r = rank
    dh = [2 * r, 2 * r + 1]
    cols = []
    for g in range(2):
        sbp = [2 * g, 2 * g + 1]
        dhg = [2 * g, 2 * g + 1]
        for u in sbp:
            cols += list(range(128 * u, 128 * u + 128))
        for u in sbp:
            cols += list(range(512 + 128 * u, 512 + 128 * u + 128))
        for d_ in dhg:
            cols += list(range(1536 + 128 * d_, 1536 + 128 * d_ + 128))
        for d_ in dhg:
            cols += list(range(2048 + 128 * d_, 2048 + 128 * d_ + 128))
    for g in range(2):
        for u in (2 * g, 2 * g + 1):
            cols += list(range(1024 + 128 * u, 1024 + 128 * u + 128))
        for d_ in (2 * g, 2 * g + 1):
            cols += list(range(2560 + 128 * d_, 2560 + 128 * d_ + 128))
    m["w_in"] = np.ascontiguousarray(w["w_in"][:L][:, :, cols])
    m["hmask"] = np.full((128, 1), float(rank), np.float32)
    def orig_row(rr, rho):
        return 256 * rr + rho if rho < 256 else 512 + 256 * rr + (rho - 256)
    rows = [orig_row(rr, 64 * k + i) for k in range(8) for rr in range(2) for i in range(64)]
    m["w_o"] = np.ascontiguousarray(w["w_o"][:L][:, rows, :])
    g = np.stack([w["g_attn"][:L], w["g_ffn"][:L], w["g_ple"][:L]], axis=1)
    m["gvec"] = np.ascontiguousarray(g.reshape(L, 3, 8, 128).transpose(3, 0, 1, 2).reshape(128, L * 3 * 8))
    m["gfin"] = np.ascontiguousarray(w["g_final"].reshape(8, 128).T)
    m["convw"] = np.ascontiguousarray(w["conv_w"][:L].reshape(L, 3, 44, 128).transpose(3, 0, 1, 2).reshape(128, L * 3 * 44))
    m["convb"] = np.ascontiguousarray(w["conv_b"][:L].reshape(L, 44, 128).transpose(2, 0, 1).reshape(128, L * 44))
    m["gsub"] = np.ascontiguousarray(w["g_subln"][:L].T)
    lam = np.stack([w["lambda_q1"][:L], w["lambda_k1"][:L], w["lambda_q2"][:L], w["lambda_k2"][:L]], axis=1)
    m["lamv"] = np.ascontiguousarray(np.broadcast_to(lam.reshape(1, L * 4 * 64), (128, L * 4 * 64)))
    kl = np.arange(128)[:, None]
    xx = np.arange(1024)[None, :]
    dd = xx - 384 - kl
    idx = _rel_bucket_np(np.maximum(dd, 0))
    maps = [2 * d_ + j for d_ in dh for j in range(2)]
    bt = np.transpose(w["rel_bias"][:, maps][idx], (2, 0, 1)).astype(f)
    bt = np.where((dd < 0)[None], f(NEG), bt)
    m["btoe"] = np.ascontiguousarray(bt)
    negw = np.zeros((2, 128, 1024), f)
    xq = xx - 512
    negw[0] = np.where(xq <= kl, NEG, 0.0)
    negw[1] = np.where(xq < kl, NEG, 0.0)
    m["negw"] = negw
    cst = np.zeros((3, 128, 128), f)
    cst[0] = np.eye(128, dtype=f)
    jj = np.arange(128)[:, None]; ss = np.arange(128)[None, :]
    cst[1] = np.where(jj >= ss, -1.0, 0.0)
    cst[2] = 1.0
    m["cst"] = cst
    return {k: np.ascontiguousarray(v, dtype=f) for k, v in m.items()}


_CACHE = {}


def kernel(**inputs):
    x = np.asarray(inputs["x"], np.float32)
    p = np.asarray(inputs["p"], np.float32)
    w = {k: np.asarray(v, np.float32) for k, v in inputs.items() if k not in ("x", "p")}
    B, T, _ = x.shape
    L = p.shape[0]
    H = T // 2
    key = (T, L)
    if key not in _CACHE:
        _CACHE[key] = build_program(T, L)[0]
    nc = _CACHE[key]
    n_cores = 8
    maps = []
    per = {}
    for core in range(n_cores):
        b = (core * B) // n_cores
        r = core % 2
        maps.append(host_prep(T, L, x[b, r * H:(r + 1) * H], p[:, b, r * H:(r + 1) * H], w, rank=r))
    res = run_bass_kernel_spmd(nc, maps, core_ids=list(range(n_cores)))
    out = np.empty((B, T, D), np.float32)
    for core in range(n_cores):
        b, r = (core * B) // n_cores, core % 2
        out[b, r * H:(r + 1) * H] = res.results[core]["outT"].T
    return out
```
